# Optimizing a Trainium2 kernel written in Bass

```python
import jax, jax.numpy as jnp
from jax import lax
import numpy as np

D_MODEL = 4096
BATCH = 2
SEQ = 4096
DEPTH = 2

GRID_W = 64
CTX_LEN = 256
MIX_WIDTH = D_MODEL
N_GROUPS = 4
GROUP_WIDTH = MIX_WIDTH // N_GROUPS
HEAD_DIM = 128
ATT_HEADS = GROUP_WIDTH // HEAD_DIM
ATT_KV_HEADS = ATT_HEADS // 4
HGRN_HEADS = GROUP_WIDTH // HEAD_DIM
HGRN_KEY_DIM = 128
HGRN_VAL_DIM = GROUP_WIDTH // HGRN_HEADS
HGRN_KEY_WIDTH = HGRN_HEADS * HGRN_KEY_DIM
HGRN_CHUNK = 64
CONV_WIDTH = 31
MLA_HEADS = GROUP_WIDTH // HEAD_DIM
MLA_Q_RANK = 768
MLA_KV_RANK = 512
MLA_NOPE_DIM = 128
MLA_ROPE_DIM = 64
MLA_V_DIM = GROUP_WIDTH // MLA_HEADS
MLA_QK_DIM = MLA_NOPE_DIM + MLA_ROPE_DIM

Q_BLOCK = 128
ROPE_THETA = 10000.0
EPS = 1e-6

IN_SPLITS = (
    ATT_HEADS * HEAD_DIM, ATT_KV_HEADS * HEAD_DIM, ATT_KV_HEADS * HEAD_DIM, GROUP_WIDTH,
    HGRN_KEY_WIDTH, GROUP_WIDTH, HGRN_KEY_WIDTH, HGRN_KEY_WIDTH, GROUP_WIDTH,
    GROUP_WIDTH, GROUP_WIDTH, GROUP_WIDTH,
    MLA_Q_RANK, MLA_KV_RANK, MLA_ROPE_DIM, GROUP_WIDTH,
)
IN_WIDTH = sum(IN_SPLITS)

kernel_name = "hybrid_parallel_groups_gqa_hgrn2_conformer_mla"


def rms_norm(x, w):
    xf = x.astype(jnp.float32)
    y = xf * lax.rsqrt(jnp.mean(xf * xf, axis=-1, keepdims=True) + EPS)
    return (y * w.astype(jnp.float32)).astype(x.dtype)


def layer_norm(x, w, b):
    xf = x.astype(jnp.float32)
    mu = jnp.mean(xf, axis=-1, keepdims=True)
    var = jnp.mean(jnp.square(xf - mu), axis=-1, keepdims=True)
    y = (xf - mu) * lax.rsqrt(var + EPS)
    return (y * w.astype(jnp.float32) + b.astype(jnp.float32)).astype(x.dtype)


def heads(t, n):
    return t.reshape(t.shape[:-1] + (n, t.shape[-1] // n))


def split_cols(t):
    idx = [int(i) for i in np.cumsum(IN_SPLITS)[:-1]]
    return jnp.split(t, idx, axis=-1)


def axial_rope(x, rows, cols):
    half = x.shape[-1] // 2
    quarter = half // 2
    inv_freq = ROPE_THETA ** (-jnp.arange(quarter, dtype=jnp.float32) / quarter)

    def rotate(xa, pos):
        ang = pos.astype(jnp.float32)[:, None] * inv_freq
        cos = jnp.cos(ang)[None, :, None, :].astype(x.dtype)
        sin = jnp.sin(ang)[None, :, None, :].astype(x.dtype)
        x1, x2 = jnp.split(xa, 2, axis=-1)
        return jnp.concatenate([x1 * cos - x2 * sin, x2 * cos + x1 * sin], axis=-1)

    return jnp.concatenate([rotate(x[..., :half], rows), rotate(x[..., half:], cols)], axis=-1)


def block_attention(q, k, v, scale):
    b, lq, hq, dk = q.shape
    hkv, dv = k.shape[2], v.shape[-1]
    grp = hq // hkv
    nb = lq // Q_BLOCK
    qb = q.reshape(b, nb, Q_BLOCK, hkv, grp, dk).transpose(1, 0, 2, 3, 4, 5)

    def attend(qblk):
        s = jnp.einsum('bqhgd,bkhd->bhgqk', qblk, k).astype(jnp.float32) * scale
        p = jax.nn.softmax(s, axis=-1).astype(v.dtype)
        return jnp.einsum('bhgqk,bkhe->bqhge', p, v)

    o = lax.map(attend, qb)
    return o.transpose(1, 0, 2, 3, 4, 5).reshape(b, lq, hq * dv)


def gla_chunk_scan(q, k, v, log_f, s0):
    b, l, h, _ = q.shape
    dv = v.shape[-1]
    nc = l // HGRN_CHUNK

    def chunks(t):
        return t.reshape(b, nc, HGRN_CHUNK, h, t.shape[-1]).transpose(1, 0, 3, 2, 4)

    lower = jnp.tril(jnp.ones((HGRN_CHUNK, HGRN_CHUNK), dtype=bool))[:, :, None]

    def step(state, xs):
        qc, kc, vc, lf = xs
        g = jnp.cumsum(lf, axis=2)
        rel = jnp.exp(jnp.where(lower, g[:, :, :, None, :] - g[:, :, None, :, :], -jnp.inf))
        scores = jnp.einsum('bhtd,bhtsd,bhsd->bhts', qc, rel, kc)
        o = (jnp.einsum('bhts,bhse->bhte', scores, vc)
             + jnp.einsum('bhtd,bhde->bhte', qc * jnp.exp(g), state))
        g_last = g[:, :, -1:, :]
        state = (jnp.exp(g_last[:, :, 0, :])[..., None] * state
                 + jnp.einsum('bhsd,bhse->bhde', kc * jnp.exp(g_last - g), vc))
        return state, o

    s_fin, o = lax.scan(step, s0, (chunks(q), chunks(k), chunks(v), chunks(log_f)))
    o = o.transpose(1, 0, 3, 2, 4).reshape(b, l, h, dv)
    return o, s_fin


def hgrn2_gates(f, lb):
    log_f = jax.nn.log_sigmoid(f) + jnp.logaddexp(0.0, jnp.log(lb) - f)
    k = (1.0 - lb) * jax.nn.sigmoid(-f)
    return k, log_f


def gqa_branch(lat, ctx, p, rows, cols, ctx_out):
    def proj(q, k, v):
        q = rms_norm(heads(q, ATT_HEADS), p['att_q_norm'])
        k = rms_norm(heads(k, ATT_KV_HEADS), p['att_k_norm'])
        return q, k, heads(v, ATT_KV_HEADS)

    ql, kl, vl = proj(*lat[:3])
    qc, kc, vc = proj(*ctx[:3])
    ql = axial_rope(ql, rows, cols)
    kl = axial_rope(kl, rows, cols)
    scale = HEAD_DIM ** -0.5
    k_all = jnp.concatenate([kc, kl], axis=1)
    v_all = jnp.concatenate([vc, vl], axis=1)
    yl = block_attention(ql, k_all, v_all, scale) * jax.nn.silu(lat[3])
    yc = block_attention(qc, kc, vc, scale) * jax.nn.silu(ctx[3]) if ctx_out else None
    return yl, yc


def hgrn2_branch(lat, ctx, lb, o_norm, ctx_out):
    lb = lb.reshape(2, HGRN_HEADS, HGRN_KEY_DIM)
    o_norm = o_norm.reshape(HGRN_HEADS, HGRN_VAL_DIM)

    def prep(q, i, f_fw, f_bw):
        q = heads(jax.nn.silu(q.astype(jnp.float32)), HGRN_HEADS) * HGRN_KEY_DIM ** -0.5
        v = heads(i.astype(jnp.float32), HGRN_HEADS)
        fw = hgrn2_gates(heads(f_fw.astype(jnp.float32), HGRN_HEADS), lb[0])
        bw = hgrn2_gates(heads(f_bw.astype(jnp.float32), HGRN_HEADS), lb[1])
        return q, v, fw, bw

    ql, vl, fwl, bwl = prep(*lat[:4])
    qc, vc, fwc, bwc = prep(*ctx[:4])
    zero = jnp.zeros((ql.shape[0], HGRN_HEADS, HGRN_KEY_DIM, HGRN_VAL_DIM), jnp.float32)

    def rev(t):
        return jnp.flip(t, axis=1)

    oc_fw, sc_fw = gla_chunk_scan(qc, fwc[0], vc, fwc[1], zero)
    ol_fw, _ = gla_chunk_scan(ql, fwl[0], vl, fwl[1], sc_fw)
    oc_bw, sc_bw = gla_chunk_scan(rev(qc), rev(bwc[0]), rev(vc), rev(bwc[1]), zero)
    ol_bw, _ = gla_chunk_scan(rev(ql), rev(bwl[0]), rev(vl), rev(bwl[1]), sc_bw)

    def readout(o, gate):
        o = rms_norm(o, o_norm)
        return o.reshape(o.shape[:2] + (-1,)).astype(gate.dtype) * jax.nn.silu(gate)

    yl = readout(ol_fw + rev(ol_bw), lat[4])
    yc = readout(oc_fw + rev(oc_bw), ctx[4]) if ctx_out else None
    return yl, yc


def conv_branch(u, glu, gate, p):
    xg = u * jax.nn.sigmoid(glu)
    y = lax.conv_general_dilated(
        xg, p['conv_w'][:, None, :].astype(xg.dtype), window_strides=(1,), padding='SAME',
        dimension_numbers=('NWC', 'WIO', 'NWC'), feature_group_count=xg.shape[-1]) + p['conv_b']
    y = jax.nn.silu(layer_norm(y, p['conv_ln_w'], p['conv_ln_b']))
    return y * jax.nn.silu(gate)


def mla_branch(lat, ctx, p, rows, cols, ctx_out):
    def proj(cq, ckv, k_rope):
        q = heads(rms_norm(cq, p['mla_q_norm']) @ p['mla_w_uq'], MLA_HEADS)
        kv = heads(rms_norm(ckv, p['mla_kv_norm']) @ p['mla_w_ukv'], MLA_HEADS)
        k_nope, v = jnp.split(kv, [MLA_NOPE_DIM], axis=-1)
        k_rope = jnp.broadcast_to(k_rope[:, :, None, :], k_nope.shape[:-1] + (MLA_ROPE_DIM,))
        k = jnp.concatenate([k_nope, k_rope], axis=-1)
        return rms_norm(q, p['mla_qk_q_norm']), rms_norm(k, p['mla_qk_k_norm']), v

    def rope_tail(t):
        return jnp.concatenate([t[..., :MLA_NOPE_DIM], axial_rope(t[..., MLA_NOPE_DIM:], rows, cols)], axis=-1)

    ql, kl, vl = proj(*lat[:3])
    qc, kc, vc = proj(*ctx[:3])
    ql = rope_tail(ql)
    kl = rope_tail(kl)
    scale = MLA_QK_DIM ** -0.5
    k_all = jnp.concatenate([kc, kl], axis=1)
    v_all = jnp.concatenate([vc, vl], axis=1)
    yl = block_attention(ql, k_all, v_all, scale) * jax.nn.silu(lat[3])
    yc = block_attention(qc, kc, vc, scale) * jax.nn.silu(ctx[3]) if ctx_out else None
    return yl, yc


def hybrid_mixer(h, hc, p, lb, rows, cols, ctx_out):
    lat = split_cols(h @ p['w_in'])
    ctx = split_cols(hc @ p['w_in'])
    ya, yca = gqa_branch(lat[0:4], ctx[0:4], p, rows, cols, ctx_out)
    yb, ycb = hgrn2_branch(lat[4:9], ctx[4:9], lb, p['hgrn_o_norm'], ctx_out)
    yc = conv_branch(lat[9], lat[10], lat[11], p)
    yd, ycd = mla_branch(lat[12:16], ctx[12:16], p, rows, cols, ctx_out)
    out = jnp.concatenate([ya, yb, yc, yd], axis=-1) @ p['w_out']
    if not ctx_out:
        return out, None
    ycc = conv_branch(ctx[9], ctx[10], ctx[11], p)
    out_c = jnp.concatenate([yca, ycb, ycc, ycd], axis=-1) @ p['w_out']
    return out, out_c


def setup_inputs(seed: int = 0) -> dict:
    key = jax.random.key(seed)
    ks = jax.random.split(key, 24)
    f32 = jnp.float32

    def nrm(k, shape, s):
        return jax.random.normal(k, shape, f32) * s

    def gain(k, shape):
        return 1.0 + 0.01 * jax.random.normal(k, shape, f32)

    return {
        'x': nrm(ks[0], (BATCH, SEQ, D_MODEL), 1.0),
        'c': nrm(ks[1], (BATCH, D_MODEL), 1.0),
        'ctx': nrm(ks[2], (BATCH, CTX_LEN, D_MODEL), 1.0),
        'c_ctx': nrm(ks[3], (D_MODEL,), 1.0),
        'w_mod': nrm(ks[4], (DEPTH, D_MODEL, 3 * D_MODEL), 0.5 * D_MODEL ** -0.5),
        'b_mod': nrm(ks[5], (DEPTH, 3 * D_MODEL), 0.01),
        'norm_w': gain(ks[6], (DEPTH, D_MODEL)),
        'w_in': nrm(ks[7], (DEPTH, D_MODEL, IN_WIDTH), D_MODEL ** -0.5),
        'w_out': nrm(ks[8], (DEPTH, MIX_WIDTH, D_MODEL), MIX_WIDTH ** -0.5),
        'att_q_norm': gain(ks[9], (DEPTH, HEAD_DIM)),
        'att_k_norm': gain(ks[10], (DEPTH, HEAD_DIM)),
        'hgrn_lb_logits': nrm(ks[11], (DEPTH, 2, HGRN_KEY_WIDTH), 0.5),
        'hgrn_o_norm': gain(ks[12], (DEPTH, GROUP_WIDTH)),
        'conv_w': nrm(ks[13], (DEPTH, CONV_WIDTH, GROUP_WIDTH), CONV_WIDTH ** -0.5),
        'conv_b': nrm(ks[14], (DEPTH, GROUP_WIDTH), 0.01),
        'conv_ln_w': gain(ks[15], (DEPTH, GROUP_WIDTH)),
        'conv_ln_b': nrm(ks[16], (DEPTH, GROUP_WIDTH), 0.01),
        'mla_q_norm': gain(ks[17], (DEPTH, MLA_Q_RANK)),
        'mla_kv_norm': gain(ks[18], (DEPTH, MLA_KV_RANK)),
        'mla_w_uq': nrm(ks[19], (DEPTH, MLA_Q_RANK, MLA_HEADS * MLA_QK_DIM), MLA_Q_RANK ** -0.5),
        'mla_w_ukv': nrm(ks[20], (DEPTH, MLA_KV_RANK, MLA_HEADS * (MLA_NOPE_DIM + MLA_V_DIM)), MLA_KV_RANK ** -0.5),
        'mla_qk_q_norm': gain(ks[21], (DEPTH, MLA_QK_DIM)),
        'mla_qk_k_norm': gain(ks[22], (DEPTH, MLA_QK_DIM)),
    }


def reference(x, c, ctx, c_ctx, w_mod, b_mod, norm_w, w_in, w_out, att_q_norm, att_k_norm,
              hgrn_lb_logits, hgrn_o_norm, conv_w, conv_b, conv_ln_w, conv_ln_b,
              mla_q_norm, mla_kv_norm, mla_w_uq, mla_w_ukv, mla_qk_q_norm, mla_qk_k_norm):
    seq_len = x.shape[1]
    ROWS = seq_len // GRID_W
    rows = jnp.repeat(jnp.arange(ROWS, dtype=jnp.int32), GRID_W)
    cols = jnp.tile(jnp.arange(GRID_W, dtype=jnp.int32), ROWS)

    lb_all = jnp.cumsum(jax.nn.softmax(hgrn_lb_logits.astype(jnp.float32), axis=0), axis=0)
    lb_all = lb_all - lb_all[0:1]

    silu_c = jax.nn.silu(c)
    silu_cc = jax.nn.silu(c_ctx)
    for l in range(DEPTH):
        ctx_out = l < DEPTH - 1
        mod = silu_c @ w_mod[l] + b_mod[l]
        mod_c = silu_cc @ w_mod[l] + b_mod[l]
        shift, scale, gate = jnp.split(mod, 3, axis=-1)
        shift_c, scale_c, gate_c = jnp.split(mod_c, 3, axis=-1)
        h = rms_norm(x, norm_w[l]) * (1.0 + scale[:, None, :]) + shift[:, None, :]
        hc = rms_norm(ctx, norm_w[l]) * (1.0 + scale_c) + shift_c
        p = dict(w_in=w_in[l], w_out=w_out[l], att_q_norm=att_q_norm[l], att_k_norm=att_k_norm[l],
                 hgrn_o_norm=hgrn_o_norm[l], conv_w=conv_w[l], conv_b=conv_b[l],
                 conv_ln_w=conv_ln_w[l], conv_ln_b=conv_ln_b[l], mla_q_norm=mla_q_norm[l],
                 mla_kv_norm=mla_kv_norm[l], mla_w_uq=mla_w_uq[l], mla_w_ukv=mla_w_ukv[l],
                 mla_qk_q_norm=mla_qk_q_norm[l], mla_qk_k_norm=mla_qk_k_norm[l])
        out, out_c = hybrid_mixer(h, hc, p, lb_all[l], rows, cols, ctx_out)
        x = x + gate[:, None, :] * out
        if ctx_out:
            ctx = ctx + gate_c * out_c
    return x
```

```python
import contextlib
import os
import numpy as np
import ml_dtypes
import concourse.bass as bass
import concourse.mybir as mybir
from concourse.bass_utils import run_bass_kernel_spmd

F32 = mybir.dt.float32
BF16 = mybir.dt.bfloat16
AF = mybir.ActivationFunctionType
ALU = mybir.AluOpType
AX = mybir.AxisListType

D = 4096
KC = 32
NCTX = 256
NLAT = 1024
TL = NCTX + NLAT
NT = TL // 128
SEQ = 4096
NKEY = NCTX + SEQ
EPS = 1e-6
IN_W = 13120

GROUPS = [
    ("aq", 1024, "F", BF16), ("ak", 256, "F", BF16), ("av", 256, "T", BF16), ("ag", 1024, "F", BF16),
    ("bq", 1024, "F", BF16), ("bi", 1024, "T", BF16), ("bff", 1024, "T", F32), ("bfb", 1024, "T", F32),
    ("bg", 1024, "F", BF16),
    ("cu", 1024, "F", BF16), ("cglu", 1024, "F", BF16), ("cg", 1024, "F", BF16),
    ("dcq", 768, "F", BF16), ("dckv", 512, "F", BF16), ("dkr", 64, "F", BF16), ("dg", 1024, "F", BF16),
]
GOFF = {}
_o = 0
for _n, _w, _l, _d in GROUPS:
    GOFF[_n] = (_o, _w, _l, _d)
    _o += _w
assert _o == IN_W


class Trk:
    __slots__ = ("name", "lw", "rd", "dsem", "psum")

    def __init__(self, name=""):
        self.name = name
        self.psum = False
        self.lw = None
        self.rd = {}
        self.dsem = None


class Eng:
    def __init__(self, name, eng, sem):
        self.name = name
        self.eng = eng
        self.sem = sem
        self.cnt = 0
        self.pending = False
        self.seen = {}


class KB:
    def __init__(self, nc, n_dma_sems=80):
        self.nc = nc
        self.es = contextlib.ExitStack()
        self.sems = {}
        self.engs = {}
        for name, eng in (("pe", nc.tensor), ("act", nc.scalar), ("dve", nc.vector),
                          ("pool", nc.gpsimd), ("sp", nc.sync)):
            h = self.es.enter_context(nc.semaphore("s_" + name))
            self.sems[name] = h
            self.engs[name] = Eng(name, eng, name)
        self.bar_sem = self.es.enter_context(nc.semaphore("s_bar"))
        self.bar_cnt = 0
        self.cc_sem = self.es.enter_context(nc.semaphore("s_cc"))
        self.cc_cnt = 0
        self.dma_free = []
        self.dma_tot = {}
        for i in range(n_dma_sems):
            k = "d%d" % i
            self.sems[k] = self.es.enter_context(nc.semaphore("s_" + k))
            self.dma_free.append(k)
            self.dma_tot[k] = 0
        self.dma_used = []
        self.trks = []

    def trk(self, name=""):
        t = Trk(name)
        self.trks.append(t)
        return t

    def _wait(self, e, semkey, val):
        if e.seen.get(semkey, 0) >= val:
            return
        e.eng.wait_ge(self.sems[semkey], val)
        e.seen[semkey] = val

    def _deps(self, e, reads, writes, acc):
        deps = {}
        reads = [getattr(t, "k", t) for t in reads]
        writes = [getattr(t, "k", t) for t in writes]

        def add(tok):
            if tok is None:
                return
            k, v = tok
            if deps.get(k, 0) < v:
                deps[k] = v

        for t in reads:
            add(t.lw)
            if t.psum:
                for k, v in t.rd.items():
                    if k != e.sem:
                        add((k, v))
        for t in writes:
            if not (acc and t.lw is not None and t.lw[0] == e.sem):
                add(t.lw)
            for k, v in t.rd.items():
                add((k, v))
        for k, v in deps.items():
            if k in self.dma_tot:
                v = self.dma_tot[k]
            self._wait(e, k, v)

    def _commit(self, tok, reads, writes):
        k, v = tok
        reads = [getattr(t, "k", t) for t in reads]
        writes = [getattr(t, "k", t) for t in writes]
        for t in reads:
            if t.rd.get(k, 0) < v:
                t.rd[k] = v
        for t in writes:
            t.lw = tok
            t.rd = {}

    def op(self, en, fn, reads=(), writes=(), inc=True, acc=False):
        e = self.engs[en]
        self._deps(e, reads, writes, acc)
        ins = fn(e.eng)
        if inc:
            e.cnt += 1
            ins.then_inc(self.sems[e.sem], 1)
            e.pending = False
            tok = (e.sem, e.cnt)
        else:
            e.pending = True
            tok = (e.sem, e.cnt + 1)
        self._commit(tok, reads, writes)
        return ins

    def dma(self, q, out, in_, sb, reads=(), writes=(), **kw):
        e = self.engs[q]
        sb = getattr(sb, "k", sb)
        if sb.dsem is None:
            sb.dsem = self.dma_free.pop()
            self.dma_used.append(sb)
        self._deps(e, reads, writes, False)
        ins = e.eng.dma_start(out=out, in_=in_, **kw)
        k = sb.dsem
        self.dma_tot[k] += 16
        ins.then_inc(self.sems[k], 16)
        self._commit((k, self.dma_tot[k]), reads, writes)
        return ins

    def collective(self, kind, ins, outs, groups):
        e = self.engs["pool"]
        i = e.eng.collective_compute(kind, ALU.bypass, replica_groups=groups,
                                     ins=[a.opt() for a in ins], outs=[a.opt() for a in outs])
        self.cc_cnt += 1
        i.then_inc(self.cc_sem)
        return i

    def barrier(self):
        sp = self.engs["sp"]
        if self.cc_cnt > 0:
            sp.eng.wait_ge(self.cc_sem, self.cc_cnt)
        for en, e in self.engs.items():
            assert not e.pending, "engine %s has un-inc'ed trailing instruction" % en
            if en != "sp" and e.cnt > 0:
                self._wait(sp, e.sem, e.cnt)
        for k, v in self.dma_tot.items():
            if v > 0:
                self._wait(sp, k, v)
        self.bar_cnt += 1
        sp.eng.sem_inc(self.bar_sem, 1)
        for en, e in self.engs.items():
            e.eng.wait_ge(self.bar_sem, self.bar_cnt)
            for en2, e2 in self.engs.items():
                e.seen[e2.sem] = e2.cnt
            for k, v in self.dma_tot.items():
                e.seen[k] = v
        for t in self.dma_used:
            self.dma_free.append(t.dsem)
            t.dsem = None
        self.dma_used = []
        for t in self.trks:
            t.lw = None
            t.rd = {}
        self.trks = []


class T:
    __slots__ = ("t", "k")

    def __init__(self, t, k):
        self.t = t
        self.k = k

    def __getitem__(self, key):
        return self.t[key]


class Stage:
    def __init__(self, kb):
        self.kb = kb
        self.es = contextlib.ExitStack()

    def __enter__(self):
        self.es.__enter__()
        return self

    def __exit__(self, *a):
        return self.es.__exit__(*a)

    _uid = [0]

    def sb(self, name, shape, dtype):
        Stage._uid[0] += 1
        t = self.es.enter_context(self.kb.nc.sbuf_tensor("sb%d_%s" % (Stage._uid[0], name), list(shape), dtype))
        return t

    def tile(self, name, shape, dtype):
        return T(self.sb(name, shape, dtype), self.kb.trk(name))

    def ptile(self, name, shape, dtype=F32):
        t = T(self.ps(name, shape, dtype), self.kb.trk(name))
        t.k.psum = True
        return t

    def ps(self, name, shape, dtype=F32):
        Stage._uid[0] += 1
        t = self.es.enter_context(self.kb.nc.psum_tensor("ps%d_%s" % (Stage._uid[0], name), list(shape), dtype))
        return t


def stage_mod(kb, io, l, st_keep):
    nc = kb.nc
    modcol = [st_keep.sb("modcol%d" % t, [128, 64], F32) for t in range(2)]
    gate_bc = [st_keep.sb("gatebc%d" % t, [128, D], F32) for t in range(2)]
    t_modcol = [kb.trk("modcol") for _ in range(2)]
    t_gate = [kb.trk("gatebc") for _ in range(2)]
    with Stage(kb) as st:
        cT = st.sb("cT", [128, KC, 2], F32)
        scT = st.sb("scT", [128, KC, 2], BF16)
        ones_row = st.sb("ones_row", [1, 128], F32)
        nwcol = st.sb("nwcol", [128, KC], F32)
        brow = [st.sb("brow%d" % i, [1, 512], F32) for i in range(2)]
        row = [st.sb("row%d" % i, [1, 512], F32) for i in range(4)]
        wm = [st.sb("wm%d" % i, [128, KC, 512], BF16) for i in range(2)]
        pacc = [st.ps("pacc%d" % i, [1, 512]) for i in range(2)]
        pcol = st.ps("pcol", [128, 512])
        pbc = [st.ps("pbc%d" % i, [128, 512]) for i in range(2)]
        t_cT, t_scT, t_ones, t_nw = kb.trk(), kb.trk(), kb.trk(), kb.trk()
        t_brow = [kb.trk() for _ in range(2)]
        t_row = [kb.trk() for _ in range(4)]
        t_wm = [kb.trk() for _ in range(2)]
        t_pacc = [kb.trk() for _ in range(2)]
        t_pcol = kb.trk()
        t_pbc = [kb.trk() for _ in range(2)]
        for _t in t_pacc + [t_pcol] + t_pbc:
            _t.psum = True

        kb.dma("sp", cT[:], io["cT"], t_cT, writes=[t_cT])
        kb.dma("sp", nwcol[:], io["normwT"][l], t_nw, writes=[t_nw])
        kb.op("act", lambda e: e.activation(out=scT[:], in_=cT[:], func=AF.Silu),
              reads=[t_cT], writes=[t_scT])
        kb.op("dve", lambda e: e.memset(ones_row[:], 1.0), writes=[t_ones])
        ri = 0
        for j in range(24):
            wb, twb = wm[j % 2], t_wm[j % 2]
            wsrc = io["w_mod"][l - WL[0], :, j * 512:(j + 1) * 512].rearrange("(kc p) c -> p kc c", p=128)
            for kq in range(4):
                kb.dma("pool", wb[:, kq * 8:(kq + 1) * 8, :], wsrc[:, kq * 8:(kq + 1) * 8, :], twb, writes=[twb])
            bb, tbb = brow[j % 2], t_brow[j % 2]
            kb.dma("sp", bb[:], io["b_mod"][l - WL[0]:l - WL[0] + 1, j * 512:(j + 1) * 512], tbb, writes=[tbb])
            for t in range(2):
                pa, tpa = pacc[t], t_pacc[t]
                for kc in range(KC):
                    kb.op("pe", lambda e, kc=kc: e.matmul(pa[:], lhsT=scT[:, kc, t:t + 1], rhs=wb[:, kc, :],
                                                         start=(kc == 0), stop=(kc == KC - 1)),
                          reads=[t_scT, twb], writes=[tpa], inc=(kc == KC - 1), acc=(kc > 0))
                rw, trw = row[ri % 4], t_row[ri % 4]
                ri += 1
                kb.op("dve", lambda e: e.tensor_tensor(out=rw[:], in0=pa[:], in1=bb[:], op=ALU.add),
                      reads=[tpa, tbb], writes=[trw])
                if j < 16:
                    for s in range(4):
                        kb.op("pe", lambda e, s=s: e.matmul(pcol[:, s:s + 1], lhsT=rw[0:1, s * 128:(s + 1) * 128],
                                                           rhs=ones_row[0:1, 0:1], start=True, stop=True),
                              reads=[trw, t_ones], writes=[t_pcol], inc=(s == 3), acc=(s > 0))
                    c0 = j * 4
                    if j < 8:
                        kb.op("dve", lambda e: e.tensor_copy(out=modcol[t][:, c0:c0 + 4], in_=pcol[:, 0:4]),
                              reads=[t_pcol], writes=[t_modcol[t]])
                    else:
                        kb.op("dve", lambda e: e.scalar_tensor_tensor(
                            out=modcol[t][:, c0:c0 + 4], in0=pcol[:, 0:4], scalar=1.0,
                            in1=nwcol[:, c0 - 32:c0 - 32 + 4], op0=ALU.add, op1=ALU.mult),
                            reads=[t_pcol, t_nw], writes=[t_modcol[t]])
                else:
                    pb, tpb = pbc[t], t_pbc[t]
                    kb.op("pe", lambda e: e.matmul(pb[:], lhsT=ones_row[0:1, :], rhs=rw[0:1, :],
                                                   start=True, stop=True),
                          reads=[trw, t_ones], writes=[tpb])
                    g0 = (j - 16) * 512
                    kb.op("act", lambda e: e.copy(out=gate_bc[t][:, g0:g0 + 512], in_=pb[:]),
                          reads=[tpb], writes=[t_gate[t]])
        kb.barrier()
    return modcol, gate_bc


def stage_norm(kb, io, l, st_keep, modcol, x_src, ctx_src):
    nc = kb.nc
    hT = st_keep.sb("hT", [128, KC, TL], BF16)
    t_hT = kb.trk("hT")
    with Stage(kb) as st:
        ident = st.sb("ident", [128, 128], BF16)
        xt = [st.sb("xt%d" % i, [128, D], F32) for i in range(2)]
        xn = [st.sb("xn%d" % i, [128, D], BF16) for i in range(2)]
        junk = st.sb("junk", [128, D], BF16)
        ss = [st.sb("ss%d" % i, [128, 1], F32) for i in range(2)]
        rs = [st.sb("rs%d" % i, [128, 1], F32) for i in range(2)]
        tmp = [st.sb("tmp%d" % i, [128, 8, 128], F32) for i in range(2)]
        ptr = [st.ps("ptr%d" % i, [128, 8, 128], BF16) for i in range(4)]
        t_id = kb.trk()
        t_xt = [kb.trk() for _ in range(2)]
        t_xn = [kb.trk() for _ in range(2)]
        t_junk = kb.trk()
        t_ss = [kb.trk() for _ in range(2)]
        t_rs = [kb.trk() for _ in range(2)]
        t_tmp = [kb.trk() for _ in range(2)]
        t_ptr = [kb.trk() for _ in range(4)]
        for _t in t_ptr:
            _t.psum = True
        kb.dma("sp", ident[:], io["ident"], t_id, writes=[t_id])
        pi = 0
        for i in range(NT):
            b = i % 2
            t = 1 if i < 2 else 0
            src = ctx_src[i * 128:(i + 1) * 128, :] if i < 2 else x_src[(i - 2) * 128:(i - 1) * 128, :]
            kb.dma("sp", xt[b][:], src, t_xt[b], writes=[t_xt[b]])
            kb.op("act", lambda e: e.activation(out=junk[:], in_=xt[b][:], func=AF.Square, accum_out=ss[b][:]),
                  reads=[t_xt[b]], writes=[t_junk, t_ss[b]])
            kb.op("act", lambda e: e.activation(out=ss[b][:], in_=ss[b][:], func=AF.Sqrt, scale=1.0 / D, bias=EPS),
                  reads=[t_ss[b]], writes=[t_ss[b]])
            kb.op("dve", lambda e: e.reciprocal(out=rs[b][:], in_=ss[b][:]),
                  reads=[t_ss[b]], writes=[t_rs[b]])
            kb.op("act", lambda e: e.activation(out=xn[b][:], in_=xt[b][:], func=AF.Copy, scale=rs[b][:, 0:1]),
                  reads=[t_xt[b], t_rs[b]], writes=[t_xn[b]])
            for g in range(4):
                p, tp = ptr[pi % 4], t_ptr[pi % 4]
                pi += 1
                for q in range(8):
                    kc = g * 8 + q
                    kb.op("pe", lambda e, kc=kc, q=q: e.transpose(p[:, q, :], xn[b][:, kc * 128:(kc + 1) * 128], ident[:]),
                          reads=[t_xn[b], t_id], writes=[tp], inc=(q == 7), acc=(q > 0))
                tm, ttm = tmp[g % 2], t_tmp[g % 2]
                s1 = modcol[t][:, 32 + g * 8:32 + g * 8 + 8].unsqueeze(2).broadcast_to([128, 8, 128])
                sh = modcol[t][:, g * 8:g * 8 + 8].unsqueeze(2).broadcast_to([128, 8, 128])
                kb.op("dve", lambda e: e.tensor_tensor(out=tm[:], in0=p[:], in1=s1, op=ALU.mult),
                      reads=[tp], writes=[ttm])
                kb.op("pool", lambda e: e.tensor_tensor(out=hT[:, g * 8:(g + 1) * 8, i * 128:(i + 1) * 128],
                                                        in0=tm[:], in1=sh, op=ALU.add),
                      reads=[ttm], writes=[t_hT])
        kb.barrier()
    return hT


TOKBLK = [(0, 512), (512, 512), (1024, 256)]


def stage_inproj(kb, io, l, hT, P):
    with Stage(kb) as st:
        wb = [st.sb("wb%d" % i, [128, KC, 512], BF16) for i in range(2)]
        t_wb = [kb.trk() for _ in range(2)]
        stF = [st.sb("stF%d" % i, [128, TL], BF16) for i in range(3)]
        t_stF = [kb.trk() for _ in range(3)]
        stT = [st.sb("stT%d" % i, [128, 512], BF16) for i in range(3)]
        stT32 = [st.sb("stT32_%d" % i, [128, 512], F32) for i in range(3)]
        t_stT = [kb.trk() for _ in range(3)]
        pacc = [st.ps("pa%d" % i, [128, 512]) for i in range(4)]
        t_pacc = [kb.trk() for _ in range(4)]
        for _t in t_pacc:
            _t.psum = True
        t_hT = kb.trk()
        wi = 0
        pi = 0
        fi = 0
        ti = 0
        ev = 0
        for name, width, lay, dt in GROUPS:
            g0 = GOFF[name][0]
            for c0 in range(0, width, 512):
                ncol = min(512, width - c0)
                w, tw = wb[wi % 2], t_wb[wi % 2]
                wi += 1
                wsrc = io["w_in"][l - WL[0], :, g0 + c0:g0 + c0 + ncol].rearrange("(kc p) c -> p kc c", p=128)
                for kq in range(4):
                    kb.dma("pool", w[:, kq * 8:(kq + 1) * 8, 0:ncol], wsrc[:, kq * 8:(kq + 1) * 8, :], tw, writes=[tw])
                if lay == "F":
                    for s0 in range(0, ncol, 128):
                        ns = min(128, ncol - s0)
                        sf, tsf = stF[fi % 3], t_stF[fi % 3]
                        fi += 1
                        for (t0, tn) in TOKBLK:
                            pa, tpa = pacc[pi % 4], t_pacc[pi % 4]
                            pi += 1
                            for kc in range(KC):
                                kb.op("pe", lambda e, kc=kc: e.matmul(pa[0:ns, 0:tn], lhsT=w[:, kc, s0:s0 + ns],
                                                                     rhs=hT[:, kc, t0:t0 + tn],
                                                                     start=(kc == 0), stop=(kc == KC - 1)),
                                      reads=[tw, t_hT], writes=[tpa], inc=(kc == KC - 1), acc=(kc > 0))
                            en = "act" if ev % 2 == 0 else "dve"
                            ev += 1
                            if en == "act":
                                kb.op("act", lambda e: e.copy(out=sf[0:ns, t0:t0 + tn], in_=pa[0:ns, 0:tn]),
                                      reads=[tpa], writes=[tsf])
                            else:
                                kb.op("dve", lambda e: e.tensor_copy(out=sf[0:ns, t0:t0 + tn], in_=pa[0:ns, 0:tn]),
                                      reads=[tpa], writes=[tsf])
                        kb.dma("sp", P[name][c0 + s0:c0 + s0 + ns, :], sf[0:ns, :], tsf, reads=[tsf])
                else:
                    for i in range(NT):
                        pa, tpa = pacc[pi % 4], t_pacc[pi % 4]
                        pi += 1
                        for kc in range(KC):
                            kb.op("pe", lambda e, kc=kc: e.matmul(pa[:, 0:ncol], lhsT=hT[:, kc, i * 128:(i + 1) * 128],
                                                                 rhs=w[:, kc, 0:ncol],
                                                                 start=(kc == 0), stop=(kc == KC - 1)),
                                  reads=[tw, t_hT], writes=[tpa], inc=(kc == KC - 1), acc=(kc > 0))
                        stt = (stT32 if dt == F32 else stT)[ti % 3]
                        tst = t_stT[ti % 3]
                        ti += 1
                        en = "act" if ev % 2 == 0 else "dve"
                        ev += 1
                        if en == "act":
                            kb.op("act", lambda e: e.copy(out=stt[:, 0:ncol], in_=pa[:, 0:ncol]),
                                  reads=[tpa], writes=[tst])
                        else:
                            kb.op("dve", lambda e: e.tensor_copy(out=stt[:, 0:ncol], in_=pa[:, 0:ncol]),
                                  reads=[tpa], writes=[tst])
                        kb.dma("sp", P[name][i * 128:(i + 1) * 128, c0:c0 + ncol], stt[:, 0:ncol], tst, reads=[tst])
        kb.barrier()


def attn_core(kb, bufs, q_tiles, k_tiles, V, sg, ysb, blocks, scale):
    psS, pO, pD, E, ones_bf, rden, tmpo = bufs
    si = 0
    for (q0, qn, ktl) in blocks:
        for ii, kt in enumerate(ktl):
            pS = psS[si % 2]
            Eb = E[si % 3]
            si += 1
            for qi, ((qt, kp), (ktile, kp2)) in enumerate(zip(q_tiles, k_tiles)):
                kb.op("pe", lambda e: e.matmul(pS[:, 0:qn], lhsT=ktile[0:kp, kt * 128:(kt + 1) * 128],
                                               rhs=qt[0:kp, q0:q0 + qn],
                                               start=(qi == 0), stop=(qi == len(q_tiles) - 1)),
                      reads=[ktile, qt], writes=[pS], inc=(qi == len(q_tiles) - 1), acc=(qi > 0))
            kb.op("act", lambda e: e.activation(out=Eb[:, 0:qn], in_=pS[:, 0:qn], func=AF.Exp, scale=scale),
                  reads=[pS], writes=[Eb])
            first, last = (ii == 0), (ii == len(ktl) - 1)
            kb.op("pe", lambda e: e.matmul(pO[:, 0:qn], lhsT=V[:, kt, :], rhs=Eb[:, 0:qn], start=first, stop=last),
                  reads=[V, Eb], writes=[pO], inc=False, acc=(not first))
            kb.op("pe", lambda e: e.matmul(pD[:, 0:qn], lhsT=ones_bf[:, :], rhs=Eb[:, 0:qn], start=first, stop=last),
                  reads=[ones_bf, Eb], writes=[pD], inc=True, acc=(not first))
        kb.op("dve", lambda e: e.reciprocal(out=rden[:, 0:qn], in_=pD[:, 0:qn]), reads=[pD], writes=[rden])
        kb.op("dve", lambda e: e.tensor_tensor(out=tmpo[:, 0:qn], in0=pO[:, 0:qn], in1=rden[:, 0:qn], op=ALU.mult),
              reads=[pO, rden], writes=[tmpo])
        kb.op("pool", lambda e: e.tensor_tensor(out=ysb[:, q0:q0 + qn], in0=tmpo[:, 0:qn], in1=sg[:, q0:q0 + qn],
                                                op=ALU.mult),
              reads=[tmpo, sg], writes=[ysb])


def attn_bufs(kb, st, ones_bf):
    psS = [st.ptile("psS%d" % i, [128, 512]) for i in range(2)]
    pO = st.ptile("pO", [128, 512])
    pD = st.ptile("pD", [128, 512])
    E = [st.tile("E%d" % i, [128, 512], BF16) for i in range(3)]
    rden = st.tile("rden", [128, 512], F32)
    tmpo = st.tile("tmpo", [128, 512], F32)
    return (psS, pO, pD, E, ones_bf, rden, tmpo)


def rstd_from_psum(kb, pss, sd, rstd, n, inv_dim, rows=128):
    kb.op("act", lambda e: e.activation(out=sd[0:rows, 0:n], in_=pss[0:rows, 0:n], func=AF.Sqrt, scale=inv_dim, bias=EPS),
          reads=[pss], writes=[sd])
    kb.op("dve", lambda e: e.reciprocal(out=rstd[0:rows, 0:n], in_=sd[0:rows, 0:n]), reads=[sd], writes=[rstd])


def rope_apply(kb, xn, rm, cos, sin, c0, out, o0, n, kp, prot, t1, t2):
    kb.op("pe", lambda e: e.matmul(prot[0:kp, 0:n], lhsT=rm[0:kp, 0:kp], rhs=xn[0:kp, 0:n], start=True, stop=True),
          reads=[rm, xn], writes=[prot])
    kb.op("pool", lambda e: e.tensor_tensor(out=t1[0:kp, 0:n], in0=xn[0:kp, 0:n], in1=cos[0:kp, c0:c0 + n], op=ALU.mult),
          reads=[xn, cos], writes=[t1])
    kb.op("dve", lambda e: e.tensor_tensor(out=t2[0:kp, 0:n], in0=prot[0:kp, 0:n], in1=sin[0:kp, c0:c0 + n], op=ALU.mult),
          reads=[prot, sin], writes=[t2])
    kb.op("pool", lambda e: e.tensor_tensor(out=out[0:kp, o0:o0 + n], in0=t1[0:kp, 0:n], in1=t2[0:kp, 0:n], op=ALU.add),
          reads=[t1, t2], writes=[out])


def load_const(kb, st, io, name, shape, dtype, src=None):
    t = st.tile("c_" + name, shape, dtype)
    kb.dma("sp", t[:], io[name] if src is None else src, t, writes=[t])
    return t


def qk_norm_rope_128(kb, st_b, raw, wcol, rm, cos, sin, out, ones_bf):
    sq, pss, sd, rstd, xn, prot, t1, t2 = st_b
    for (t0, tn) in TOKBLK:
        kb.op("act", lambda e: e.activation(out=sq[:, 0:tn], in_=raw[:, t0:t0 + tn], func=AF.Square),
              reads=[raw], writes=[sq])
        kb.op("pe", lambda e: e.matmul(pss[:, 0:tn], lhsT=ones_bf[:, :], rhs=sq[:, 0:tn], start=True, stop=True),
              reads=[ones_bf, sq], writes=[pss])
        rstd_from_psum(kb, pss, sd, rstd, tn, 1.0 / 128)
        kb.op("dve", lambda e: e.scalar_tensor_tensor(out=xn[:, 0:tn], in0=raw[:, t0:t0 + tn], scalar=wcol,
                                                      in1=rstd[:, 0:tn], op0=ALU.mult, op1=ALU.mult),
              reads=[raw, rstd], writes=[xn])
        rope_apply(kb, xn, rm, cos, sin, t0, out, t0, tn, 128, prot, t1, t2)


def normrope_bufs(kb, st):
    sq = st.tile("nr_sq", [128, 512], BF16)
    pss = st.ptile("nr_pss", [128, 512])
    sd = st.tile("nr_sd", [128, 512], F32)
    rstd = st.tile("nr_rstd", [128, 512], F32)
    xn = st.tile("nr_xn", [128, 512], BF16)
    prot = st.ptile("nr_prot", [128, 512])
    t1 = st.tile("nr_t1", [128, 512], F32)
    t2 = st.tile("nr_t2", [128, 512], F32)
    return (sq, pss, sd, rstd, xn, prot, t1, t2)


def stage_x_gqa(kb, io, l, P, XK):
    with Stage(kb) as st:
        ones_bf = load_const(kb, st, io, "ones_bf", [128, 128], BF16)
        rmA = load_const(kb, st, io, "rmA", [128, 128], BF16)
        cosA = load_const(kb, st, io, "cosA", [128, TL], F32)
        sinA = load_const(kb, st, io, "sinA", [128, TL], F32)
        cols = load_const(kb, st, io, "cols", [128, NCOLS], F32, src=io["cols"][l])
        nb = normrope_bufs(kb, st)
        for g in range(2):
            raw = st.tile("kraw%d" % g, [128, TL], BF16)
            out = st.tile("kout%d" % g, [128, TL], BF16)
            kb.dma("sp", raw[:], P["ak"][g * 128:(g + 1) * 128, :], raw, writes=[raw])
            qk_norm_rope_128(kb, nb, raw, cols[:, 1:2], rmA, cosA, sinA, out, ones_bf)
            kb.dma("sp", XK[g], out[:], out, reads=[out])
        kb.barrier()


def stage_attn_A(kb, io, l, P, G, Y, ctx_out):
    with Stage(kb) as st:
        ones_bf = load_const(kb, st, io, "ones_bf", [128, 128], BF16)
        rmA = load_const(kb, st, io, "rmA", [128, 128], BF16)
        cosA = load_const(kb, st, io, "cosA", [128, TL], F32)
        sinA = load_const(kb, st, io, "sinA", [128, TL], F32)
        cols = load_const(kb, st, io, "cols", [128, NCOLS], F32, src=io["cols"][l])
        nb = normrope_bufs(kb, st)
        ab = attn_bufs(kb, st, ones_bf)
        kT = st.tile("kT", [128, NKEY], BF16)
        V = st.tile("V", [128, NKEY // 128, 128], BF16)
        qraw = [st.tile("qraw%d" % i, [128, TL], BF16) for i in range(2)]
        graw = [st.tile("graw%d" % i, [128, TL], BF16) for i in range(2)]
        qr = [st.tile("qr%d" % i, [128, TL], BF16) for i in range(2)]
        sg = [st.tile("sg%d" % i, [128, TL], F32) for i in range(2)]
        ysb = [st.tile("ysb%d" % i, [128, TL], BF16) for i in range(2)]
        blocks = []
        if ctx_out:
            blocks.append((0, NCTX, [0, 1]))
        allk = list(range(NKEY // 128))
        blocks += [(NCTX, 512, allk), (NCTX + 512, 512, allk)]
        t0 = 0 if ctx_out else NCTX
        for g in range(2):
            G.load_kT(kb, kT, g)
            G.load_V(kb, V, g)
            for hh in range(4):
                h = g * 4 + hh
                b = h % 2
                kb.dma("sp", qraw[b][:], P["aq"][h * 128:(h + 1) * 128, :], qraw[b], writes=[qraw[b]])
                kb.dma("sp", graw[b][:], P["ag"][h * 128:(h + 1) * 128, :], graw[b], writes=[graw[b]])
                qk_norm_rope_128(kb, nb, qraw[b], cols[:, 0:1], rmA, cosA, sinA, qr[b], ones_bf)
                kb.op("act", lambda e: e.activation(out=sg[b][:], in_=graw[b][:], func=AF.Silu),
                      reads=[graw[b]], writes=[sg[b]])
                attn_core(kb, ab, [(qr[b], 128)], [(kT, 128)], V, sg[b], ysb[b], blocks, 128 ** -0.5)
                kb.dma("sp", Y[h * 128:(h + 1) * 128, t0:TL], ysb[b][:, t0:TL], ysb[b], reads=[ysb[b]])
        kb.barrier()


NCOLS = 48 + 31 * 8
WL = [0]


def rope_tables(R, tpos):
    half = R // 2
    quarter = half // 2
    inv_freq = (10000.0 ** (-np.arange(quarter, dtype=np.float32) / quarter)).astype(np.float32)
    tpos = np.asarray(tpos)
    rows = (tpos // 64).astype(np.float32)
    cols = (tpos % 64).astype(np.float32)
    cos = np.ones((R, len(tpos)), np.float32)
    sin = np.zeros((R, len(tpos)), np.float32)
    valid = tpos >= 0
    for d in range(R):
        pos = rows if d < half else cols
        i = (d % half) % quarter
        ang = (pos * inv_freq[i]).astype(np.float32)
        cos[d, valid] = np.cos(ang)[valid]
        sin[d, valid] = np.sin(ang)[valid]
    return cos, sin


def rope_rot_matrix(R):
    half = R // 2
    quarter = half // 2
    rm = np.zeros((R, R), np.float32)
    for m in range(R):
        dd = m % half
        if dd < quarter:
            rm[m + quarter, m] = -1.0
        else:
            rm[m - quarter, m] = 1.0
    return rm


def make_cols(inp, l):
    c = np.zeros((128, NCOLS), np.float32)
    c[:, 0] = inp["att_q_norm"][l]
    c[:, 1] = inp["att_k_norm"][l]
    c[:, 2] = inp["mla_qk_q_norm"][l][0:128]
    c[0:64, 3] = inp["mla_qk_q_norm"][l][128:192]
    c[:, 4] = inp["mla_qk_k_norm"][l][0:128]
    c[0:64, 5] = inp["mla_qk_k_norm"][l][128:192]
    c[:, 6:12] = inp["mla_q_norm"][l].reshape(6, 128).T
    c[:, 12:16] = inp["mla_kv_norm"][l].reshape(4, 128).T
    c[:, 16:24] = inp["hgrn_o_norm"][l].reshape(8, 128).T
    c[:, 24:32] = inp["conv_b"][l].reshape(8, 128).T
    c[:, 32:40] = inp["conv_ln_w"][l].reshape(8, 128).T
    c[:, 40:48] = inp["conv_ln_b"][l].reshape(8, 128).T
    c[:, 48:] = inp["conv_w"][l].reshape(31, 8, 128).transpose(2, 0, 1).reshape(128, 31 * 8)
    return c


def local_tpos(j):
    return np.concatenate([-np.ones(NCTX, np.int64), np.arange(NLAT, dtype=np.int64) + NLAT * j])


def key_tpos():
    return np.concatenate([-np.ones(NCTX, np.int64), np.arange(SEQ, dtype=np.int64)])


def rms_tiles_F(kb, st, raw, ntile, wcol0, cols, out, ones_bf, inv_dim, blocks, nbufs):
    sq, pss, sd, rstd = nbufs[0], nbufs[1], nbufs[2], nbufs[3]
    for (t0, tn) in blocks:
        for c in range(ntile):
            kb.op("act", lambda e: e.activation(out=sq[:, 0:tn], in_=raw[:, c, t0:t0 + tn], func=AF.Square),
                  reads=[raw], writes=[sq])
            kb.op("pe", lambda e: e.matmul(pss[:, 0:tn], lhsT=ones_bf[:, :], rhs=sq[:, 0:tn],
                                           start=(c == 0), stop=(c == ntile - 1)),
                  reads=[ones_bf, sq], writes=[pss], acc=(c > 0))
        rstd_from_psum(kb, pss, sd, rstd, tn, inv_dim)
        for c in range(ntile):
            kb.op("dve", lambda e: e.scalar_tensor_tensor(out=out[:, c, t0:t0 + tn], in0=raw[:, c, t0:t0 + tn],
                                                          scalar=cols[:, wcol0 + c:wcol0 + c + 1], in1=rstd[:, 0:tn],
                                                          op0=ALU.mult, op1=ALU.mult),
                  reads=[raw, rstd], writes=[out])


def stage_x_mla(kb, io, l, P, XC):
    with Stage(kb) as st:
        ones_bf = load_const(kb, st, io, "ones_bf", [128, 128], BF16)
        cols = load_const(kb, st, io, "cols", [128, NCOLS], F32, src=io["cols"][l])
        nb = normrope_bufs(kb, st)
        raw = st.tile("ckraw", [128, 4, TL], BF16)
        out = st.tile("ckout", [128, 4, TL], BF16)
        kb.dma("sp", raw[:], P["dckv"].rearrange("(c p) t -> p c t", p=128), raw, writes=[raw])
        rms_tiles_F(kb, st, raw, 4, 12, cols, out, ones_bf, 1.0 / 512, TOKBLK, nb)
        kb.dma("sp", XC.rearrange("(c p) t -> p c t", p=128), out[:], out, reads=[out])
        kb.barrier()


KEYBLK = [(i * 512, min(512, NKEY - i * 512)) for i in range((NKEY + 511) // 512)]


def stage_attn_D(kb, io, l, P, G, Y, ctx_out):
    with Stage(kb) as st:
        ones_bf = load_const(kb, st, io, "ones_bf", [128, 128], BF16)
        rmD = load_const(kb, st, io, "rmD", [64, 64], BF16)
        cols = load_const(kb, st, io, "cols", [128, NCOLS], F32, src=io["cols"][l])
        cosDq = load_const(kb, st, io, "cosDq", [64, TL], F32)
        sinDq = load_const(kb, st, io, "sinDq", [64, TL], F32)
        nb = normrope_bufs(kb, st)
        sq, pss, sd, rstd, xn, prot, t1, t2 = nb
        ab = attn_bufs(kb, st, ones_bf)
        pA = st.ptile("pA", [128, 512])
        pB = st.ptile("pB", [128, 512])
        ckv = st.tile("ckv", [128, 4, NKEY], BF16)
        sqr = st.tile("sqr", [64, NKEY], BF16)
        krr = st.tile("krr", [64, NKEY], F32)
        cqn = st.tile("cqn", [128, 6, TL], BF16)
        G.load_ckv(kb, ckv)
        with Stage(kb) as s0:
            cosDk = load_const(kb, s0, io, "cosDk", [64, NKEY], F32)
            sinDk = load_const(kb, s0, io, "sinDk", [64, NKEY], F32)
            kr = s0.tile("kr", [64, NKEY], BF16)
            cqraw = s0.tile("cqraw", [128, 6, TL], BF16)
            G.load_kr(kb, kr)
            kb.dma("sp", cqraw[:], P["dcq"].rearrange("(c p) t -> p c t", p=128), cqraw, writes=[cqraw])
            kb.op("act", lambda e: e.activation(out=sqr[:], in_=kr[:], func=AF.Square), reads=[kr], writes=[sqr])
            for (k0, kn_) in KEYBLK:
                kb.op("dve", lambda e: e.tensor_scalar(out=xn[0:64, 0:kn_], in0=kr[:, k0:k0 + kn_], scalar1=cols[0:64, 5:6],
                                                       scalar2=None, op0=ALU.mult),
                      reads=[kr], writes=[xn])
                rope_apply(kb, xn, rmD, cosDk, sinDk, k0, krr, k0, kn_, 64, prot, t1, t2)
            rms_tiles_F(kb, s0, cqraw, 6, 6, cols, cqn, ones_bf, 1.0 / 768, TOKBLK, nb)
            kb.barrier()
        wuq = [st.tile("wuq%d" % i, [128, 6, 192], BF16) for i in range(2)]
        wukv = [st.tile("wukv%d" % i, [128, 4, 256], BF16) for i in range(2)]
        qnope = [st.tile("qnope%d" % i, [128, TL], BF16) for i in range(2)]
        qrope = [st.tile("qrope%d" % i, [64, TL], BF16) for i in range(2)]
        kn = st.tile("kn", [128, NKEY], BF16)
        krh = st.tile("krh", [64, NKEY], BF16)
        V = st.tile("Vd", [128, NKEY // 128, 128], BF16)
        graw = [st.tile("dgraw%d" % i, [128, TL], BF16) for i in range(2)]
        sg = [st.tile("dsg%d" % i, [128, TL], F32) for i in range(2)]
        ysb = [st.tile("dysb%d" % i, [128, TL], BF16) for i in range(2)]
        sq2 = st.tile("sq2", [64, 512], BF16)
        blocks = []
        if ctx_out:
            blocks.append((0, NCTX, [0, 1]))
        allk = list(range(NKEY // 128))
        blocks += [(NCTX, 512, allk), (NCTX + 512, 512, allk)]
        t0o = 0 if ctx_out else NCTX
        for h in range(8):
            b = h % 2
            kb.dma("pool", wuq[b][:], io["mla_w_uq"][l - WL[0], :, h * 192:(h + 1) * 192].rearrange("(c p) n -> p c n", p=128),
                   wuq[b], writes=[wuq[b]])
            kb.dma("pool", wukv[b][:], io["mla_w_ukv"][l - WL[0], :, h * 256:(h + 1) * 256].rearrange("(c p) n -> p c n", p=128),
                   wukv[b], writes=[wukv[b]])
            kb.dma("sp", graw[b][:], P["dg"][h * 128:(h + 1) * 128, :], graw[b], writes=[graw[b]])
            kb.op("act", lambda e: e.activation(out=sg[b][:], in_=graw[b][:], func=AF.Silu), reads=[graw[b]], writes=[sg[b]])
            for (t0, tn) in TOKBLK:
                for c in range(6):
                    kb.op("pe", lambda e: e.matmul(pA[:, 0:tn], lhsT=wuq[b][:, c, 0:128], rhs=cqn[:, c, t0:t0 + tn],
                                                   start=(c == 0), stop=(c == 5)),
                          reads=[wuq[b], cqn], writes=[pA], inc=(c == 5), acc=(c > 0))
                for c in range(6):
                    kb.op("pe", lambda e: e.matmul(pB[0:64, 0:tn], lhsT=wuq[b][:, c, 128:192], rhs=cqn[:, c, t0:t0 + tn],
                                                   start=(c == 0), stop=(c == 5)),
                          reads=[wuq[b], cqn], writes=[pB], inc=(c == 5), acc=(c > 0))
                kb.op("act", lambda e: e.activation(out=sq[:, 0:tn], in_=pA[:, 0:tn], func=AF.Square), reads=[pA], writes=[sq])
                kb.op("act", lambda e: e.activation(out=sq2[:, 0:tn], in_=pB[0:64, 0:tn], func=AF.Square), reads=[pB], writes=[sq2])
                kb.op("pe", lambda e: e.matmul(pss[:, 0:tn], lhsT=ones_bf[:, :], rhs=sq[:, 0:tn], start=True, stop=False),
                      reads=[ones_bf, sq], writes=[pss], inc=False)
                kb.op("pe", lambda e: e.matmul(pss[:, 0:tn], lhsT=ones_bf[0:64, :], rhs=sq2[:, 0:tn], start=False, stop=True),
                      reads=[ones_bf, sq2], writes=[pss], acc=True)
                rstd_from_psum(kb, pss, sd, rstd, tn, 1.0 / 192)
                kb.op("dve", lambda e: e.scalar_tensor_tensor(out=qnope[b][:, t0:t0 + tn], in0=pA[:, 0:tn], scalar=cols[:, 2:3],
                                                              in1=rstd[:, 0:tn], op0=ALU.mult, op1=ALU.mult),
                      reads=[pA, rstd], writes=[qnope[b]])
                kb.op("dve", lambda e: e.scalar_tensor_tensor(out=xn[0:64, 0:tn], in0=pB[0:64, 0:tn], scalar=cols[0:64, 3:4],
                                                              in1=rstd[0:64, 0:tn], op0=ALU.mult, op1=ALU.mult),
                      reads=[pB, rstd], writes=[xn])
                rope_apply(kb, xn, rmD, cosDq, sinDq, t0, qrope[b], t0, tn, 64, prot, t1, t2)
            for (k0, kn_) in KEYBLK:
                for c in range(4):
                    kb.op("pe", lambda e: e.matmul(pA[:, 0:kn_], lhsT=wukv[b][:, c, 0:128], rhs=ckv[:, c, k0:k0 + kn_],
                                                   start=(c == 0), stop=(c == 3)),
                          reads=[wukv[b], ckv], writes=[pA], inc=(c == 3), acc=(c > 0))
                kb.op("act", lambda e: e.activation(out=sq[:, 0:kn_], in_=pA[:, 0:kn_], func=AF.Square), reads=[pA], writes=[sq])
                kb.op("pe", lambda e: e.matmul(pss[:, 0:kn_], lhsT=ones_bf[:, :], rhs=sq[:, 0:kn_], start=True, stop=False),
                      reads=[ones_bf, sq], writes=[pss], inc=False)
                kb.op("pe", lambda e: e.matmul(pss[:, 0:kn_], lhsT=ones_bf[0:64, :], rhs=sqr[:, k0:k0 + kn_], start=False, stop=True),
                      reads=[ones_bf, sqr], writes=[pss], acc=True)
                rstd_from_psum(kb, pss, sd, rstd, kn_, 1.0 / 192)
                kb.op("dve", lambda e: e.scalar_tensor_tensor(out=kn[:, k0:k0 + kn_], in0=pA[:, 0:kn_], scalar=cols[:, 4:5],
                                                              in1=rstd[:, 0:kn_], op0=ALU.mult, op1=ALU.mult),
                      reads=[pA, rstd], writes=[kn])
                kb.op("pool", lambda e: e.tensor_tensor(out=krh[:, k0:k0 + kn_], in0=krr[:, k0:k0 + kn_], in1=rstd[0:64, 0:kn_],
                                                        op=ALU.mult),
                      reads=[krr, rstd], writes=[krh])
            nkt = NKEY // 128
            for k4 in range(0, nkt, 4):
                n4 = min(4, nkt - k4)
                for i in range(n4):
                    kt = k4 + i
                    for c in range(4):
                        kb.op("pe", lambda e: e.matmul(pB[:, i * 128:(i + 1) * 128], lhsT=ckv[:, c, kt * 128:(kt + 1) * 128],
                                                       rhs=wukv[b][:, c, 128:256], start=(c == 0), stop=(c == 3)),
                              reads=[wukv[b], ckv], writes=[pB], inc=(c == 3 and i == n4 - 1), acc=(c > 0 or i > 0))
                kb.op("act", lambda e: e.copy(out=V[:, k4:k4 + n4, :], in_=pB[:, 0:n4 * 128].rearrange("p (a b) -> p a b", b=128)),
                      reads=[pB], writes=[V])
            attn_core(kb, ab, [(qnope[b], 128), (qrope[b], 64)], [(kn, 128), (krh, 64)], V, sg[b], ysb[b], blocks, 192 ** -0.5)
            kb.dma("sp", Y[3072 + h * 128:3072 + (h + 1) * 128, t0o:TL], ysb[b][:, t0o:TL], ysb[b], reads=[ysb[b]])
        kb.barrier()


XO_CTX = 15
XO_LAT = 15 + NCTX + 15 + 15
XW = XO_LAT + NLAT + 15
CVW = XW - 30
CVBLK = [(0, 512), (512, 512), (1024, CVW - 1024)]
NTAP_DVE = 31


def stage_conv(kb, io, l, P, G, Y, ctx_out):
    with Stage(kb) as st:
        ones_bf = load_const(kb, st, io, "ones_bf", [128, 128], BF16)
        cols = load_const(kb, st, io, "cols", [128, NCOLS], F32, src=io["cols"][l])
        cv = st.tile("cv", [128, 8, CVW], F32)
        ybf = st.tile("ybf", [128, 8, CVW], BF16)
        ysq = st.tile("ysq", [128, 8, CVW], BF16)
        u = [st.tile("cu%d" % i, [128, TL], BF16) for i in range(2)]
        glu = [st.tile("cglu%d" % i, [128, TL], BF16) for i in range(2)]
        hu = [st.tile("hu%d" % i, [128, 4, 32], BF16) for i in range(2)]
        hg = [st.tile("hg%d" % i, [128, 4, 32], BF16) for i in range(2)]
        sig = st.tile("csig", [128, TL], F32)
        hsig = st.tile("chsig", [128, 4, 32], F32)
        hxg = st.tile("chxg", [128, 4, 32], F32)
        hsel = st.tile("chsel", [128, 2, 16], F32)
        hmk = load_const(kb, st, io, "halomask", [128, 2, 4], F32)
        xg = [st.tile("xg%d" % i, [128, XW], F32) for i in range(2)]
        acc2 = st.tile("acc2", [128, CVW], F32)
        for i in range(2):
            kb.op("pool", lambda e: e.memset(xg[i][:], 0.0), writes=[xg[i]])
        for ct in range(8):
            b = ct % 2
            r0 = ct * 128
            kb.dma("sp", u[b][:], P["cu"][r0:r0 + 128, :], u[b], writes=[u[b]])
            kb.dma("sp", glu[b][:], P["cglu"][r0:r0 + 128, :], glu[b], writes=[glu[b]])
            G.load_halo(kb, hu[b], hg[b], r0)
            kb.op("act", lambda e: e.activation(out=sig[:], in_=glu[b][:], func=AF.Sigmoid), reads=[glu[b]], writes=[sig])
            kb.op("act", lambda e: e.activation(out=hsig[:], in_=hg[b][:], func=AF.Sigmoid), reads=[hg[b]], writes=[hsig])
            X = xg[b]
            kb.op("dve", lambda e: e.tensor_tensor(out=X[:, XO_CTX:XO_CTX + NCTX], in0=u[b][:, 0:NCTX], in1=sig[:, 0:NCTX], op=ALU.mult),
                  reads=[u[b], sig], writes=[X])
            kb.op("dve", lambda e: e.tensor_tensor(out=X[:, XO_LAT:XO_LAT + NLAT], in0=u[b][:, NCTX:TL], in1=sig[:, NCTX:TL], op=ALU.mult),
                  reads=[u[b], sig], writes=[X])
            kb.op("dve", lambda e: e.tensor_tensor(out=hxg[:], in0=hu[b][:], in1=hsig[:], op=ALU.mult),
                  reads=[hu[b], hsig], writes=[hxg])
            for side in range(2):
                slot = 1 - side
                for r in range(4):
                    src = hxg[:, r, slot * 16:slot * 16 + 16]
                    mcol = hmk[:, side, r:r + 1]
                    if r == 0:
                        kb.op("dve", lambda e: e.tensor_scalar(out=hsel[:, side, :], in0=src, scalar1=mcol, scalar2=None, op0=ALU.mult),
                              reads=[hxg, hmk], writes=[hsel])
                    else:
                        kb.op("dve", lambda e: e.scalar_tensor_tensor(out=hsel[:, side, :], in0=src, scalar=mcol, in1=hsel[:, side, :],
                                                                      op0=ALU.mult, op1=ALU.add),
                              reads=[hxg, hmk, hsel], writes=[hsel])
            kb.op("dve", lambda e: e.tensor_copy(out=X[:, XO_LAT - 15:XO_LAT], in_=hsel[:, 0, 0:15]), reads=[hsel], writes=[X])
            kb.op("dve", lambda e: e.tensor_copy(out=X[:, XO_LAT + NLAT:XO_LAT + NLAT + 15], in_=hsel[:, 1, 0:15]), reads=[hsel], writes=[X])
            for tap in range(31):
                wc = cols[:, 48 + tap * 8 + ct:48 + tap * 8 + ct + 1]
                if tap < NTAP_DVE:
                    en, dst = "dve", cv[:, ct, :]
                    first = (tap == 0)
                    dtk = cv
                else:
                    en, dst = "pool", acc2[:, :]
                    first = (tap == NTAP_DVE)
                    dtk = acc2
                if first:
                    kb.op(en, lambda e: e.tensor_scalar(out=dst, in0=X[:, tap:tap + CVW], scalar1=wc, scalar2=None, op0=ALU.mult),
                          reads=[X], writes=[dtk])
                else:
                    kb.op(en, lambda e: e.scalar_tensor_tensor(out=dst, in0=X[:, tap:tap + CVW], scalar=wc, in1=dst,
                                                              op0=ALU.mult, op1=ALU.add),
                          reads=[X, dtk], writes=[dtk])
            kb.op("dve", lambda e: e.tensor_scalar(out=cv[:, ct, :], in0=cv[:, ct, :], scalar1=cols[:, 24 + ct:25 + ct],
                                                   scalar2=None, op0=ALU.add),
                  reads=[cv], writes=[cv])
            kb.op("act", lambda e: e.copy(out=ybf[:, ct, :], in_=cv[:, ct, :]), reads=[cv], writes=[ybf])
            kb.op("act", lambda e: e.activation(out=ysq[:, ct, :], in_=cv[:, ct, :], func=AF.Square), reads=[cv], writes=[ysq])
        pm = st.ptile("cpm", [128, 512])
        pq = st.ptile("cpq", [128, 512])
        mean = st.tile("cmean", [128, CVW], F32)
        rstd = st.tile("crstd", [128, CVW], F32)
        var = st.tile("cvar", [128, 512], F32)
        msq = st.tile("cmsq", [128, 512], F32)
        for (c0, cn) in CVBLK:
            for ct in range(8):
                kb.op("pe", lambda e: e.matmul(pm[:, 0:cn], lhsT=ones_bf[:, :], rhs=ybf[:, ct, c0:c0 + cn], start=(ct == 0), stop=(ct == 7)),
                      reads=[ones_bf, ybf], writes=[pm], inc=(ct == 7), acc=(ct > 0))
            for ct in range(8):
                kb.op("pe", lambda e: e.matmul(pq[:, 0:cn], lhsT=ones_bf[:, :], rhs=ysq[:, ct, c0:c0 + cn], start=(ct == 0), stop=(ct == 7)),
                      reads=[ones_bf, ysq], writes=[pq], inc=(ct == 7), acc=(ct > 0))
            kb.op("act", lambda e: e.activation(out=mean[:, c0:c0 + cn], in_=pm[:, 0:cn], func=AF.Copy, scale=1.0 / 1024),
                  reads=[pm], writes=[mean])
            kb.op("dve", lambda e: e.tensor_tensor(out=msq[:, 0:cn], in0=mean[:, c0:c0 + cn], in1=mean[:, c0:c0 + cn], op=ALU.mult),
                  reads=[mean], writes=[msq])
            kb.op("dve", lambda e: e.scalar_tensor_tensor(out=var[:, 0:cn], in0=pq[:, 0:cn], scalar=1.0 / 1024, in1=msq[:, 0:cn],
                                                          op0=ALU.mult, op1=ALU.subtract),
                  reads=[pq, msq], writes=[var])
            kb.op("act", lambda e: e.activation(out=var[:, 0:cn], in_=var[:, 0:cn], func=AF.Sqrt, bias=EPS, scale=1.0),
                  reads=[var], writes=[var])
            kb.op("dve", lambda e: e.reciprocal(out=rstd[:, c0:c0 + cn], in_=var[:, 0:cn]), reads=[var], writes=[rstd])
        gt = [st.tile("cgt%d" % i, [128, TL], BF16) for i in range(2)]
        sgt = [st.tile("csg%d" % i, [128, TL], F32) for i in range(2)]
        z = [st.tile("cz%d" % i, [128, CVW], F32) for i in range(2)]
        yo = [st.tile("cyo%d" % i, [128, TL], BF16) for i in range(2)]
        t0o = 0 if ctx_out else NCTX
        for ct in range(8):
            b = ct % 2
            r0 = ct * 128
            kb.dma("sp", gt[b][:], P["cg"][r0:r0 + 128, :], gt[b], writes=[gt[b]])
            kb.op("act", lambda e: e.activation(out=sgt[b][:], in_=gt[b][:], func=AF.Silu), reads=[gt[b]], writes=[sgt[b]])
            kb.op("dve", lambda e: e.tensor_tensor(out=z[b][:], in0=cv[:, ct, :], in1=mean[:], op=ALU.subtract),
                  reads=[cv, mean], writes=[z[b]])
            kb.op("pool", lambda e: e.tensor_tensor(out=z[b][:], in0=z[b][:], in1=rstd[:], op=ALU.mult),
                  reads=[z[b], rstd], writes=[z[b]])
            kb.op("act", lambda e: e.activation(out=z[b][:], in_=z[b][:], func=AF.Silu, scale=cols[:, 32 + ct:33 + ct],
                                                bias=cols[:, 40 + ct:41 + ct]),
                  reads=[z[b]], writes=[z[b]])
            kb.op("dve", lambda e: e.tensor_tensor(out=yo[b][:, 0:NCTX], in0=z[b][:, 0:NCTX], in1=sgt[b][:, 0:NCTX], op=ALU.mult),
                  reads=[z[b], sgt[b]], writes=[yo[b]])
            kb.op("pool", lambda e: e.tensor_tensor(out=yo[b][:, NCTX:TL], in0=z[b][:, XO_LAT - 15:XO_LAT - 15 + NLAT],
                                                    in1=sgt[b][:, NCTX:TL], op=ALU.mult),
                  reads=[z[b], sgt[b]], writes=[yo[b]])
            kb.dma("sp", Y[2048 + r0:2048 + r0 + 128, t0o:TL], yo[b][:, t0o:TL], yo[b], reads=[yo[b]])
        kb.barrier()


def stage_outproj(kb, io, l, Y, gate_src, x_src, ctx_src, x_dst, ctx_dst, ctx_out):
    with Stage(kb) as st:
        yT = st.tile("yT", [128, KC, TL], BF16)
        gbc = [st.tile("gbc%d" % t, [128, D], F32) for t in range(2)]
        wo = [st.tile("wo%d" % i, [128, KC, 256], BF16) for i in range(2)]
        xt = [st.tile("oxt%d" % i, [128, 256], F32) for i in range(3)]
        ot = [st.tile("oot%d" % i, [128, 256], F32) for i in range(3)]
        pacc = [st.ptile("opa%d" % i, [128, 256]) for i in range(4)]
        t0o = 0 if ctx_out else NCTX
        ysrc = Y.rearrange("(c p) t -> p c t", p=128)
        for q in range(4):
            kb.dma("sp", yT[:, q * 8:(q + 1) * 8, t0o:TL], ysrc[:, q * 8:(q + 1) * 8, t0o:TL], yT, writes=[yT])
        kb.dma("sp", gbc[0][:], gate_src[0], gbc[0], writes=[gbc[0]])
        if ctx_out:
            kb.dma("sp", gbc[1][:], gate_src[1], gbc[1], writes=[gbc[1]])
        pi = 0
        xi = 0
        for cb in range(D // 256):
            c0 = cb * 256
            w = wo[cb % 2]
            wsrc = io["w_out"][l - WL[0], :, c0:c0 + 256].rearrange("(kc p) c -> p kc c", p=128)
            for kq in range(4):
                kb.dma("pool", w[:, kq * 8:(kq + 1) * 8, :], wsrc[:, kq * 8:(kq + 1) * 8, :], w, writes=[w])
            for i in range(NT):
                if i < 2 and not ctx_out:
                    continue
                t = 1 if i < 2 else 0
                srcx = ctx_src[i * 128:(i + 1) * 128, c0:c0 + 256] if i < 2 else x_src[(i - 2) * 128:(i - 1) * 128, c0:c0 + 256]
                dstx = ctx_dst[i * 128:(i + 1) * 128, c0:c0 + 256] if i < 2 else x_dst[(i - 2) * 128:(i - 1) * 128, c0:c0 + 256]
                pa = pacc[pi % 4]
                pi += 1
                xx, oo = xt[xi % 3], ot[xi % 3]
                xi += 1
                kb.dma("sp", xx[:], srcx, xx, writes=[xx])
                for kc in range(KC):
                    kb.op("pe", lambda e: e.matmul(pa[:, :], lhsT=yT[:, kc, i * 128:(i + 1) * 128], rhs=w[:, kc, :],
                                                   start=(kc == 0), stop=(kc == KC - 1)),
                          reads=[yT, w], writes=[pa], inc=(kc == KC - 1), acc=(kc > 0))
                kb.op("dve", lambda e: e.tensor_tensor(out=oo[:], in0=pa[:], in1=gbc[t][:, c0:c0 + 256], op=ALU.mult),
                      reads=[pa, gbc[t]], writes=[oo])
                kb.op("pool", lambda e: e.tensor_tensor(out=oo[:], in0=oo[:], in1=xx[:], op=ALU.add),
                      reads=[oo, xx], writes=[oo])
                kb.dma("sp", dstx, oo[:], oo, reads=[oo])
        kb.barrier()


def hgrn_consts():
    s = np.arange(64)[:, None]
    t = np.arange(64)[None, :]
    out = {}
    M1 = np.zeros((2, 128, 128), np.float32)
    SC = np.zeros((2, 128, 128), np.float32)
    MK = np.zeros((2, 128, 128), np.float32)
    MC = np.zeros((2, 128, 4), np.float32)
    for d in range(2):
        if d == 0:
            m1 = (s <= t).astype(np.float32) - (s <= 31).astype(np.float32)
            sc = (s > t).astype(np.float32)
            mk = (s <= t).astype(np.float32)
            mid = (np.arange(64) <= 31).astype(np.float32)
        else:
            m1 = (s >= t).astype(np.float32) - (s >= 32).astype(np.float32)
            sc = (s < t).astype(np.float32)
            mk = (s >= t).astype(np.float32)
            mid = (np.arange(64) >= 32).astype(np.float32)
        for hf in range(2):
            sl = slice(hf * 64, hf * 64 + 64)
            M1[d, sl, sl] = m1
            SC[d, sl, sl] = sc
            MK[d, sl, sl] = mk
            MC[d, sl, hf] = mid
            MC[d, sl, 2 + hf] = 1.0
    out["hM1"] = M1
    out["hM1n"] = -M1
    out["hSC"] = SC
    out["hMK"] = MK
    out["hMC"] = MC
    out["identf"] = np.eye(128, dtype=np.float32)
    return out


def stage_hgrn(kb, io, l, P, G, Y, XS, XD, pass_id, ctx_out, dbg=None):
    full = (pass_id == 2)
    with Stage(kb) as st:
        identf = load_const(kb, st, io, "identf", [128, 128], F32)
        cols = load_const(kb, st, io, "cols", [128, NCOLS], F32, src=io["cols"][l])
        PA = st.ptile("hPA", [128, 1024])
        PB = st.ptile("hPB", [128, 1024])
        PC = st.ptile("hPC", [128, 1024])
        PD = st.ptile("hPD", [128, 512])
        oml = st.tile("oml", [128, 2, 1024], F32)
        if l == 0:
            kb.op("dve", lambda e: e.memset(oml[:], 1.0), writes=[oml])
        else:
            with Stage(kb) as s0:
                lg = s0.tile("lg", [1, 2, 2, 1024], F32)
                df = s0.tile("lgd", [1, 2, 1024], F32)
                ones_row = s0.tile("onesr", [1, 128], F32)
                kb.dma("sp", lg[:], io["hgrn_lb_logits"].rearrange("(o a) b c -> o a b c", o=1), lg, writes=[lg])
                kb.op("dve", lambda e: e.memset(ones_row[:], 1.0), writes=[ones_row])
                kb.op("dve", lambda e: e.tensor_tensor(out=df[:], in0=lg[:, 0, :, :], in1=lg[:, 1, :, :], op=ALU.subtract),
                      reads=[lg], writes=[df])
                kb.op("act", lambda e: e.activation(out=df[:], in_=df[:], func=AF.Sigmoid), reads=[df], writes=[df])
                for d in range(2):
                    for n in range(2):
                        kb.op("pe", lambda e: e.matmul(PA[:, n * 512:(n + 1) * 512], lhsT=ones_row[0:1, :],
                                                       rhs=df[0:1, d, n * 512:(n + 1) * 512], start=True, stop=True),
                              reads=[ones_row, df], writes=[PA], acc=(n > 0))
                    kb.op("act", lambda e: e.copy(out=oml[:, d, :], in_=PA[:, :]), reads=[PA], writes=[oml])
                kb.barrier()
        S = st.tile("hS", [128, 8, 128], F32)
        Sbf = st.tile("hSbf", [128, 8, 128], BF16)
        stmp = st.tile("hstmp", [128, 8, 128], F32)
        Dcum = st.tile("hDcum", [128, 8], F32)
        ft = [st.tile("hf%d" % i, [128, 1024], F32) for i in range(2)]
        kt = st.tile("hk", [128, 1024], F32)
        lf = [st.tile("hlf%d" % i, [128, 1024], F32) for i in range(2)]
        logk = [st.tile("hlogk%d" % i, [128, 1024], F32) for i in range(2)]
        Kd = [st.tile("hKd%d" % i, [128, 1024], BF16) for i in range(2)]
        vt = [st.tile("hv%d" % i, [128, 1024], BF16) for i in range(2)]
        gc = [st.tile("hgc%d" % i, [128, 8, 4], F32) for i in range(2)]
        if full:
            qs = st.tile("hqs", [128, 8, TL], BF16)
            oacc = st.tile("hoacc", [128, 8, TL], F32)
            eA = st.tile("heA", [128, 8, 128], F32)
            AT = [st.tile("hAT%d" % i, [128, 8, 128], BF16) for i in range(2)]
            BT = [st.tile("hBT%d" % i, [128, 8, 2, 128], BF16) for i in range(2)]
            for i in range(2):
                kb.op("pool", lambda e: e.memset(BT[i][:], 0.0), writes=[BT[i]])
            scm = [st.tile("hscm%d" % i, [128, 8, 128], BF16) for i in range(2)]
            kb.dma("sp", qs[:], P["bq"].rearrange("(h p) t -> p h t", p=128), qs, writes=[qs])
            kb.op("act", lambda e: e.activation(out=qs[:], in_=qs[:], func=AF.Silu), reads=[qs], writes=[qs])
            kb.op("dve", lambda e: e.tensor_scalar(out=qs[:], in0=qs[:], scalar1=float(128 ** -0.5), scalar2=None, op0=ALU.mult),
                  reads=[qs], writes=[qs])
        ti = 0
        for d in range(2):
            M1 = load_const(kb, st, io, "hM1_%d" % d, [128, 128], F32, src=io["hM1"][d])
            M1n = load_const(kb, st, io, "hM1n_%d" % d, [128, 128], F32, src=io["hM1n"][d])
            SC = load_const(kb, st, io, "hSC_%d" % d, [128, 128], F32, src=io["hSC"][d])
            MK = load_const(kb, st, io, "hMK_%d" % d, [128, 128], F32, src=io["hMK"][d])
            MC = load_const(kb, st, io, "hMC_%d" % d, [128, 4], F32, src=io["hMC"][d])
            if full:
                MK8 = st.tile("hMK8_%d" % d, [128, 8, 128], F32)
                kb.op("dve", lambda e: e.tensor_copy(out=MK8[:], in_=MK[:, :].unsqueeze(1).broadcast_to([128, 8, 128])),
                      reads=[MK], writes=[MK8])
                for i in range(2):
                    kb.op("pool", lambda e: e.memset(scm[i][:], 0.0), writes=[scm[i]])
            fsrc = P["bff"] if d == 0 else P["bfb"]
            tiles = ([0, 1] + list(range(2, NT))) if d == 0 else ([1, 0] + list(range(NT - 1, 1, -1)))
            kb.op("pool", lambda e: e.memset(S[:], 0.0), writes=[S])
            for n_, i in enumerate(tiles):
                b = ti % 2
                ti += 1
                if n_ == 2:
                    if not full:
                        kb.op("pool", lambda e: e.memset(S[:], 0.0), writes=[S])
                        kb.op("pool", lambda e: e.memset(Dcum[:], 1.0), writes=[Dcum])
                    else:
                        with Stage(kb) as s1:
                            hm = s1.tile("hm", [128, 2, 4], F32)
                            kb.dma("sp", hm[:], io["hmask"], hm, writes=[hm])
                            Si = [s1.tile("hSi%d" % q, [128, 8, 128], F32) for q in range(2)]
                            Di = [s1.tile("hDi%d" % q, [128, 8], F32) for q in range(2)]
                            order = [0, 1, 2, 3] if d == 0 else [3, 2, 1, 0]
                            for q_, ci in enumerate(order):
                                sb_, db_ = Si[q_ % 2], Di[q_ % 2]
                                kb.dma("sp", sb_[:], G.hS(ci, d), sb_, writes=[sb_])
                                kb.dma("sp", db_[:], G.hD(ci, d), db_, writes=[db_])
                                mcol = hm[:, d, ci:ci + 1]
                                kb.op("dve", lambda e: e.tensor_scalar(out=db_[:], in0=db_[:], scalar1=-1.0, scalar2=mcol,
                                                                       op0=ALU.add, op1=ALU.mult),
                                      reads=[db_, hm], writes=[db_])
                                kb.op("dve", lambda e: e.tensor_scalar(out=db_[:], in0=db_[:], scalar1=1.0, scalar2=None, op0=ALU.add),
                                      reads=[db_], writes=[db_])
                                kb.op("dve", lambda e: e.tensor_tensor(out=stmp[:], in0=S[:],
                                                                       in1=db_[:].unsqueeze(2).broadcast_to([128, 8, 128]), op=ALU.mult),
                                      reads=[S, db_], writes=[stmp])
                                kb.op("dve", lambda e: e.scalar_tensor_tensor(out=S[:], in0=sb_[:], scalar=mcol, in1=stmp[:],
                                                                              op0=ALU.mult, op1=ALU.add),
                                      reads=[sb_, stmp, hm], writes=[S])
                            kb.barrier()
                F_ = ft[b]
                kb.dma("sp", F_[:], fsrc[i * 128:(i + 1) * 128, :], F_, writes=[F_])
                kb.dma("sp", vt[b][:], P["bi"][i * 128:(i + 1) * 128, :], vt[b], writes=[vt[b]])
                kb.op("act", lambda e: e.activation(out=kt[:], in_=F_[:], func=AF.Sigmoid, scale=-1.0), reads=[F_], writes=[kt])
                kb.op("dve", lambda e: e.tensor_tensor(out=kt[:], in0=kt[:], in1=oml[:, d, :], op=ALU.mult), reads=[kt, oml], writes=[kt])
                kb.op("act", lambda e: e.activation(out=logk[b][:], in_=kt[:], func=AF.Ln), reads=[kt], writes=[logk[b]])
                kb.op("act", lambda e: e.activation(out=lf[b][:], in_=kt[:], func=AF.Ln, scale=-1.0, bias=1.0), reads=[kt], writes=[lf[b]])
                for n in range(2):
                    kb.op("pe", lambda e: e.matmul(PA[:, n * 512:(n + 1) * 512], lhsT=SC[:, :], rhs=lf[b][:, n * 512:(n + 1) * 512],
                                                   start=True, stop=False),
                          reads=[SC, lf[b]], writes=[PA], inc=False, acc=(n > 0))
                    kb.op("pe", lambda e: e.matmul(PA[:, n * 512:(n + 1) * 512], lhsT=identf[:, :], rhs=logk[b][:, n * 512:(n + 1) * 512],
                                                   start=False, stop=True),
                          reads=[identf, logk[b]], writes=[PA], inc=(n == 1), acc=True)
                kb.op("act", lambda e: e.activation(out=Kd[b][:], in_=PA[:, :], func=AF.Exp), reads=[PA], writes=[Kd[b]])
                for h in range(8):
                    kb.op("pe", lambda e: e.matmul(PD[:, h * 4:(h + 1) * 4], lhsT=lf[b][:, h * 128:(h + 1) * 128], rhs=MC[:, :],
                                                   start=True, stop=True),
                          reads=[MC, lf[b]], writes=[PD], inc=(h == 7), acc=(h > 0))
                kb.op("act", lambda e: e.activation(out=gc[b][:], in_=PD[:, 0:32].rearrange("p (h c) -> p h c", c=4), func=AF.Exp),
                      reads=[PD], writes=[gc[b]])
                if full:
                    for h in range(8):
                        kb.op("pe", lambda e: e.matmul(PB[:, h * 128:(h + 1) * 128], lhsT=lf[b][:, h * 128:(h + 1) * 128], rhs=M1[:, :],
                                                       start=True, stop=True),
                              reads=[M1, lf[b]], writes=[PB], inc=(h == 7), acc=(h > 0))
                    for h in range(8):
                        kb.op("pe", lambda e: e.matmul(PC[:, h * 128:(h + 1) * 128], lhsT=lf[b][:, h * 128:(h + 1) * 128], rhs=M1n[:, :],
                                                       start=True, stop=False),
                              reads=[M1n, lf[b]], writes=[PC], inc=False, acc=(h > 0))
                        kb.op("pe", lambda e: e.matmul(PC[:, h * 128:(h + 1) * 128], lhsT=logk[b][:, h * 128:(h + 1) * 128], rhs=identf[:, :],
                                                       start=False, stop=True),
                              reads=[identf, logk[b]], writes=[PC], inc=(h == 7), acc=True)
                    kb.op("act", lambda e: e.activation(out=eA[:], in_=PB[:, :].rearrange("p (h t) -> p h t", t=128), func=AF.Exp),
                          reads=[PB], writes=[eA])
                    kb.op("dve", lambda e: e.tensor_tensor(out=AT[b][:], in0=eA[:], in1=qs[:, :, i * 128:(i + 1) * 128], op=ALU.mult),
                          reads=[eA, qs], writes=[AT[b]])
                    pcv = PC[:, :].rearrange("p (h t) -> p h t", t=128)
                    for c in range(2):
                        kb.op("act", lambda e: e.activation(out=BT[b][:, :, c, c * 64:(c + 1) * 64], in_=pcv[:, :, c * 64:(c + 1) * 64],
                                                            func=AF.Exp),
                              reads=[PC], writes=[BT[b]])
                    for h in range(8):
                        for c in range(2):
                            kb.op("pe", lambda e: e.matmul(PA[:, h * 128 + c * 64:h * 128 + (c + 1) * 64], lhsT=BT[b][:, h, c, :],
                                                           rhs=AT[b][:, h, c * 64:(c + 1) * 64], start=True, stop=True),
                                  reads=[BT[b], AT[b]], writes=[PA], inc=(h == 7 and c == 1), acc=(h > 0 or c > 0))
                    kb.op("dve", lambda e: e.copy_predicated(out=scm[b][:], mask=MK8[:].bitcast(mybir.dt.uint32),
                                                             data=PA[:, :].rearrange("p (h t) -> p h t", t=128)),
                          reads=[PA, MK8, scm[b]], writes=[scm[b]])
                    for h in range(8):
                        kb.op("pe", lambda e: e.matmul(PB[:, h * 128:(h + 1) * 128], lhsT=vt[b][:, h * 128:(h + 1) * 128], rhs=scm[b][:, h, :],
                                                       start=(h % 4 == 0), stop=False, skip_group_check=True),
                              reads=[vt[b], scm[b]], writes=[PB], inc=(h == 7), acc=(h > 0))
                for hf in ((0, 1) if d == 0 else (1, 0)):
                    p0 = hf * 64
                    if full:
                        kb.op("dve", lambda e: e.tensor_tensor(out=Sbf[:], in0=S[:],
                                                               in1=gc[b][:, :, hf:hf + 1].broadcast_to([128, 8, 128]), op=ALU.mult),
                              reads=[S, gc[b]], writes=[Sbf])
                        for h in range(8):
                            kb.op("pe", lambda e: e.matmul(PB[:, h * 128 + p0:h * 128 + p0 + 64], lhsT=Sbf[:, h, :],
                                                           rhs=AT[b][:, h, p0:p0 + 64], start=False, stop=True, skip_group_check=True),
                                  reads=[Sbf, AT[b]], writes=[PB], inc=(h == 7), acc=True)
                    for h in range(8):
                        kb.op("pe", lambda e: e.matmul(PC[:, h * 128:(h + 1) * 128], lhsT=Kd[b][p0:p0 + 64, h * 128:(h + 1) * 128],
                                                       rhs=vt[b][p0:p0 + 64, h * 128:(h + 1) * 128], start=True, stop=True),
                              reads=[Kd[b], vt[b]], writes=[PC], inc=(h == 7), acc=(h > 0))
                    kb.op("dve", lambda e: e.tensor_tensor(out=stmp[:], in0=S[:],
                                                           in1=gc[b][:, :, 2 + hf:3 + hf].broadcast_to([128, 8, 128]), op=ALU.mult),
                          reads=[S, gc[b]], writes=[stmp])
                    kb.op("dve", lambda e: e.tensor_tensor(out=S[:], in0=stmp[:], in1=PC[:, :].rearrange("p (h t) -> p h t", t=128), op=ALU.add),
                          reads=[stmp, PC], writes=[S])
                    if not full and n_ >= 2:
                        kb.op("pool", lambda e: e.tensor_tensor(out=Dcum[:], in0=Dcum[:], in1=gc[b][:, :, 2 + hf], op=ALU.mult),
                              reads=[Dcum, gc[b]], writes=[Dcum])
                if dbg is not None and d == 0 and n_ == 0:
                    for nm, tt in (("lf", lf[b]), ("logk", logk[b]), ("Kd", Kd[b]), ("gc", gc[b]), ("S", S), ("kt", kt)) + \
                            ((("AT", AT[b]), ("scm", scm[b]), ("eA", eA)) if full else ()):
                        kb.dma("sp", dbg[nm], tt[:], tt, reads=[tt])
                if full:
                    pov = PB[:, :].rearrange("p (h t) -> p h t", t=128)
                    if d == 0:
                        kb.op("act", lambda e: e.copy(out=oacc[:, :, i * 128:(i + 1) * 128], in_=pov), reads=[PB], writes=[oacc])
                    else:
                        kb.op("dve", lambda e: e.tensor_tensor(out=oacc[:, :, i * 128:(i + 1) * 128], in0=oacc[:, :, i * 128:(i + 1) * 128],
                                                               in1=pov, op=ALU.add),
                              reads=[PB, oacc], writes=[oacc])
            if dbg is not None and full:
                kb.dma("sp", dbg["oacc%d" % d], oacc[:], oacc, reads=[oacc])
                kb.dma("sp", dbg["Send%d" % d], S[:], S, reads=[S])
            if not full:
                kb.dma("sp", XS[d], S[:], S, reads=[S])
                kb.dma("sp", XD[d], Dcum[:], Dcum, reads=[Dcum])
        if full:
            ones_bf = load_const(kb, st, io, "ones_bf", [128, 128], BF16)
            sq = st.tile("hsq", [128, 512], BF16)
            sd = st.tile("hsd", [128, 512], F32)
            rstd = st.tile("hrstd", [128, 512], F32)
            gt = [st.tile("hgt%d" % q, [128, TL], BF16) for q in range(2)]
            sgt = [st.tile("hsgt%d" % q, [128, TL], F32) for q in range(2)]
            yo = [st.tile("hyo%d" % q, [128, TL], BF16) for q in range(2)]
            tmp = st.tile("htmp", [128, 512], F32)
            t0o = 0 if ctx_out else NCTX
            for h in range(8):
                b = h % 2
                kb.dma("sp", gt[b][:], P["bg"][h * 128:(h + 1) * 128, :], gt[b], writes=[gt[b]])
                kb.op("act", lambda e: e.activation(out=sgt[b][:], in_=gt[b][:], func=AF.Silu), reads=[gt[b]], writes=[sgt[b]])
                for (t0, tn) in TOKBLK:
                    kb.op("act", lambda e: e.activation(out=sq[:, 0:tn], in_=oacc[:, h, t0:t0 + tn], func=AF.Square), reads=[oacc], writes=[sq])
                    kb.op("pe", lambda e: e.matmul(PA[:, 0:tn], lhsT=ones_bf[:, :], rhs=sq[:, 0:tn], start=True, stop=True),
                          reads=[ones_bf, sq], writes=[PA])
                    rstd_from_psum(kb, PA, sd, rstd, tn, 1.0 / 128)
                    kb.op("dve", lambda e: e.scalar_tensor_tensor(out=tmp[:, 0:tn], in0=oacc[:, h, t0:t0 + tn], scalar=cols[:, 16 + h:17 + h],
                                                                  in1=rstd[:, 0:tn], op0=ALU.mult, op1=ALU.mult),
                          reads=[oacc, rstd], writes=[tmp])
                    kb.op("pool", lambda e: e.tensor_tensor(out=yo[b][:, t0:t0 + tn], in0=tmp[:, 0:tn], in1=sgt[b][:, t0:t0 + tn], op=ALU.mult),
                          reads=[tmp, sgt[b]], writes=[yo[b]])
                kb.dma("sp", Y[1024 + h * 128:1024 + (h + 1) * 128, t0o:TL], yo[b][:, t0o:TL], yo[b], reads=[yo[b]])
        kb.barrier()


P_SHAPES = {name: ([w, TL] if lay == "F" else [TL, w], dt) for name, w, lay, dt in GROUPS}
XCH = {"K": (256, BF16), "C": (512, BF16), "V": (384, BF16), "S0": (128, F32), "S1": (130, F32)}
RV_V, RV_R, RV_HU, RV_HG = 0, 256, 320, 352
GROUPS4 = [[0, 1, 2, 3], [4, 5, 6, 7]]


class GFused:
    def __init__(self, P, X, GT):
        self.P, self.X, self.GT = P, X, GT

    def rk(self, name, r, r0, n):
        rows = XCH[name][0]
        return self.GT[name][r * rows + r0:r * rows + r0 + n, :]

    def load_kT(self, kb, kT, g):
        kb.dma("sp", kT[:, 0:NCTX], self.X["XK"][g][:, 0:NCTX], kT, writes=[kT])
        for r in range(4):
            kb.dma("sp", kT[:, NCTX + r * NLAT:NCTX + (r + 1) * NLAT], self.rk("K", r, g * 128, 128), kT, writes=[kT])

    def load_V(self, kb, V, g):
        kb.dma("sp", V[:, 0:2, :], self.P["av"][0:NCTX, g * 128:(g + 1) * 128].rearrange("(t p) c -> p t c", p=128), V, writes=[V])
        for r in range(4):
            src = self.rk("V", r, RV_V, 256).rearrange("a (t c) -> (a t) c", c=256)
            kb.dma("sp", V[:, 2 + r * 8:2 + (r + 1) * 8, :], src[:, g * 128:(g + 1) * 128].rearrange("(t p) c -> p t c", p=128),
                   V, writes=[V])

    def load_ckv(self, kb, ckv):
        kb.dma("sp", ckv[:, :, 0:NCTX], self.X["XC"][:, 0:NCTX].rearrange("(c p) t -> p c t", p=128), ckv, writes=[ckv])
        for r in range(4):
            kb.dma("sp", ckv[:, :, NCTX + r * NLAT:NCTX + (r + 1) * NLAT],
                   self.rk("C", r, 0, 512).rearrange("(c p) t -> p c t", p=128), ckv, writes=[ckv])

    def load_kr(self, kb, kr):
        kb.dma("sp", kr[:, 0:NCTX], self.P["dkr"][:, 0:NCTX], kr, writes=[kr])
        for r in range(4):
            kb.dma("sp", kr[:, NCTX + r * NLAT:NCTX + (r + 1) * NLAT], self.rk("V", r, RV_R, 64), kr, writes=[kr])

    def load_halo(self, kb, hu, hg, r0):
        for r in range(4):
            for (t, ro) in ((hu, RV_HU), (hg, RV_HG)):
                src = self.rk("V", r, ro, 32).rearrange("a (c k) -> (a c) k", k=32)
                kb.dma("sp", t[:, r, :], src[r0:r0 + 128, :], t, writes=[t])

    def hS(self, ci, d):
        src = self.rk("S0", ci, 0, 128) if d == 0 else self.rk("S1", ci, 0, 128)
        return src.rearrange("p (h e) -> p h e", e=128)

    def hD(self, ci, d):
        return self.rk("S1", ci, 128 + d, 1).rearrange("o (p h) -> (o p) h", h=8)


def stage_exchange(kb, P, X, ST, GT):
    t = kb.trk("pack")
    for g in range(2):
        kb.dma("sp", ST["K"][g * 128:(g + 1) * 128, :], X["XK"][g][:, NCTX:TL], t)
    kb.dma("sp", ST["C"][:, :], X["XC"][:, NCTX:TL], t)
    kb.dma("sp", ST["V"][RV_R:RV_R + 64, :], P["dkr"][:, NCTX:TL], t)
    kb.dma("sp", ST["V"][RV_V:RV_V + 256, :].rearrange("a (t c) -> (a t) c", c=256), P["av"][NCTX:TL, :], t)
    for (nm, ro) in (("cu", RV_HU), ("cglu", RV_HG)):
        dst = ST["V"][ro:ro + 32, :].rearrange("a (c k) -> (a c) k", k=32)
        kb.dma("sp", dst[:, 0:15], P[nm][:, NCTX:NCTX + 15], t)
        kb.dma("sp", dst[:, 16:31], P[nm][:, TL - 15:TL], t)
    kb.dma("sp", ST["S0"][:, :].rearrange("p (h e) -> p h e", e=128), X["XS"][0], t)
    kb.dma("sp", ST["S1"][0:128, :].rearrange("p (h e) -> p h e", e=128), X["XS"][1], t)
    for d in range(2):
        kb.dma("sp", ST["S1"][128 + d:129 + d, :].rearrange("o (p h) -> (o p) h", h=8), X["XD"][d], t)
    kb.barrier()
    for n in XCH:
        kb.collective("AllGather", [ST[n]], [GT[n]], GROUPS4)
    kb.barrier()


CONST_SPECS = {
    "ident": ([128, 128], BF16), "ones_bf": ([128, 128], BF16), "rmA": ([128, 128], BF16), "rmD": ([64, 64], BF16),
    "cosA": ([128, TL], F32), "sinA": ([128, TL], F32), "cosDq": ([64, TL], F32), "sinDq": ([64, TL], F32),
    "cosDk": ([64, NKEY], F32), "sinDk": ([64, NKEY], F32), "cols": ([2, 128, NCOLS], F32),
    "hM1": ([2, 128, 128], F32), "hM1n": ([2, 128, 128], F32), "hSC": ([2, 128, 128], F32), "hMK": ([2, 128, 128], F32),
    "hMC": ([2, 128, 4], F32), "identf": ([128, 128], F32), "hmask": ([128, 2, 4], F32), "halomask": ([128, 2, 4], F32),
}
X_SPECS = {"XK": ([2, 128, TL], BF16), "XC": ([512, TL], BF16), "XS": ([2, 128, 8, 128], F32), "XD": ([2, 128, 8], F32),
           "MOD": ([2, 128, D], F32)}
W_SPECS = {"w_mod": [2, D, 3 * D], "b_mod": [2, 3 * D], "w_in": [2, D, IN_W], "w_out": [2, D, D],
           "mla_w_uq": [2, 768, 1536], "mla_w_ukv": [2, 512, 2048], "hgrn_lb_logits": [2, 2, 1024],
           "normwT": [2, 128, KC], "cT": [128, KC, 2]}


def build_fused(n_layers=2):
    WL[0] = 0
    nc = bass.Bass("TRN2", target_bir_lowering=False)
    io = {}

    def inp(name, shape, dt=F32):
        io[name] = nc.dram_tensor(name, list(shape), dt, kind="ExternalInput").ap()

    def scratch(name, shape, dt):
        return nc.dram_tensor(name, list(shape), dt, kind="Internal").ap()

    inp("x_loc", [NLAT, D])
    inp("ctx_b", [NCTX, D])
    for n, sh in W_SPECS.items():
        inp(n, sh)
    for n, (sh, dt) in CONST_SPECS.items():
        inp(n, sh, dt)
    out = nc.dram_tensor("out", [NLAT, D], F32, kind="ExternalOutput").ap()
    P = {n: scratch("P_" + n, sh, dt) for n, (sh, dt) in P_SHAPES.items()}
    X = {n: scratch(n, sh, dt) for n, (sh, dt) in X_SPECS.items()}
    Y = scratch("Y", [4096, TL], BF16)
    x1 = scratch("x1", [NLAT, D], F32)
    ctx1 = scratch("ctx1", [NCTX, D], F32)
    ST = {n: scratch("ST_" + n, [r, 1024], dt) for n, (r, dt) in XCH.items()}
    GT = {n: scratch("GT_" + n, [4 * r, 1024], dt) for n, (r, dt) in XCH.items()}
    G = GFused(P, X, GT)
    kb = KB(nc)
    for l in range(n_layers):
        ctx_out = (l == 0)
        x_src = io["x_loc"] if l == 0 else x1
        ctx_src = io["ctx_b"] if l == 0 else ctx1
        last = (l == n_layers - 1)
        with Stage(kb) as keep:
            modcol, gate_bc = stage_mod(kb, io, l, keep)
            t = kb.trk()
            kb.dma("sp", X["MOD"][0], gate_bc[0][:], t, reads=[t])
            kb.dma("sp", X["MOD"][1], gate_bc[1][:], t, reads=[t])
            hT = stage_norm(kb, io, l, keep, modcol, x_src, ctx_src)
            stage_inproj(kb, io, l, hT, P)
            kb.barrier()
        stage_x_gqa(kb, io, l, P, X["XK"])
        stage_x_mla(kb, io, l, P, X["XC"])
        stage_hgrn(kb, io, l, P, None, None, X["XS"], X["XD"], 1, True)
        stage_exchange(kb, P, X, ST, GT)
        stage_attn_A(kb, io, l, P, G, Y, ctx_out)
        stage_hgrn(kb, io, l, P, G, Y, None, None, 2, ctx_out)
        stage_conv(kb, io, l, P, G, Y, ctx_out)
        stage_attn_D(kb, io, l, P, G, Y, ctx_out)
        stage_outproj(kb, io, l, Y, X["MOD"], x_src, ctx_src, out if last else x1, ctx1, ctx_out)
    kb.es.close()
    return nc


BF = ml_dtypes.bfloat16
NCORE = 8


def host_consts(inp):
    c = {}
    c["ident"] = np.eye(128, dtype=np.float32).astype(BF)
    c["ones_bf"] = np.ones((128, 128), np.float32).astype(BF)
    c["rmA"] = rope_rot_matrix(128).astype(BF)
    c["rmD"] = rope_rot_matrix(64).astype(BF)
    c["cosDk"], c["sinDk"] = rope_tables(64, key_tpos())
    c["cols"] = np.stack([make_cols(inp, 0), make_cols(inp, 1)])
    c.update(hgrn_consts())
    per_core = []
    for core in range(NCORE):
        j = core % 4
        d = {}
        d["cosA"], d["sinA"] = rope_tables(128, local_tpos(j))
        d["cosDq"], d["sinDq"] = rope_tables(64, local_tpos(j))
        hm = np.zeros((128, 2, 4), np.float32)
        ha = np.zeros((128, 2, 4), np.float32)
        for i in range(4):
            hm[:, 0, i] = 1.0 if i < j else 0.0
            hm[:, 1, i] = 1.0 if i > j else 0.0
            ha[:, 0, i] = 1.0 if i == j - 1 else 0.0
            ha[:, 1, i] = 1.0 if i == j + 1 else 0.0
        d["hmask"] = hm
        d["halomask"] = ha
        per_core.append(d)
    return c, per_core


def make_in_maps(inp):
    consts, pc = host_consts(inp)
    normwT = np.ascontiguousarray(inp["norm_w"].reshape(2, KC, 128).transpose(0, 2, 1))
    maps = []
    for c in range(NCORE):
        b, j = c // 4, c % 4
        m = {"x_loc": np.ascontiguousarray(inp["x"][b, j * NLAT:(j + 1) * NLAT]),
             "ctx_b": np.ascontiguousarray(inp["ctx"][b]),
             "normwT": normwT,
             "cT": np.ascontiguousarray(np.stack([inp["c"][b], inp["c_ctx"]], axis=-1).reshape(KC, 128, 2).transpose(1, 0, 2))}
        for n in ("w_mod", "b_mod", "w_in", "w_out", "mla_w_uq", "mla_w_ukv", "hgrn_lb_logits"):
            m[n] = inp[n]
        for n in CONST_SPECS:
            m[n] = pc[c][n] if n in pc[c] else consts[n]
        maps.append(m)
    return maps


def kernel(**inp):
    inp = {k: np.asarray(v) for k, v in inp.items()}
    nc = build_fused()
    res = run_bass_kernel_spmd(nc, make_in_maps(inp), core_ids=list(range(NCORE))).results
    out = np.zeros((2, SEQ, D), np.float32)
    for c in range(NCORE):
        out[c // 4, (c % 4) * NLAT:(c % 4 + 1) * NLAT] = np.asarray(res[c]["out"])
    return out
```

```python
import contextlib
import os
import numpy as np
import ml_dtypes
import concourse.bass as bass
import concourse.mybir as mybir
from concourse.bass_utils import run_bass_kernel_spmd

F32 = mybir.dt.float32
BF16 = mybir.dt.bfloat16
AF = mybir.ActivationFunctionType
ALU = mybir.AluOpType
AX = mybir.AxisListType

D = 4096
KC = 32
NCTX = 256
NLAT = 1024
TL = NCTX + NLAT
NT = TL // 128
SEQ = 4096
NKEY = NCTX + SEQ
EPS = 1e-6
IN_W = 13120

GROUPS = [
    ("aq", 1024, "F", BF16), ("ak", 256, "F", BF16), ("av", 256, "T", BF16), ("ag", 1024, "F", BF16),
    ("bq", 1024, "F", BF16), ("bi", 1024, "T", BF16), ("bff", 1024, "T", F32), ("bfb", 1024, "T", F32),
    ("bg", 1024, "F", BF16),
    ("cu", 1024, "F", BF16), ("cglu", 1024, "F", BF16), ("cg", 1024, "F", BF16),
    ("dcq", 768, "F", BF16), ("dckv", 512, "F", BF16), ("dkr", 64, "F", BF16), ("dg", 1024, "F", BF16),
]
GOFF = {}
_o = 0
for _n, _w, _l, _d in GROUPS:
    GOFF[_n] = (_o, _w, _l, _d)
    _o += _w
assert _o == IN_W


class Trk:
    __slots__ = ("name", "lw", "rd", "dsem", "psum")

    def __init__(self, name=""):
        self.name = name
        self.psum = False
        self.lw = None
        self.rd = {}
        self.dsem = None


class Eng:
    def __init__(self, name, eng, sem):
        self.name = name
        self.eng = eng
        self.sem = sem
        self.cnt = 0
        self.pending = False
        self.seen = {}


class KB:
    def __init__(self, nc, n_dma_sems=80):
        self.nc = nc
        self.es = contextlib.ExitStack()
        self.sems = {}
        self.engs = {}
        for name, eng in (("pe", nc.tensor), ("act", nc.scalar), ("dve", nc.vector),
                          ("pool", nc.gpsimd), ("sp", nc.sync)):
            h = self.es.enter_context(nc.semaphore("s_" + name))
            self.sems[name] = h
            self.engs[name] = Eng(name, eng, name)
        self.bar_sem = self.es.enter_context(nc.semaphore("s_bar"))
        self.bar_cnt = 0
        self.cc_sem = self.es.enter_context(nc.semaphore("s_cc"))
        self.cc_cnt = 0
        self.dma_free = []
        self.dma_tot = {}
        for i in range(n_dma_sems):
            k = "d%d" % i
            self.sems[k] = self.es.enter_context(nc.semaphore("s_" + k))
            self.dma_free.append(k)
            self.dma_tot[k] = 0
        self.dma_used = []
        self.trks = []

    def trk(self, name=""):
        t = Trk(name)
        self.trks.append(t)
        return t

    def _wait(self, e, semkey, val):
        if e.seen.get(semkey, 0) >= val:
            return
        e.eng.wait_ge(self.sems[semkey], val)
        e.seen[semkey] = val

    def _deps(self, e, reads, writes, acc):
        deps = {}
        reads = [getattr(t, "k", t) for t in reads]
        writes = [getattr(t, "k", t) for t in writes]

        def add(tok):
            if tok is None:
                return
            k, v = tok
            if deps.get(k, 0) < v:
                deps[k] = v

        for t in reads:
            add(t.lw)
            if t.psum:
                for k, v in t.rd.items():
                    if k != e.sem:
                        add((k, v))
        for t in writes:
            if not (acc and t.lw is not None and t.lw[0] == e.sem):
                add(t.lw)
            for k, v in t.rd.items():
                add((k, v))
        for k, v in deps.items():
            if k in self.dma_tot:
                v = self.dma_tot[k]
            self._wait(e, k, v)

    def _commit(self, tok, reads, writes):
        k, v = tok
        reads = [getattr(t, "k", t) for t in reads]
        writes = [getattr(t, "k", t) for t in writes]
        for t in reads:
            if t.rd.get(k, 0) < v:
                t.rd[k] = v
        for t in writes:
            t.lw = tok
            t.rd = {}

    def op(self, en, fn, reads=(), writes=(), inc=True, acc=False):
        e = self.engs[en]
        self._deps(e, reads, writes, acc)
        ins = fn(e.eng)
        if inc:
            e.cnt += 1
            ins.then_inc(self.sems[e.sem], 1)
            e.pending = False
            tok = (e.sem, e.cnt)
        else:
            e.pending = True
            tok = (e.sem, e.cnt + 1)
        self._commit(tok, reads, writes)
        return ins

    def dma(self, q, out, in_, sb, reads=(), writes=(), **kw):
        e = self.engs[q]
        sb = getattr(sb, "k", sb)
        if sb.dsem is None:
            sb.dsem = self.dma_free.pop()
            self.dma_used.append(sb)
        self._deps(e, reads, writes, False)
        ins = e.eng.dma_start(out=out, in_=in_, **kw)
        k = sb.dsem
        self.dma_tot[k] += 16
        ins.then_inc(self.sems[k], 16)
        self._commit((k, self.dma_tot[k]), reads, writes)
        return ins

    def collective(self, kind, ins, outs, groups):
        e = self.engs["pool"]
        i = e.eng.collective_compute(kind, ALU.bypass, replica_groups=groups,
                                     ins=[a.opt() for a in ins], outs=[a.opt() for a in outs])
        self.cc_cnt += 1
        i.then_inc(self.cc_sem)
        return i

    def barrier(self):
        sp = self.engs["sp"]
        if self.cc_cnt > 0:
            sp.eng.wait_ge(self.cc_sem, self.cc_cnt)
        for en, e in self.engs.items():
            assert not e.pending, "engine %s has un-inc'ed trailing instruction" % en
            if en != "sp" and e.cnt > 0:
                self._wait(sp, e.sem, e.cnt)
        for k, v in self.dma_tot.items():
            if v > 0:
                self._wait(sp, k, v)
        self.bar_cnt += 1
        sp.eng.sem_inc(self.bar_sem, 1)
        for en, e in self.engs.items():
            e.eng.wait_ge(self.bar_sem, self.bar_cnt)
            for en2, e2 in self.engs.items():
                e.seen[e2.sem] = e2.cnt
            for k, v in self.dma_tot.items():
                e.seen[k] = v
        for t in self.dma_used:
            self.dma_free.append(t.dsem)
            t.dsem = None
        self.dma_used = []
        for t in self.trks:
            t.lw = None
            t.rd = {}
        self.trks = []


class T:
    __slots__ = ("t", "k")

    def __init__(self, t, k):
        self.t = t
        self.k = k

    def __getitem__(self, key):
        return self.t[key]


class Stage:
    def __init__(self, kb):
        self.kb = kb
        self.es = contextlib.ExitStack()

    def __enter__(self):
        self.es.__enter__()
        return self

    def __exit__(self, *a):
        return self.es.__exit__(*a)

    _uid = [0]

    def sb(self, name, shape, dtype):
        Stage._uid[0] += 1
        t = self.es.enter_context(self.kb.nc.sbuf_tensor("sb%d_%s" % (Stage._uid[0], name), list(shape), dtype))
        return t

    def tile(self, name, shape, dtype):
        return T(self.sb(name, shape, dtype), self.kb.trk(name))

    def ptile(self, name, shape, dtype=F32):
        t = T(self.ps(name, shape, dtype), self.kb.trk(name))
        t.k.psum = True
        return t

    def ps(self, name, shape, dtype=F32):
        Stage._uid[0] += 1
        t = self.es.enter_context(self.kb.nc.psum_tensor("ps%d_%s" % (Stage._uid[0], name), list(shape), dtype))
        return t


def stage_mod(kb, io, l, st_keep):
    nc = kb.nc
    modcol = [st_keep.sb("modcol%d" % t, [128, 64], F32) for t in range(2)]
    gate_bc = [st_keep.sb("gatebc%d" % t, [128, D], F32) for t in range(2)]
    t_modcol = [kb.trk("modcol") for _ in range(2)]
    t_gate = [kb.trk("gatebc") for _ in range(2)]
    with Stage(kb) as st:
        cT = st.sb("cT", [128, KC, 2], F32)
        scT = st.sb("scT", [128, KC, 2], BF16)
        ones_row = st.sb("ones_row", [1, 128], F32)
        nwcol = st.sb("nwcol", [128, KC], F32)
        brow = [st.sb("brow%d" % i, [1, 512], F32) for i in range(2)]
        row = [st.sb("row%d" % i, [1, 512], F32) for i in range(4)]
        wm = [st.sb("wm%d" % i, [128, KC, 512], BF16) for i in range(2)]
        pacc = [st.ps("pacc%d" % i, [1, 512]) for i in range(2)]
        pcol = st.ps("pcol", [128, 512])
        pbc = [st.ps("pbc%d" % i, [128, 512]) for i in range(2)]
        t_cT, t_scT, t_ones, t_nw = kb.trk(), kb.trk(), kb.trk(), kb.trk()
        t_brow = [kb.trk() for _ in range(2)]
        t_row = [kb.trk() for _ in range(4)]
        t_wm = [kb.trk() for _ in range(2)]
        t_pacc = [kb.trk() for _ in range(2)]
        t_pcol = kb.trk()
        t_pbc = [kb.trk() for _ in range(2)]
        for _t in t_pacc + [t_pcol] + t_pbc:
            _t.psum = True

        kb.dma("sp", cT[:], io["cT"], t_cT, writes=[t_cT])
        kb.dma("sp", nwcol[:], io["normwT"][l], t_nw, writes=[t_nw])
        kb.op("act", lambda e: e.activation(out=scT[:], in_=cT[:], func=AF.Silu),
              reads=[t_cT], writes=[t_scT])
        kb.op("dve", lambda e: e.memset(ones_row[:], 1.0), writes=[t_ones])
        ri = 0
        for j in range(24):
            wb, twb = wm[j % 2], t_wm[j % 2]
            wsrc = io["w_mod"][l - WL[0], :, j * 512:(j + 1) * 512].rearrange("(kc p) c -> p kc c", p=128)
            for kq in range(4):
                kb.dma("pool", wb[:, kq * 8:(kq + 1) * 8, :], wsrc[:, kq * 8:(kq + 1) * 8, :], twb, writes=[twb])
            bb, tbb = brow[j % 2], t_brow[j % 2]
            kb.dma("sp", bb[:], io["b_mod"][l - WL[0]:l - WL[0] + 1, j * 512:(j + 1) * 512], tbb, writes=[tbb])
            for t in range(2):
                pa, tpa = pacc[t], t_pacc[t]
                for kc in range(KC):
                    kb.op("pe", lambda e, kc=kc: e.matmul(pa[:], lhsT=scT[:, kc, t:t + 1], rhs=wb[:, kc, :],
                                                         start=(kc == 0), stop=(kc == KC - 1)),
                          reads=[t_scT, twb], writes=[tpa], inc=(kc == KC - 1), acc=(kc > 0))
                rw, trw = row[ri % 4], t_row[ri % 4]
                ri += 1
                kb.op("dve", lambda e: e.tensor_tensor(out=rw[:], in0=pa[:], in1=bb[:], op=ALU.add),
                      reads=[tpa, tbb], writes=[trw])
                if j < 16:
                    for s in range(4):
                        kb.op("pe", lambda e, s=s: e.matmul(pcol[:, s:s + 1], lhsT=rw[0:1, s * 128:(s + 1) * 128],
                                                           rhs=ones_row[0:1, 0:1], start=True, stop=True),
                              reads=[trw, t_ones], writes=[t_pcol], inc=(s == 3), acc=(s > 0))
                    c0 = j * 4
                    if j < 8:
                        kb.op("dve", lambda e: e.tensor_copy(out=modcol[t][:, c0:c0 + 4], in_=pcol[:, 0:4]),
                              reads=[t_pcol], writes=[t_modcol[t]])
                    else:
                        kb.op("dve", lambda e: e.scalar_tensor_tensor(
                            out=modcol[t][:, c0:c0 + 4], in0=pcol[:, 0:4], scalar=1.0,
                            in1=nwcol[:, c0 - 32:c0 - 32 + 4], op0=ALU.add, op1=ALU.mult),
                            reads=[t_pcol, t_nw], writes=[t_modcol[t]])
                else:
                    pb, tpb = pbc[t], t_pbc[t]
                    kb.op("pe", lambda e: e.matmul(pb[:], lhsT=ones_row[0:1, :], rhs=rw[0:1, :],
                                                   start=True, stop=True),
                          reads=[trw, t_ones], writes=[tpb])
                    g0 = (j - 16) * 512
                    kb.op("act", lambda e: e.copy(out=gate_bc[t][:, g0:g0 + 512], in_=pb[:]),
                          reads=[tpb], writes=[t_gate[t]])
        kb.barrier()
    return modcol, gate_bc


def stage_norm(kb, io, l, st_keep, modcol, x_src, ctx_src):
    nc = kb.nc
    hT = st_keep.sb("hT", [128, KC, TL], BF16)
    t_hT = kb.trk("hT")
    with Stage(kb) as st:
        ident = st.sb("ident", [128, 128], BF16)
        xt = [st.sb("xt%d" % i, [128, D], F32) for i in range(2)]
        xn = [st.sb("xn%d" % i, [128, D], BF16) for i in range(2)]
        junk = st.sb("junk", [128, D], BF16)
        ss = [st.sb("ss%d" % i, [128, 1], F32) for i in range(2)]
        rs = [st.sb("rs%d" % i, [128, 1], F32) for i in range(2)]
        tmp = [st.sb("tmp%d" % i, [128, 8, 128], F32) for i in range(2)]
        ptr = [st.ps("ptr%d" % i, [128, 8, 128], BF16) for i in range(4)]
        t_id = kb.trk()
        t_xt = [kb.trk() for _ in range(2)]
        t_xn = [kb.trk() for _ in range(2)]
        t_junk = kb.trk()
        t_ss = [kb.trk() for _ in range(2)]
        t_rs = [kb.trk() for _ in range(2)]
        t_tmp = [kb.trk() for _ in range(2)]
        t_ptr = [kb.trk() for _ in range(4)]
        for _t in t_ptr:
            _t.psum = True
        kb.dma("sp", ident[:], io["ident"], t_id, writes=[t_id])
        pi = 0
        for i in range(NT):
            b = i % 2
            t = 1 if i < 2 else 0
            src = ctx_src[i * 128:(i + 1) * 128, :] if i < 2 else x_src[(i - 2) * 128:(i - 1) * 128, :]
            kb.dma("sp", xt[b][:], src, t_xt[b], writes=[t_xt[b]])
            kb.op("act", lambda e: e.activation(out=junk[:], in_=xt[b][:], func=AF.Square, accum_out=ss[b][:]),
                  reads=[t_xt[b]], writes=[t_junk, t_ss[b]])
            kb.op("act", lambda e: e.activation(out=ss[b][:], in_=ss[b][:], func=AF.Sqrt, scale=1.0 / D, bias=EPS),
                  reads=[t_ss[b]], writes=[t_ss[b]])
            kb.op("dve", lambda e: e.reciprocal(out=rs[b][:], in_=ss[b][:]),
                  reads=[t_ss[b]], writes=[t_rs[b]])
            kb.op("act", lambda e: e.activation(out=xn[b][:], in_=xt[b][:], func=AF.Copy, scale=rs[b][:, 0:1]),
                  reads=[t_xt[b], t_rs[b]], writes=[t_xn[b]])
            for g in range(4):
                p, tp = ptr[pi % 4], t_ptr[pi % 4]
                pi += 1
                for q in range(8):
                    kc = g * 8 + q
                    kb.op("pe", lambda e, kc=kc, q=q: e.transpose(p[:, q, :], xn[b][:, kc * 128:(kc + 1) * 128], ident[:]),
                          reads=[t_xn[b], t_id], writes=[tp], inc=(q == 7), acc=(q > 0))
                tm, ttm = tmp[g % 2], t_tmp[g % 2]
                s1 = modcol[t][:, 32 + g * 8:32 + g * 8 + 8].unsqueeze(2).broadcast_to([128, 8, 128])
                sh = modcol[t][:, g * 8:g * 8 + 8].unsqueeze(2).broadcast_to([128, 8, 128])
                kb.op("dve", lambda e: e.tensor_tensor(out=tm[:], in0=p[:], in1=s1, op=ALU.mult),
                      reads=[tp], writes=[ttm])
                kb.op("pool", lambda e: e.tensor_tensor(out=hT[:, g * 8:(g + 1) * 8, i * 128:(i + 1) * 128],
                                                        in0=tm[:], in1=sh, op=ALU.add),
                      reads=[ttm], writes=[t_hT])
        kb.barrier()
    return hT


TOKBLK = [(0, 512), (512, 512), (1024, 256)]
TOKBLK_LAT = [(NCTX, 512), (NCTX + 512, 512)]


def stage_inproj(kb, io, l, hT, P):
    with Stage(kb) as st:
        wb = [st.sb("wb%d" % i, [128, KC, 512], BF16) for i in range(2)]
        t_wb = [kb.trk() for _ in range(2)]
        stF = [st.sb("stF%d" % i, [128, TL], BF16) for i in range(3)]
        t_stF = [kb.trk() for _ in range(3)]
        stT = [st.sb("stT%d" % i, [128, 512], BF16) for i in range(3)]
        stT32 = [st.sb("stT32_%d" % i, [128, 512], F32) for i in range(3)]
        t_stT = [kb.trk() for _ in range(3)]
        pacc = [st.ps("pa%d" % i, [128, 512]) for i in range(4)]
        t_pacc = [kb.trk() for _ in range(4)]
        for _t in t_pacc:
            _t.psum = True
        t_hT = kb.trk()
        wi = 0
        pi = 0
        fi = 0
        ti = 0
        ev = 0
        for name, width, lay, dt in GROUPS:
            g0 = GOFF[name][0]
            for c0 in range(0, width, 512):
                ncol = min(512, width - c0)
                w, tw = wb[wi % 2], t_wb[wi % 2]
                wi += 1
                wsrc = io["w_in"][l - WL[0], :, g0 + c0:g0 + c0 + ncol].rearrange("(kc p) c -> p kc c", p=128)
                for kq in range(4):
                    kb.dma("pool", w[:, kq * 8:(kq + 1) * 8, 0:ncol], wsrc[:, kq * 8:(kq + 1) * 8, :], tw, writes=[tw])
                tokblk = TOKBLK if (l == 0 or name in ("ak", "dckv", "dkr")) else TOKBLK_LAT
                if lay == "F":
                    for s0 in range(0, ncol, 128):
                        ns = min(128, ncol - s0)
                        sf, tsf = stF[fi % 3], t_stF[fi % 3]
                        fi += 1
                        for (t0, tn) in tokblk:
                            pa, tpa = pacc[pi % 4], t_pacc[pi % 4]
                            pi += 1
                            for kc in range(KC):
                                kb.op("pe", lambda e, kc=kc: e.matmul(pa[0:ns, 0:tn], lhsT=w[:, kc, s0:s0 + ns],
                                                                     rhs=hT[:, kc, t0:t0 + tn],
                                                                     start=(kc == 0), stop=(kc == KC - 1)),
                                      reads=[tw, t_hT], writes=[tpa], inc=(kc == KC - 1), acc=(kc > 0))
                            en = "act" if ev % 2 == 0 else "dve"
                            ev += 1
                            if en == "act":
                                kb.op("act", lambda e: e.copy(out=sf[0:ns, t0:t0 + tn], in_=pa[0:ns, 0:tn]),
                                      reads=[tpa], writes=[tsf])
                            else:
                                kb.op("dve", lambda e: e.tensor_copy(out=sf[0:ns, t0:t0 + tn], in_=pa[0:ns, 0:tn]),
                                      reads=[tpa], writes=[tsf])
                        tb0 = tokblk[0][0]
                        kb.dma("sp", P[name][c0 + s0:c0 + s0 + ns, tb0:TL], sf[0:ns, tb0:TL], tsf, reads=[tsf])
                else:
                    for i in range(NT):
                        pa, tpa = pacc[pi % 4], t_pacc[pi % 4]
                        pi += 1
                        for kc in range(KC):
                            kb.op("pe", lambda e, kc=kc: e.matmul(pa[:, 0:ncol], lhsT=hT[:, kc, i * 128:(i + 1) * 128],
                                                                 rhs=w[:, kc, 0:ncol],
                                                                 start=(kc == 0), stop=(kc == KC - 1)),
                                  reads=[tw, t_hT], writes=[tpa], inc=(kc == KC - 1), acc=(kc > 0))
                        stt = (stT32 if dt == F32 else stT)[ti % 3]
                        tst = t_stT[ti % 3]
                        ti += 1
                        en = "act" if ev % 2 == 0 else "dve"
                        ev += 1
                        if en == "act":
                            kb.op("act", lambda e: e.copy(out=stt[:, 0:ncol], in_=pa[:, 0:ncol]),
                                  reads=[tpa], writes=[tst])
                        else:
                            kb.op("dve", lambda e: e.tensor_copy(out=stt[:, 0:ncol], in_=pa[:, 0:ncol]),
                                  reads=[tpa], writes=[tst])
                        kb.dma("sp", P[name][i * 128:(i + 1) * 128, c0:c0 + ncol], stt[:, 0:ncol], tst, reads=[tst])
        kb.barrier()


def attn_core(kb, bufs, q_tiles, k_tiles, V, sg, ysb, blocks, scale):
    psS, pO, pD, E, ones_bf, rden, tmpo = bufs
    nq = len(q_tiles)
    work = [(q0, qn, kt, ii == 0, ii == len(ktl) - 1) for (q0, qn, ktl) in blocks for ii, kt in enumerate(ktl)]

    def emit_S(idx):
        q0, qn, kt, _, _ = work[idx]
        pS = psS[idx % 2]
        for qi, ((qt, kp), (ktile, kp2)) in enumerate(zip(q_tiles, k_tiles)):
            kb.op("pe", lambda e: e.matmul(pS[:, 0:qn], lhsT=ktile[0:kp, kt * 128:(kt + 1) * 128],
                                           rhs=qt[0:kp, q0:q0 + qn], start=(qi == 0), stop=(qi == nq - 1)),
                  reads=[ktile, qt], writes=[pS], inc=(qi == nq - 1), acc=(qi > 0))

    emit_S(0)
    for idx, (q0, qn, kt, first, last) in enumerate(work):
        if idx + 1 < len(work):
            emit_S(idx + 1)
        pS = psS[idx % 2]
        Eb = E[idx % 3]
        kb.op("act", lambda e: e.activation(out=Eb[:, 0:qn], in_=pS[:, 0:qn], func=AF.Exp, scale=scale),
              reads=[pS], writes=[Eb])
        kb.op("pe", lambda e: e.matmul(pO[:, 0:qn], lhsT=V[:, kt, :], rhs=Eb[:, 0:qn], start=first, stop=last),
              reads=[V, Eb], writes=[pO], inc=False, acc=(not first))
        kb.op("pe", lambda e: e.matmul(pD[:, 0:qn], lhsT=ones_bf[:, :], rhs=Eb[:, 0:qn], start=first, stop=last),
              reads=[ones_bf, Eb], writes=[pD], inc=True, acc=(not first))
        if last:
            kb.op("dve", lambda e: e.reciprocal(out=rden[:, 0:qn], in_=pD[:, 0:qn]), reads=[pD], writes=[rden])
            kb.op("dve", lambda e: e.tensor_tensor(out=tmpo[:, 0:qn], in0=pO[:, 0:qn], in1=rden[:, 0:qn], op=ALU.mult),
                  reads=[pO, rden], writes=[tmpo])
            kb.op("pool", lambda e: e.tensor_tensor(out=ysb[:, q0:q0 + qn], in0=tmpo[:, 0:qn], in1=sg[:, q0:q0 + qn],
                                                    op=ALU.mult),
                  reads=[tmpo, sg], writes=[ysb])


def attn_bufs(kb, st, ones_bf):
    psS = [st.ptile("psS%d" % i, [128, 512]) for i in range(2)]
    pO = st.ptile("pO", [128, 512])
    pD = st.ptile("pD", [128, 512])
    E = [st.tile("E%d" % i, [128, 512], BF16) for i in range(3)]
    rden = st.tile("rden", [128, 512], F32)
    tmpo = st.tile("tmpo", [128, 512], F32)
    return (psS, pO, pD, E, ones_bf, rden, tmpo)


def rstd_from_psum(kb, pss, sd, rstd, n, inv_dim, rows=128):
    kb.op("act", lambda e: e.activation(out=sd[0:rows, 0:n], in_=pss[0:rows, 0:n], func=AF.Sqrt, scale=inv_dim, bias=EPS),
          reads=[pss], writes=[sd])
    kb.op("dve", lambda e: e.reciprocal(out=rstd[0:rows, 0:n], in_=sd[0:rows, 0:n]), reads=[sd], writes=[rstd])


def rope_apply(kb, xn, rm, cos, sin, c0, out, o0, n, kp, prot, t1, t2):
    kb.op("pe", lambda e: e.matmul(prot[0:kp, 0:n], lhsT=rm[0:kp, 0:kp], rhs=xn[0:kp, 0:n], start=True, stop=True),
          reads=[rm, xn], writes=[prot])
    kb.op("pool", lambda e: e.tensor_tensor(out=t1[0:kp, 0:n], in0=xn[0:kp, 0:n], in1=cos[0:kp, c0:c0 + n], op=ALU.mult),
          reads=[xn, cos], writes=[t1])
    kb.op("dve", lambda e: e.tensor_tensor(out=t2[0:kp, 0:n], in0=prot[0:kp, 0:n], in1=sin[0:kp, c0:c0 + n], op=ALU.mult),
          reads=[prot, sin], writes=[t2])
    kb.op("pool", lambda e: e.tensor_tensor(out=out[0:kp, o0:o0 + n], in0=t1[0:kp, 0:n], in1=t2[0:kp, 0:n], op=ALU.add),
          reads=[t1, t2], writes=[out])


def load_const(kb, st, io, name, shape, dtype, src=None):
    t = st.tile("c_" + name, shape, dtype)
    kb.dma("sp", t[:], io[name] if src is None else src, t, writes=[t])
    return t


def qk_norm_rope_128(kb, st_b, raw, wcol, rm, cos, sin, out, ones_bf):
    sq, pss, sd, rstd, xn, prot, t1, t2 = st_b
    for (t0, tn) in TOKBLK:
        kb.op("act", lambda e: e.activation(out=sq[:, 0:tn], in_=raw[:, t0:t0 + tn], func=AF.Square),
              reads=[raw], writes=[sq])
        kb.op("pe", lambda e: e.matmul(pss[:, 0:tn], lhsT=ones_bf[:, :], rhs=sq[:, 0:tn], start=True, stop=True),
              reads=[ones_bf, sq], writes=[pss])
        rstd_from_psum(kb, pss, sd, rstd, tn, 1.0 / 128)
        kb.op("dve", lambda e: e.scalar_tensor_tensor(out=xn[:, 0:tn], in0=raw[:, t0:t0 + tn], scalar=wcol,
                                                      in1=rstd[:, 0:tn], op0=ALU.mult, op1=ALU.mult),
              reads=[raw, rstd], writes=[xn])
        rope_apply(kb, xn, rm, cos, sin, t0, out, t0, tn, 128, prot, t1, t2)


def normrope_bufs(kb, st):
    sq = st.tile("nr_sq", [128, 512], BF16)
    pss = st.ptile("nr_pss", [128, 512])
    sd = st.tile("nr_sd", [128, 512], F32)
    rstd = st.tile("nr_rstd", [128, 512], F32)
    xn = st.tile("nr_xn", [128, 512], BF16)
    prot = st.ptile("nr_prot", [128, 512])
    t1 = st.tile("nr_t1", [128, 512], F32)
    t2 = st.tile("nr_t2", [128, 512], F32)
    return (sq, pss, sd, rstd, xn, prot, t1, t2)


def stage_x_gqa(kb, io, l, P, XK):
    with Stage(kb) as st:
        ones_bf = load_const(kb, st, io, "ones_bf", [128, 128], BF16)
        rmA = load_const(kb, st, io, "rmA", [128, 128], BF16)
        cosA = load_const(kb, st, io, "cosA", [128, TL], F32)
        sinA = load_const(kb, st, io, "sinA", [128, TL], F32)
        cols = load_const(kb, st, io, "cols", [128, NCOLS], F32, src=io["cols"][l])
        nb = normrope_bufs(kb, st)
        for g in range(2):
            raw = st.tile("kraw%d" % g, [128, TL], BF16)
            out = st.tile("kout%d" % g, [128, TL], BF16)
            kb.dma("sp", raw[:], P["ak"][g * 128:(g + 1) * 128, :], raw, writes=[raw])
            qk_norm_rope_128(kb, nb, raw, cols[:, 1:2], rmA, cosA, sinA, out, ones_bf)
            kb.dma("sp", XK[g], out[:], out, reads=[out])
        kb.barrier()


def stage_attn_A(kb, io, l, P, G, Y, ctx_out):
    with Stage(kb) as st:
        ones_bf = load_const(kb, st, io, "ones_bf", [128, 128], BF16)
        rmA = load_const(kb, st, io, "rmA", [128, 128], BF16)
        cosA = load_const(kb, st, io, "cosA", [128, TL], F32)
        sinA = load_const(kb, st, io, "sinA", [128, TL], F32)
        cols = load_const(kb, st, io, "cols", [128, NCOLS], F32, src=io["cols"][l])
        nb = normrope_bufs(kb, st)
        ab = attn_bufs(kb, st, ones_bf)
        kT = st.tile("kT", [128, NKEY], BF16)
        V = st.tile("V", [128, NKEY // 128, 128], BF16)
        qraw = [st.tile("qraw%d" % i, [128, TL], BF16) for i in range(2)]
        graw = [st.tile("graw%d" % i, [128, TL], BF16) for i in range(2)]
        qr = [st.tile("qr%d" % i, [128, TL], BF16) for i in range(2)]
        sg = [st.tile("sg%d" % i, [128, TL], F32) for i in range(2)]
        ysb = [st.tile("ysb%d" % i, [128, TL], BF16) for i in range(2)]
        blocks = []
        if ctx_out:
            blocks.append((0, NCTX, [0, 1]))
        allk = list(range(NKEY // 128))
        blocks += [(NCTX, 512, allk), (NCTX + 512, 512, allk)]
        t0 = 0 if ctx_out else NCTX
        for g in range(2):
            G.load_kT(kb, kT, g)
            G.load_V(kb, V, g)
            for hh in range(4):
                h = g * 4 + hh
                b = h % 2
                kb.dma("sp", qraw[b][:], P["aq"][h * 128:(h + 1) * 128, :], qraw[b], writes=[qraw[b]])
                kb.dma("sp", graw[b][:], P["ag"][h * 128:(h + 1) * 128, :], graw[b], writes=[graw[b]])
                qk_norm_rope_128(kb, nb, qraw[b], cols[:, 0:1], rmA, cosA, sinA, qr[b], ones_bf)
                kb.op("act", lambda e: e.activation(out=sg[b][:], in_=graw[b][:], func=AF.Silu),
                      reads=[graw[b]], writes=[sg[b]])
                attn_core(kb, ab, [(qr[b], 128)], [(kT, 128)], V, sg[b], ysb[b], blocks, 128 ** -0.5)
                kb.dma("sp", Y[h * 128:(h + 1) * 128, t0:TL], ysb[b][:, t0:TL], ysb[b], reads=[ysb[b]])
        kb.barrier()


NCOLS = 48 + 31 * 8
WL = [0]


def rope_tables(R, tpos):
    half = R // 2
    quarter = half // 2
    inv_freq = (10000.0 ** (-np.arange(quarter, dtype=np.float32) / quarter)).astype(np.float32)
    tpos = np.asarray(tpos)
    rows = (tpos // 64).astype(np.float32)
    cols = (tpos % 64).astype(np.float32)
    cos = np.ones((R, len(tpos)), np.float32)
    sin = np.zeros((R, len(tpos)), np.float32)
    valid = tpos >= 0
    for d in range(R):
        pos = rows if d < half else cols
        i = (d % half) % quarter
        ang = (pos * inv_freq[i]).astype(np.float32)
        cos[d, valid] = np.cos(ang)[valid]
        sin[d, valid] = np.sin(ang)[valid]
    return cos, sin


def rope_rot_matrix(R):
    half = R // 2
    quarter = half // 2
    rm = np.zeros((R, R), np.float32)
    for m in range(R):
        dd = m % half
        if dd < quarter:
            rm[m + quarter, m] = -1.0
        else:
            rm[m - quarter, m] = 1.0
    return rm


def make_cols(inp, l):
    c = np.zeros((128, NCOLS), np.float32)
    c[:, 0] = inp["att_q_norm"][l]
    c[:, 1] = inp["att_k_norm"][l]
    c[:, 2] = inp["mla_qk_q_norm"][l][0:128]
    c[0:64, 3] = inp["mla_qk_q_norm"][l][128:192]
    c[:, 4] = inp["mla_qk_k_norm"][l][0:128]
    c[0:64, 5] = inp["mla_qk_k_norm"][l][128:192]
    c[:, 6:12] = inp["mla_q_norm"][l].reshape(6, 128).T
    c[:, 12:16] = inp["mla_kv_norm"][l].reshape(4, 128).T
    c[:, 16:24] = inp["hgrn_o_norm"][l].reshape(8, 128).T
    c[:, 24:32] = inp["conv_b"][l].reshape(8, 128).T
    c[:, 32:40] = inp["conv_ln_w"][l].reshape(8, 128).T
    c[:, 40:48] = inp["conv_ln_b"][l].reshape(8, 128).T
    c[:, 48:] = inp["conv_w"][l].reshape(31, 8, 128).transpose(2, 0, 1).reshape(128, 31 * 8)
    return c


def local_tpos(j):
    return np.concatenate([-np.ones(NCTX, np.int64), np.arange(NLAT, dtype=np.int64) + NLAT * j])


def key_tpos():
    return np.concatenate([-np.ones(NCTX, np.int64), np.arange(SEQ, dtype=np.int64)])


def rms_tiles_F(kb, st, raw, ntile, wcol0, cols, out, ones_bf, inv_dim, blocks, nbufs):
    sq, pss, sd, rstd = nbufs[0], nbufs[1], nbufs[2], nbufs[3]
    for (t0, tn) in blocks:
        for c in range(ntile):
            kb.op("act", lambda e: e.activation(out=sq[:, 0:tn], in_=raw[:, c, t0:t0 + tn], func=AF.Square),
                  reads=[raw], writes=[sq])
            kb.op("pe", lambda e: e.matmul(pss[:, 0:tn], lhsT=ones_bf[:, :], rhs=sq[:, 0:tn],
                                           start=(c == 0), stop=(c == ntile - 1)),
                  reads=[ones_bf, sq], writes=[pss], acc=(c > 0))
        rstd_from_psum(kb, pss, sd, rstd, tn, inv_dim)
        for c in range(ntile):
            kb.op("dve", lambda e: e.scalar_tensor_tensor(out=out[:, c, t0:t0 + tn], in0=raw[:, c, t0:t0 + tn],
                                                          scalar=cols[:, wcol0 + c:wcol0 + c + 1], in1=rstd[:, 0:tn],
                                                          op0=ALU.mult, op1=ALU.mult),
                  reads=[raw, rstd], writes=[out])


def stage_x_mla(kb, io, l, P, XC):
    with Stage(kb) as st:
        ones_bf = load_const(kb, st, io, "ones_bf", [128, 128], BF16)
        cols = load_const(kb, st, io, "cols", [128, NCOLS], F32, src=io["cols"][l])
        nb = normrope_bufs(kb, st)
        raw = st.tile("ckraw", [128, 4, TL], BF16)
        out = st.tile("ckout", [128, 4, TL], BF16)
        kb.dma("sp", raw[:], P["dckv"].rearrange("(c p) t -> p c t", p=128), raw, writes=[raw])
        rms_tiles_F(kb, st, raw, 4, 12, cols, out, ones_bf, 1.0 / 512, TOKBLK, nb)
        kb.dma("sp", XC.rearrange("(c p) t -> p c t", p=128), out[:], out, reads=[out])
        kb.barrier()


KEYBLK = [(i * 512, min(512, NKEY - i * 512)) for i in range((NKEY + 511) // 512)]


def stage_attn_D(kb, io, l, P, G, Y, ctx_out):
    with Stage(kb) as st:
        ones_bf = load_const(kb, st, io, "ones_bf", [128, 128], BF16)
        rmD = load_const(kb, st, io, "rmD", [64, 64], BF16)
        cols = load_const(kb, st, io, "cols", [128, NCOLS], F32, src=io["cols"][l])
        cosDq = load_const(kb, st, io, "cosDq", [64, TL], F32)
        sinDq = load_const(kb, st, io, "sinDq", [64, TL], F32)
        nb = normrope_bufs(kb, st)
        sq, pss, sd, rstd, xn, prot, t1, t2 = nb
        ab = attn_bufs(kb, st, ones_bf)
        pA = st.ptile("pA", [128, 512])
        pB = st.ptile("pB", [128, 512])
        ckv = st.tile("ckv", [128, 4, NKEY], BF16)
        sqr = st.tile("sqr", [64, NKEY], BF16)
        krr = st.tile("krr", [64, NKEY], F32)
        cqn = st.tile("cqn", [128, 6, TL], BF16)
        G.load_ckv(kb, ckv)
        with Stage(kb) as s0:
            cosDk = load_const(kb, s0, io, "cosDk", [64, NKEY], F32)
            sinDk = load_const(kb, s0, io, "sinDk", [64, NKEY], F32)
            kr = s0.tile("kr", [64, NKEY], BF16)
            cqraw = s0.tile("cqraw", [128, 6, TL], BF16)
            G.load_kr(kb, kr)
            kb.dma("sp", cqraw[:], P["dcq"].rearrange("(c p) t -> p c t", p=128), cqraw, writes=[cqraw])
            kb.op("act", lambda e: e.activation(out=sqr[:], in_=kr[:], func=AF.Square), reads=[kr], writes=[sqr])
            for (k0, kn_) in KEYBLK:
                kb.op("dve", lambda e: e.tensor_scalar(out=xn[0:64, 0:kn_], in0=kr[:, k0:k0 + kn_], scalar1=cols[0:64, 5:6],
                                                       scalar2=None, op0=ALU.mult),
                      reads=[kr], writes=[xn])
                rope_apply(kb, xn, rmD, cosDk, sinDk, k0, krr, k0, kn_, 64, prot, t1, t2)
            rms_tiles_F(kb, s0, cqraw, 6, 6, cols, cqn, ones_bf, 1.0 / 768, TOKBLK, nb)
            kb.barrier()
        wuq = [st.tile("wuq%d" % i, [128, 6, 192], BF16) for i in range(2)]
        wukv = [st.tile("wukv%d" % i, [128, 4, 256], BF16) for i in range(2)]
        qnope = [st.tile("qnope%d" % i, [128, TL], BF16) for i in range(2)]
        qrope = [st.tile("qrope%d" % i, [64, TL], BF16) for i in range(2)]
        kn = st.tile("kn", [128, NKEY], BF16)
        krh = st.tile("krh", [64, NKEY], BF16)
        V = st.tile("Vd", [128, NKEY // 128, 128], BF16)
        graw = [st.tile("dgraw%d" % i, [128, TL], BF16) for i in range(2)]
        sg = [st.tile("dsg%d" % i, [128, TL], F32) for i in range(2)]
        ysb = [st.tile("dysb%d" % i, [128, TL], BF16) for i in range(2)]
        sq2 = st.tile("sq2", [64, 512], BF16)
        blocks = []
        if ctx_out:
            blocks.append((0, NCTX, [0, 1]))
        allk = list(range(NKEY // 128))
        blocks += [(NCTX, 512, allk), (NCTX + 512, 512, allk)]
        t0o = 0 if ctx_out else NCTX
        for h in range(8):
            b = h % 2
            kb.dma("pool", wuq[b][:], io["mla_w_uq"][l - WL[0], :, h * 192:(h + 1) * 192].rearrange("(c p) n -> p c n", p=128),
                   wuq[b], writes=[wuq[b]])
            kb.dma("pool", wukv[b][:], io["mla_w_ukv"][l - WL[0], :, h * 256:(h + 1) * 256].rearrange("(c p) n -> p c n", p=128),
                   wukv[b], writes=[wukv[b]])
            kb.dma("sp", graw[b][:], P["dg"][h * 128:(h + 1) * 128, :], graw[b], writes=[graw[b]])
            kb.op("act", lambda e: e.activation(out=sg[b][:], in_=graw[b][:], func=AF.Silu), reads=[graw[b]], writes=[sg[b]])
            for (t0, tn) in TOKBLK:
                for c in range(6):
                    kb.op("pe", lambda e: e.matmul(pA[:, 0:tn], lhsT=wuq[b][:, c, 0:128], rhs=cqn[:, c, t0:t0 + tn],
                                                   start=(c == 0), stop=(c == 5)),
                          reads=[wuq[b], cqn], writes=[pA], inc=(c == 5), acc=(c > 0))
                for c in range(6):
                    kb.op("pe", lambda e: e.matmul(pB[0:64, 0:tn], lhsT=wuq[b][:, c, 128:192], rhs=cqn[:, c, t0:t0 + tn],
                                                   start=(c == 0), stop=(c == 5)),
                          reads=[wuq[b], cqn], writes=[pB], inc=(c == 5), acc=(c > 0))
                kb.op("act", lambda e: e.activation(out=sq[:, 0:tn], in_=pA[:, 0:tn], func=AF.Square), reads=[pA], writes=[sq])
                kb.op("act", lambda e: e.activation(out=sq2[:, 0:tn], in_=pB[0:64, 0:tn], func=AF.Square), reads=[pB], writes=[sq2])
                kb.op("pe", lambda e: e.matmul(pss[:, 0:tn], lhsT=ones_bf[:, :], rhs=sq[:, 0:tn], start=True, stop=False),
                      reads=[ones_bf, sq], writes=[pss], inc=False)
                kb.op("pe", lambda e: e.matmul(pss[:, 0:tn], lhsT=ones_bf[0:64, :], rhs=sq2[:, 0:tn], start=False, stop=True),
                      reads=[ones_bf, sq2], writes=[pss], acc=True)
                rstd_from_psum(kb, pss, sd, rstd, tn, 1.0 / 192)
                kb.op("dve", lambda e: e.scalar_tensor_tensor(out=qnope[b][:, t0:t0 + tn], in0=pA[:, 0:tn], scalar=cols[:, 2:3],
                                                              in1=rstd[:, 0:tn], op0=ALU.mult, op1=ALU.mult),
                      reads=[pA, rstd], writes=[qnope[b]])
                kb.op("dve", lambda e: e.scalar_tensor_tensor(out=xn[0:64, 0:tn], in0=pB[0:64, 0:tn], scalar=cols[0:64, 3:4],
                                                              in1=rstd[0:64, 0:tn], op0=ALU.mult, op1=ALU.mult),
                      reads=[pB, rstd], writes=[xn])
                rope_apply(kb, xn, rmD, cosDq, sinDq, t0, qrope[b], t0, tn, 64, prot, t1, t2)
            for (k0, kn_) in KEYBLK:
                for c in range(4):
                    kb.op("pe", lambda e: e.matmul(pA[:, 0:kn_], lhsT=wukv[b][:, c, 0:128], rhs=ckv[:, c, k0:k0 + kn_],
                                                   start=(c == 0), stop=(c == 3)),
                          reads=[wukv[b], ckv], writes=[pA], inc=(c == 3), acc=(c > 0))
                kb.op("act", lambda e: e.activation(out=sq[:, 0:kn_], in_=pA[:, 0:kn_], func=AF.Square), reads=[pA], writes=[sq])
                kb.op("pe", lambda e: e.matmul(pss[:, 0:kn_], lhsT=ones_bf[:, :], rhs=sq[:, 0:kn_], start=True, stop=False),
                      reads=[ones_bf, sq], writes=[pss], inc=False)
                kb.op("pe", lambda e: e.matmul(pss[:, 0:kn_], lhsT=ones_bf[0:64, :], rhs=sqr[:, k0:k0 + kn_], start=False, stop=True),
                      reads=[ones_bf, sqr], writes=[pss], acc=True)
                rstd_from_psum(kb, pss, sd, rstd, kn_, 1.0 / 192)
                kb.op("dve", lambda e: e.scalar_tensor_tensor(out=kn[:, k0:k0 + kn_], in0=pA[:, 0:kn_], scalar=cols[:, 4:5],
                                                              in1=rstd[:, 0:kn_], op0=ALU.mult, op1=ALU.mult),
                      reads=[pA, rstd], writes=[kn])
                kb.op("pool", lambda e: e.tensor_tensor(out=krh[:, k0:k0 + kn_], in0=krr[:, k0:k0 + kn_], in1=rstd[0:64, 0:kn_],
                                                        op=ALU.mult),
                      reads=[krr, rstd], writes=[krh])
            nkt = NKEY // 128
            for k4 in range(0, nkt, 4):
                n4 = min(4, nkt - k4)
                for i in range(n4):
                    kt = k4 + i
                    for c in range(4):
                        kb.op("pe", lambda e: e.matmul(pB[:, i * 128:(i + 1) * 128], lhsT=ckv[:, c, kt * 128:(kt + 1) * 128],
                                                       rhs=wukv[b][:, c, 128:256], start=(c == 0), stop=(c == 3)),
                              reads=[wukv[b], ckv], writes=[pB], inc=(c == 3 and i == n4 - 1), acc=(c > 0 or i > 0))
                kb.op("act", lambda e: e.copy(out=V[:, k4:k4 + n4, :], in_=pB[:, 0:n4 * 128].rearrange("p (a b) -> p a b", b=128)),
                      reads=[pB], writes=[V])
            attn_core(kb, ab, [(qnope[b], 128), (qrope[b], 64)], [(kn, 128), (krh, 64)], V, sg[b], ysb[b], blocks, 192 ** -0.5)
            kb.dma("sp", Y[3072 + h * 128:3072 + (h + 1) * 128, t0o:TL], ysb[b][:, t0o:TL], ysb[b], reads=[ysb[b]])
        kb.barrier()


XO_CTX = 15
XO_LAT = 15 + NCTX + 15 + 15
XW = XO_LAT + NLAT + 15
CVW = XW - 30
CVBLK = [(0, 512), (512, 512), (1024, CVW - 1024)]
NTAP_DVE = 31


def stage_conv(kb, io, l, P, G, Y, ctx_out):
    with Stage(kb) as st:
        ones_bf = load_const(kb, st, io, "ones_bf", [128, 128], BF16)
        cols = load_const(kb, st, io, "cols", [128, NCOLS], F32, src=io["cols"][l])
        cv = st.tile("cv", [128, 8, CVW], F32)
        ybf = st.tile("ybf", [128, 8, CVW], BF16)
        ysq = st.tile("ysq", [128, 8, CVW], BF16)
        u = [st.tile("cu%d" % i, [128, TL], BF16) for i in range(2)]
        glu = [st.tile("cglu%d" % i, [128, TL], BF16) for i in range(2)]
        hu = [st.tile("hu%d" % i, [128, 4, 32], BF16) for i in range(2)]
        hg = [st.tile("hg%d" % i, [128, 4, 32], BF16) for i in range(2)]
        sig = st.tile("csig", [128, TL], F32)
        hsig = st.tile("chsig", [128, 4, 32], F32)
        hxg = st.tile("chxg", [128, 4, 32], F32)
        hsel = st.tile("chsel", [128, 2, 16], F32)
        hmk = load_const(kb, st, io, "halomask", [128, 2, 4], F32)
        xg = [st.tile("xg%d" % i, [128, XW], F32) for i in range(2)]
        acc2 = st.tile("acc2", [128, CVW], F32)
        for i in range(2):
            kb.op("pool", lambda e: e.memset(xg[i][:], 0.0), writes=[xg[i]])
        for ct in range(8):
            b = ct % 2
            r0 = ct * 128
            kb.dma("sp", u[b][:], P["cu"][r0:r0 + 128, :], u[b], writes=[u[b]])
            kb.dma("sp", glu[b][:], P["cglu"][r0:r0 + 128, :], glu[b], writes=[glu[b]])
            G.load_halo(kb, hu[b], hg[b], r0)
            kb.op("act", lambda e: e.activation(out=sig[:], in_=glu[b][:], func=AF.Sigmoid), reads=[glu[b]], writes=[sig])
            kb.op("act", lambda e: e.activation(out=hsig[:], in_=hg[b][:], func=AF.Sigmoid), reads=[hg[b]], writes=[hsig])
            X = xg[b]
            kb.op("dve", lambda e: e.tensor_tensor(out=X[:, XO_CTX:XO_CTX + NCTX], in0=u[b][:, 0:NCTX], in1=sig[:, 0:NCTX], op=ALU.mult),
                  reads=[u[b], sig], writes=[X])
            kb.op("dve", lambda e: e.tensor_tensor(out=X[:, XO_LAT:XO_LAT + NLAT], in0=u[b][:, NCTX:TL], in1=sig[:, NCTX:TL], op=ALU.mult),
                  reads=[u[b], sig], writes=[X])
            kb.op("dve", lambda e: e.tensor_tensor(out=hxg[:], in0=hu[b][:], in1=hsig[:], op=ALU.mult),
                  reads=[hu[b], hsig], writes=[hxg])
            for side in range(2):
                slot = 1 - side
                for r in range(4):
                    src = hxg[:, r, slot * 16:slot * 16 + 16]
                    mcol = hmk[:, side, r:r + 1]
                    if r == 0:
                        kb.op("dve", lambda e: e.tensor_scalar(out=hsel[:, side, :], in0=src, scalar1=mcol, scalar2=None, op0=ALU.mult),
                              reads=[hxg, hmk], writes=[hsel])
                    else:
                        kb.op("dve", lambda e: e.scalar_tensor_tensor(out=hsel[:, side, :], in0=src, scalar=mcol, in1=hsel[:, side, :],
                                                                      op0=ALU.mult, op1=ALU.add),
                              reads=[hxg, hmk, hsel], writes=[hsel])
            kb.op("dve", lambda e: e.tensor_copy(out=X[:, XO_LAT - 15:XO_LAT], in_=hsel[:, 0, 0:15]), reads=[hsel], writes=[X])
            kb.op("dve", lambda e: e.tensor_copy(out=X[:, XO_LAT + NLAT:XO_LAT + NLAT + 15], in_=hsel[:, 1, 0:15]), reads=[hsel], writes=[X])
            for tap in range(31):
                wc = cols[:, 48 + tap * 8 + ct:48 + tap * 8 + ct + 1]
                if tap < NTAP_DVE:
                    en, dst = "dve", cv[:, ct, :]
                    first = (tap == 0)
                    dtk = cv
                else:
                    en, dst = "pool", acc2[:, :]
                    first = (tap == NTAP_DVE)
                    dtk = acc2
                if first:
                    kb.op(en, lambda e: e.tensor_scalar(out=dst, in0=X[:, tap:tap + CVW], scalar1=wc, scalar2=None, op0=ALU.mult),
                          reads=[X], writes=[dtk])
                else:
                    kb.op(en, lambda e: e.scalar_tensor_tensor(out=dst, in0=X[:, tap:tap + CVW], scalar=wc, in1=dst,
                                                              op0=ALU.mult, op1=ALU.add),
                          reads=[X, dtk], writes=[dtk])
            kb.op("dve", lambda e: e.tensor_scalar(out=cv[:, ct, :], in0=cv[:, ct, :], scalar1=cols[:, 24 + ct:25 + ct],
                                                   scalar2=None, op0=ALU.add),
                  reads=[cv], writes=[cv])
            kb.op("act", lambda e: e.copy(out=ybf[:, ct, :], in_=cv[:, ct, :]), reads=[cv], writes=[ybf])
            kb.op("act", lambda e: e.activation(out=ysq[:, ct, :], in_=cv[:, ct, :], func=AF.Square), reads=[cv], writes=[ysq])
        pm = st.ptile("cpm", [128, 512])
        pq = st.ptile("cpq", [128, 512])
        mean = st.tile("cmean", [128, CVW], F32)
        rstd = st.tile("crstd", [128, CVW], F32)
        var = st.tile("cvar", [128, 512], F32)
        msq = st.tile("cmsq", [128, 512], F32)
        for (c0, cn) in CVBLK:
            for ct in range(8):
                kb.op("pe", lambda e: e.matmul(pm[:, 0:cn], lhsT=ones_bf[:, :], rhs=ybf[:, ct, c0:c0 + cn], start=(ct == 0), stop=(ct == 7)),
                      reads=[ones_bf, ybf], writes=[pm], inc=(ct == 7), acc=(ct > 0))
            for ct in range(8):
                kb.op("pe", lambda e: e.matmul(pq[:, 0:cn], lhsT=ones_bf[:, :], rhs=ysq[:, ct, c0:c0 + cn], start=(ct == 0), stop=(ct == 7)),
                      reads=[ones_bf, ysq], writes=[pq], inc=(ct == 7), acc=(ct > 0))
            kb.op("act", lambda e: e.activation(out=mean[:, c0:c0 + cn], in_=pm[:, 0:cn], func=AF.Copy, scale=1.0 / 1024),
                  reads=[pm], writes=[mean])
            kb.op("dve", lambda e: e.tensor_tensor(out=msq[:, 0:cn], in0=mean[:, c0:c0 + cn], in1=mean[:, c0:c0 + cn], op=ALU.mult),
                  reads=[mean], writes=[msq])
            kb.op("dve", lambda e: e.scalar_tensor_tensor(out=var[:, 0:cn], in0=pq[:, 0:cn], scalar=1.0 / 1024, in1=msq[:, 0:cn],
                                                          op0=ALU.mult, op1=ALU.subtract),
                  reads=[pq, msq], writes=[var])
            kb.op("act", lambda e: e.activation(out=var[:, 0:cn], in_=var[:, 0:cn], func=AF.Sqrt, bias=EPS, scale=1.0),
                  reads=[var], writes=[var])
            kb.op("dve", lambda e: e.reciprocal(out=rstd[:, c0:c0 + cn], in_=var[:, 0:cn]), reads=[var], writes=[rstd])
        gt = [st.tile("cgt%d" % i, [128, TL], BF16) for i in range(2)]
        sgt = [st.tile("csg%d" % i, [128, TL], F32) for i in range(2)]
        z = [st.tile("cz%d" % i, [128, CVW], F32) for i in range(2)]
        yo = [st.tile("cyo%d" % i, [128, TL], BF16) for i in range(2)]
        t0o = 0 if ctx_out else NCTX
        for ct in range(8):
            b = ct % 2
            r0 = ct * 128
            kb.dma("sp", gt[b][:], P["cg"][r0:r0 + 128, :], gt[b], writes=[gt[b]])
            kb.op("act", lambda e: e.activation(out=sgt[b][:], in_=gt[b][:], func=AF.Silu), reads=[gt[b]], writes=[sgt[b]])
            kb.op("dve", lambda e: e.tensor_tensor(out=z[b][:], in0=cv[:, ct, :], in1=mean[:], op=ALU.subtract),
                  reads=[cv, mean], writes=[z[b]])
            kb.op("pool", lambda e: e.tensor_tensor(out=z[b][:], in0=z[b][:], in1=rstd[:], op=ALU.mult),
                  reads=[z[b], rstd], writes=[z[b]])
            kb.op("act", lambda e: e.activation(out=z[b][:], in_=z[b][:], func=AF.Silu, scale=cols[:, 32 + ct:33 + ct],
                                                bias=cols[:, 40 + ct:41 + ct]),
                  reads=[z[b]], writes=[z[b]])
            kb.op("dve", lambda e: e.tensor_tensor(out=yo[b][:, 0:NCTX], in0=z[b][:, 0:NCTX], in1=sgt[b][:, 0:NCTX], op=ALU.mult),
                  reads=[z[b], sgt[b]], writes=[yo[b]])
            kb.op("pool", lambda e: e.tensor_tensor(out=yo[b][:, NCTX:TL], in0=z[b][:, XO_LAT - 15:XO_LAT - 15 + NLAT],
                                                    in1=sgt[b][:, NCTX:TL], op=ALU.mult),
                  reads=[z[b], sgt[b]], writes=[yo[b]])
            kb.dma("sp", Y[2048 + r0:2048 + r0 + 128, t0o:TL], yo[b][:, t0o:TL], yo[b], reads=[yo[b]])
        kb.barrier()


def stage_outproj(kb, io, l, Y, gate_src, x_src, ctx_src, x_dst, ctx_dst, ctx_out):
    with Stage(kb) as st:
        yT = st.tile("yT", [128, KC, TL], BF16)
        gbc = [st.tile("gbc%d" % t, [128, D], F32) for t in range(2)]
        wo = [st.tile("wo%d" % i, [128, KC, 256], BF16) for i in range(2)]
        xt = [st.tile("oxt%d" % i, [128, 256], F32) for i in range(3)]
        ot = [st.tile("oot%d" % i, [128, 256], F32) for i in range(3)]
        pacc = [st.ptile("opa%d" % i, [128, 256]) for i in range(4)]
        t0o = 0 if ctx_out else NCTX
        ysrc = Y.rearrange("(c p) t -> p c t", p=128)
        for q in range(4):
            kb.dma("sp", yT[:, q * 8:(q + 1) * 8, t0o:TL], ysrc[:, q * 8:(q + 1) * 8, t0o:TL], yT, writes=[yT])
        kb.dma("sp", gbc[0][:], gate_src[0], gbc[0], writes=[gbc[0]])
        if ctx_out:
            kb.dma("sp", gbc[1][:], gate_src[1], gbc[1], writes=[gbc[1]])
        pi = 0
        xi = 0
        for cb in range(D // 256):
            c0 = cb * 256
            w = wo[cb % 2]
            wsrc = io["w_out"][l - WL[0], :, c0:c0 + 256].rearrange("(kc p) c -> p kc c", p=128)
            for kq in range(4):
                kb.dma("pool", w[:, kq * 8:(kq + 1) * 8, :], wsrc[:, kq * 8:(kq + 1) * 8, :], w, writes=[w])
            for i in range(NT):
                if i < 2 and not ctx_out:
                    continue
                t = 1 if i < 2 else 0
                srcx = ctx_src[i * 128:(i + 1) * 128, c0:c0 + 256] if i < 2 else x_src[(i - 2) * 128:(i - 1) * 128, c0:c0 + 256]
                dstx = ctx_dst[i * 128:(i + 1) * 128, c0:c0 + 256] if i < 2 else x_dst[(i - 2) * 128:(i - 1) * 128, c0:c0 + 256]
                pa = pacc[pi % 4]
                pi += 1
                xx, oo = xt[xi % 3], ot[xi % 3]
                xi += 1
                kb.dma("sp", xx[:], srcx, xx, writes=[xx])
                for kc in range(KC):
                    kb.op("pe", lambda e: e.matmul(pa[:, :], lhsT=yT[:, kc, i * 128:(i + 1) * 128], rhs=w[:, kc, :],
                                                   start=(kc == 0), stop=(kc == KC - 1)),
                          reads=[yT, w], writes=[pa], inc=(kc == KC - 1), acc=(kc > 0))
                kb.op("dve", lambda e: e.tensor_tensor(out=oo[:], in0=pa[:], in1=gbc[t][:, c0:c0 + 256], op=ALU.mult),
                      reads=[pa, gbc[t]], writes=[oo])
                kb.op("pool", lambda e: e.tensor_tensor(out=oo[:], in0=oo[:], in1=xx[:], op=ALU.add),
                      reads=[oo, xx], writes=[oo])
                kb.dma("sp", dstx, oo[:], oo, reads=[oo])
        kb.barrier()


def hgrn_consts():
    s = np.arange(64)[:, None]
    t = np.arange(64)[None, :]
    out = {}
    M1 = np.zeros((2, 128, 128), np.float32)
    SC = np.zeros((2, 128, 128), np.float32)
    MK = np.zeros((2, 128, 128), np.float32)
    MC = np.zeros((2, 128, 4), np.float32)
    for d in range(2):
        if d == 0:
            m1 = (s <= t).astype(np.float32) - (s <= 31).astype(np.float32)
            sc = (s > t).astype(np.float32)
            mk = (s <= t).astype(np.float32)
            mid = (np.arange(64) <= 31).astype(np.float32)
        else:
            m1 = (s >= t).astype(np.float32) - (s >= 32).astype(np.float32)
            sc = (s < t).astype(np.float32)
            mk = (s >= t).astype(np.float32)
            mid = (np.arange(64) >= 32).astype(np.float32)
        for hf in range(2):
            sl = slice(hf * 64, hf * 64 + 64)
            M1[d, sl, sl] = m1
            SC[d, sl, sl] = sc
            MK[d, sl, sl] = mk
            MC[d, sl, hf] = mid
            MC[d, sl, 2 + hf] = 1.0
    out["hM1"] = M1
    out["hM1n"] = -M1
    out["hSC"] = SC
    out["hMK"] = MK
    out["hMC"] = MC
    out["identf"] = np.eye(128, dtype=np.float32)
    return out


def stage_hgrn(kb, io, l, P, G, Y, XS, XD, pass_id, ctx_out, dbg=None):
    full = (pass_id == 2)
    with Stage(kb) as st:
        identf = load_const(kb, st, io, "identf", [128, 128], F32)
        cols = load_const(kb, st, io, "cols", [128, NCOLS], F32, src=io["cols"][l])
        PA = st.ptile("hPA", [128, 1024])
        PB = st.ptile("hPB", [128, 1024])
        PC = st.ptile("hPC", [128, 1024])
        PD = st.ptile("hPD", [128, 512])
        oml = st.tile("oml", [128, 2, 1024], F32)
        if l == 0:
            kb.op("dve", lambda e: e.memset(oml[:], 1.0), writes=[oml])
        else:
            with Stage(kb) as s0:
                lg = s0.tile("lg", [1, 2, 2, 1024], F32)
                df = s0.tile("lgd", [1, 2, 1024], F32)
                ones_row = s0.tile("onesr", [1, 128], F32)
                kb.dma("sp", lg[:], io["hgrn_lb_logits"].rearrange("(o a) b c -> o a b c", o=1), lg, writes=[lg])
                kb.op("dve", lambda e: e.memset(ones_row[:], 1.0), writes=[ones_row])
                kb.op("dve", lambda e: e.tensor_tensor(out=df[:], in0=lg[:, 0, :, :], in1=lg[:, 1, :, :], op=ALU.subtract),
                      reads=[lg], writes=[df])
                kb.op("act", lambda e: e.activation(out=df[:], in_=df[:], func=AF.Sigmoid), reads=[df], writes=[df])
                for d in range(2):
                    for n in range(2):
                        kb.op("pe", lambda e: e.matmul(PA[:, n * 512:(n + 1) * 512], lhsT=ones_row[0:1, :],
                                                       rhs=df[0:1, d, n * 512:(n + 1) * 512], start=True, stop=True),
                              reads=[ones_row, df], writes=[PA], acc=(n > 0))
                    kb.op("act", lambda e: e.copy(out=oml[:, d, :], in_=PA[:, :]), reads=[PA], writes=[oml])
                kb.barrier()
        S = st.tile("hS", [128, 8, 128], F32)
        Sbf = st.tile("hSbf", [128, 8, 128], BF16)
        stmp = st.tile("hstmp", [128, 8, 128], F32)
        Dcum = st.tile("hDcum", [128, 8], F32)
        ft = [st.tile("hf%d" % i, [128, 1024], F32) for i in range(2)]
        kt = st.tile("hk", [128, 1024], F32)
        lf = [st.tile("hlf%d" % i, [128, 1024], F32) for i in range(2)]
        logk = [st.tile("hlogk%d" % i, [128, 1024], F32) for i in range(2)]
        Kd = [st.tile("hKd%d" % i, [128, 1024], BF16) for i in range(2)]
        vt = [st.tile("hv%d" % i, [128, 1024], BF16) for i in range(2)]
        gc = [st.tile("hgc%d" % i, [128, 8, 4], F32) for i in range(2)]
        if full:
            qs = st.tile("hqs", [128, 8, TL], BF16)
            oacc = st.tile("hoacc", [128, 8, TL], F32)
            eA = st.tile("heA", [128, 8, 128], F32)
            AT = [st.tile("hAT%d" % i, [128, 8, 128], BF16) for i in range(2)]
            BT = [st.tile("hBT%d" % i, [128, 8, 2, 128], BF16) for i in range(2)]
            for i in range(2):
                kb.op("pool", lambda e: e.memset(BT[i][:], 0.0), writes=[BT[i]])
            scm = [st.tile("hscm%d" % i, [128, 8, 128], BF16) for i in range(2)]
            kb.dma("sp", qs[:], P["bq"].rearrange("(h p) t -> p h t", p=128), qs, writes=[qs])
            kb.op("act", lambda e: e.activation(out=qs[:], in_=qs[:], func=AF.Silu), reads=[qs], writes=[qs])
            kb.op("dve", lambda e: e.tensor_scalar(out=qs[:], in0=qs[:], scalar1=float(128 ** -0.5), scalar2=None, op0=ALU.mult),
                  reads=[qs], writes=[qs])
        ti = 0
        for d in range(2):
            M1 = load_const(kb, st, io, "hM1_%d" % d, [128, 128], F32, src=io["hM1"][d])
            M1n = load_const(kb, st, io, "hM1n_%d" % d, [128, 128], F32, src=io["hM1n"][d])
            SC = load_const(kb, st, io, "hSC_%d" % d, [128, 128], F32, src=io["hSC"][d])
            MK = load_const(kb, st, io, "hMK_%d" % d, [128, 128], F32, src=io["hMK"][d])
            MC = load_const(kb, st, io, "hMC_%d" % d, [128, 4], F32, src=io["hMC"][d])
            if full:
                MK8 = st.tile("hMK8_%d" % d, [128, 8, 128], F32)
                kb.op("dve", lambda e: e.tensor_copy(out=MK8[:], in_=MK[:, :].unsqueeze(1).broadcast_to([128, 8, 128])),
                      reads=[MK], writes=[MK8])
                for i in range(2):
                    kb.op("pool", lambda e: e.memset(scm[i][:], 0.0), writes=[scm[i]])
            fsrc = P["bff"] if d == 0 else P["bfb"]
            tiles = ([0, 1] + list(range(2, NT))) if d == 0 else ([1, 0] + list(range(NT - 1, 1, -1)))
            kb.op("pool", lambda e: e.memset(S[:], 0.0), writes=[S])
            for n_, i in enumerate(tiles):
                b = ti % 2
                ti += 1
                if n_ == 2:
                    if not full:
                        kb.op("pool", lambda e: e.memset(S[:], 0.0), writes=[S])
                        kb.op("pool", lambda e: e.memset(Dcum[:], 1.0), writes=[Dcum])
                    else:
                        with Stage(kb) as s1:
                            hm = s1.tile("hm", [128, 2, 4], F32)
                            kb.dma("sp", hm[:], io["hmask"], hm, writes=[hm])
                            Si = [s1.tile("hSi%d" % q, [128, 8, 128], F32) for q in range(2)]
                            Di = [s1.tile("hDi%d" % q, [128, 8], F32) for q in range(2)]
                            order = [0, 1, 2, 3] if d == 0 else [3, 2, 1, 0]
                            for q_, ci in enumerate(order):
                                sb_, db_ = Si[q_ % 2], Di[q_ % 2]
                                kb.dma("sp", sb_[:], G.hS(ci, d), sb_, writes=[sb_])
                                kb.dma("sp", db_[:], G.hD(ci, d), db_, writes=[db_])
                                mcol = hm[:, d, ci:ci + 1]
                                kb.op("dve", lambda e: e.tensor_scalar(out=db_[:], in0=db_[:], scalar1=-1.0, scalar2=mcol,
                                                                       op0=ALU.add, op1=ALU.mult),
                                      reads=[db_, hm], writes=[db_])
                                kb.op("dve", lambda e: e.tensor_scalar(out=db_[:], in0=db_[:], scalar1=1.0, scalar2=None, op0=ALU.add),
                                      reads=[db_], writes=[db_])
                                kb.op("dve", lambda e: e.tensor_tensor(out=stmp[:], in0=S[:],
                                                                       in1=db_[:].unsqueeze(2).broadcast_to([128, 8, 128]), op=ALU.mult),
                                      reads=[S, db_], writes=[stmp])
                                kb.op("dve", lambda e: e.scalar_tensor_tensor(out=S[:], in0=sb_[:], scalar=mcol, in1=stmp[:],
                                                                              op0=ALU.mult, op1=ALU.add),
                                      reads=[sb_, stmp, hm], writes=[S])
                            kb.barrier()
                F_ = ft[b]
                kb.dma("sp", F_[:], fsrc[i * 128:(i + 1) * 128, :], F_, writes=[F_])
                kb.dma("sp", vt[b][:], P["bi"][i * 128:(i + 1) * 128, :], vt[b], writes=[vt[b]])
                kb.op("act", lambda e: e.activation(out=kt[:], in_=F_[:], func=AF.Sigmoid, scale=-1.0), reads=[F_], writes=[kt])
                kb.op("dve", lambda e: e.tensor_tensor(out=kt[:], in0=kt[:], in1=oml[:, d, :], op=ALU.mult), reads=[kt, oml], writes=[kt])
                kb.op("act", lambda e: e.activation(out=logk[b][:], in_=kt[:], func=AF.Ln), reads=[kt], writes=[logk[b]])
                kb.op("act", lambda e: e.activation(out=lf[b][:], in_=kt[:], func=AF.Ln, scale=-1.0, bias=1.0), reads=[kt], writes=[lf[b]])
                for n in range(2):
                    kb.op("pe", lambda e: e.matmul(PA[:, n * 512:(n + 1) * 512], lhsT=SC[:, :], rhs=lf[b][:, n * 512:(n + 1) * 512],
                                                   start=True, stop=False),
                          reads=[SC, lf[b]], writes=[PA], inc=False, acc=(n > 0))
                    kb.op("pe", lambda e: e.matmul(PA[:, n * 512:(n + 1) * 512], lhsT=identf[:, :], rhs=logk[b][:, n * 512:(n + 1) * 512],
                                                   start=False, stop=True),
                          reads=[identf, logk[b]], writes=[PA], inc=(n == 1), acc=True)
                kb.op("act", lambda e: e.activation(out=Kd[b][:], in_=PA[:, :], func=AF.Exp), reads=[PA], writes=[Kd[b]])
                for h in range(8):
                    kb.op("pe", lambda e: e.matmul(PD[:, h * 4:(h + 1) * 4], lhsT=lf[b][:, h * 128:(h + 1) * 128], rhs=MC[:, :],
                                                   start=True, stop=True),
                          reads=[MC, lf[b]], writes=[PD], inc=(h == 7), acc=(h > 0))
                kb.op("act", lambda e: e.activation(out=gc[b][:], in_=PD[:, 0:32].rearrange("p (h c) -> p h c", c=4), func=AF.Exp),
                      reads=[PD], writes=[gc[b]])
                if full:
                    for h in range(8):
                        kb.op("pe", lambda e: e.matmul(PB[:, h * 128:(h + 1) * 128], lhsT=lf[b][:, h * 128:(h + 1) * 128], rhs=M1[:, :],
                                                       start=True, stop=True),
                              reads=[M1, lf[b]], writes=[PB], inc=(h == 7), acc=(h > 0))
                    for h in range(8):
                        kb.op("pe", lambda e: e.matmul(PC[:, h * 128:(h + 1) * 128], lhsT=lf[b][:, h * 128:(h + 1) * 128], rhs=M1n[:, :],
                                                       start=True, stop=False),
                              reads=[M1n, lf[b]], writes=[PC], inc=False, acc=(h > 0))
                        kb.op("pe", lambda e: e.matmul(PC[:, h * 128:(h + 1) * 128], lhsT=logk[b][:, h * 128:(h + 1) * 128], rhs=identf[:, :],
                                                       start=False, stop=True),
                              reads=[identf, logk[b]], writes=[PC], inc=(h == 7), acc=True)
                    kb.op("act", lambda e: e.activation(out=eA[:], in_=PB[:, :].rearrange("p (h t) -> p h t", t=128), func=AF.Exp),
                          reads=[PB], writes=[eA])
                    kb.op("dve", lambda e: e.tensor_tensor(out=AT[b][:], in0=eA[:], in1=qs[:, :, i * 128:(i + 1) * 128], op=ALU.mult),
                          reads=[eA, qs], writes=[AT[b]])
                    pcv = PC[:, :].rearrange("p (h t) -> p h t", t=128)
                    for c in range(2):
                        kb.op("act", lambda e: e.activation(out=BT[b][:, :, c, c * 64:(c + 1) * 64], in_=pcv[:, :, c * 64:(c + 1) * 64],
                                                            func=AF.Exp),
                              reads=[PC], writes=[BT[b]])
                    for h in range(8):
                        for c in range(2):
                            kb.op("pe", lambda e: e.matmul(PA[:, h * 128 + c * 64:h * 128 + (c + 1) * 64], lhsT=BT[b][:, h, c, :],
                                                           rhs=AT[b][:, h, c * 64:(c + 1) * 64], start=True, stop=True),
                                  reads=[BT[b], AT[b]], writes=[PA], inc=(h == 7 and c == 1), acc=(h > 0 or c > 0))
                    kb.op("dve", lambda e: e.copy_predicated(out=scm[b][:], mask=MK8[:].bitcast(mybir.dt.uint32),
                                                             data=PA[:, :].rearrange("p (h t) -> p h t", t=128)),
                          reads=[PA, MK8, scm[b]], writes=[scm[b]])
                    for h in range(8):
                        kb.op("pe", lambda e: e.matmul(PB[:, h * 128:(h + 1) * 128], lhsT=vt[b][:, h * 128:(h + 1) * 128], rhs=scm[b][:, h, :],
                                                       start=(h % 4 == 0), stop=False, skip_group_check=True),
                              reads=[vt[b], scm[b]], writes=[PB], inc=(h == 7), acc=(h > 0))
                for hf in ((0, 1) if d == 0 else (1, 0)):
                    p0 = hf * 64
                    if full:
                        kb.op("dve", lambda e: e.tensor_tensor(out=Sbf[:], in0=S[:],
                                                               in1=gc[b][:, :, hf:hf + 1].broadcast_to([128, 8, 128]), op=ALU.mult),
                              reads=[S, gc[b]], writes=[Sbf])
                        for h in range(8):
                            kb.op("pe", lambda e: e.matmul(PB[:, h * 128 + p0:h * 128 + p0 + 64], lhsT=Sbf[:, h, :],
                                                           rhs=AT[b][:, h, p0:p0 + 64], start=False, stop=True, skip_group_check=True),
                                  reads=[Sbf, AT[b]], writes=[PB], inc=(h == 7), acc=True)
                    for h in range(8):
                        kb.op("pe", lambda e: e.matmul(PC[:, h * 128:(h + 1) * 128], lhsT=Kd[b][p0:p0 + 64, h * 128:(h + 1) * 128],
                                                       rhs=vt[b][p0:p0 + 64, h * 128:(h + 1) * 128], start=True, stop=True),
                              reads=[Kd[b], vt[b]], writes=[PC], inc=(h == 7), acc=(h > 0))
                    kb.op("dve", lambda e: e.tensor_tensor(out=stmp[:], in0=S[:],
                                                           in1=gc[b][:, :, 2 + hf:3 + hf].broadcast_to([128, 8, 128]), op=ALU.mult),
                          reads=[S, gc[b]], writes=[stmp])
                    kb.op("dve", lambda e: e.tensor_tensor(out=S[:], in0=stmp[:], in1=PC[:, :].rearrange("p (h t) -> p h t", t=128), op=ALU.add),
                          reads=[stmp, PC], writes=[S])
                    if not full and n_ >= 2:
                        kb.op("pool", lambda e: e.tensor_tensor(out=Dcum[:], in0=Dcum[:], in1=gc[b][:, :, 2 + hf], op=ALU.mult),
                              reads=[Dcum, gc[b]], writes=[Dcum])
                if dbg is not None and d == 0 and n_ == 0:
                    for nm, tt in (("lf", lf[b]), ("logk", logk[b]), ("Kd", Kd[b]), ("gc", gc[b]), ("S", S), ("kt", kt)) + \
                            ((("AT", AT[b]), ("scm", scm[b]), ("eA", eA)) if full else ()):
                        kb.dma("sp", dbg[nm], tt[:], tt, reads=[tt])
                if full:
                    pov = PB[:, :].rearrange("p (h t) -> p h t", t=128)
                    if d == 0:
                        kb.op("act", lambda e: e.copy(out=oacc[:, :, i * 128:(i + 1) * 128], in_=pov), reads=[PB], writes=[oacc])
                    else:
                        kb.op("dve", lambda e: e.tensor_tensor(out=oacc[:, :, i * 128:(i + 1) * 128], in0=oacc[:, :, i * 128:(i + 1) * 128],
                                                               in1=pov, op=ALU.add),
                              reads=[PB, oacc], writes=[oacc])
            if dbg is not None and full:
                kb.dma("sp", dbg["oacc%d" % d], oacc[:], oacc, reads=[oacc])
                kb.dma("sp", dbg["Send%d" % d], S[:], S, reads=[S])
            if not full:
                kb.dma("sp", XS[d], S[:], S, reads=[S])
                kb.dma("sp", XD[d], Dcum[:], Dcum, reads=[Dcum])
        if full:
            ones_bf = load_const(kb, st, io, "ones_bf", [128, 128], BF16)
            sq = st.tile("hsq", [128, 512], BF16)
            sd = st.tile("hsd", [128, 512], F32)
            rstd = st.tile("hrstd", [128, 512], F32)
            gt = [st.tile("hgt%d" % q, [128, TL], BF16) for q in range(2)]
            sgt = [st.tile("hsgt%d" % q, [128, TL], F32) for q in range(2)]
            yo = [st.tile("hyo%d" % q, [128, TL], BF16) for q in range(2)]
            tmp = st.tile("htmp", [128, 512], F32)
            t0o = 0 if ctx_out else NCTX
            for h in range(8):
                b = h % 2
                kb.dma("sp", gt[b][:], P["bg"][h * 128:(h + 1) * 128, :], gt[b], writes=[gt[b]])
                kb.op("act", lambda e: e.activation(out=sgt[b][:], in_=gt[b][:], func=AF.Silu), reads=[gt[b]], writes=[sgt[b]])
                for (t0, tn) in TOKBLK:
                    kb.op("act", lambda e: e.activation(out=sq[:, 0:tn], in_=oacc[:, h, t0:t0 + tn], func=AF.Square), reads=[oacc], writes=[sq])
                    kb.op("pe", lambda e: e.matmul(PA[:, 0:tn], lhsT=ones_bf[:, :], rhs=sq[:, 0:tn], start=True, stop=True),
                          reads=[ones_bf, sq], writes=[PA])
                    rstd_from_psum(kb, PA, sd, rstd, tn, 1.0 / 128)
                    kb.op("dve", lambda e: e.scalar_tensor_tensor(out=tmp[:, 0:tn], in0=oacc[:, h, t0:t0 + tn], scalar=cols[:, 16 + h:17 + h],
                                                                  in1=rstd[:, 0:tn], op0=ALU.mult, op1=ALU.mult),
                          reads=[oacc, rstd], writes=[tmp])
                    kb.op("pool", lambda e: e.tensor_tensor(out=yo[b][:, t0:t0 + tn], in0=tmp[:, 0:tn], in1=sgt[b][:, t0:t0 + tn], op=ALU.mult),
                          reads=[tmp, sgt[b]], writes=[yo[b]])
                kb.dma("sp", Y[1024 + h * 128:1024 + (h + 1) * 128, t0o:TL], yo[b][:, t0o:TL], yo[b], reads=[yo[b]])
        kb.barrier()


P_SHAPES = {name: ([w, TL] if lay == "F" else [TL, w], dt) for name, w, lay, dt in GROUPS}
XCH = {"K": (256, BF16), "C": (512, BF16), "V": (384, BF16), "S0": (128, F32), "S1": (130, F32)}
RV_V, RV_R, RV_HU, RV_HG = 0, 256, 320, 352
GROUPS4 = [[0, 1, 2, 3], [4, 5, 6, 7]]


class GFused:
    def __init__(self, P, X, GT):
        self.P, self.X, self.GT = P, X, GT

    def rk(self, name, r, r0, n):
        rows = XCH[name][0]
        return self.GT[name][r * rows + r0:r * rows + r0 + n, :]

    def load_kT(self, kb, kT, g):
        kb.dma("sp", kT[:, 0:NCTX], self.X["XK"][g][:, 0:NCTX], kT, writes=[kT])
        for r in range(4):
            kb.dma("sp", kT[:, NCTX + r * NLAT:NCTX + (r + 1) * NLAT], self.rk("K", r, g * 128, 128), kT, writes=[kT])

    def load_V(self, kb, V, g):
        kb.dma("sp", V[:, 0:2, :], self.P["av"][0:NCTX, g * 128:(g + 1) * 128].rearrange("(t p) c -> p t c", p=128), V, writes=[V])
        for r in range(4):
            src = self.rk("V", r, RV_V, 256).rearrange("a (t c) -> (a t) c", c=256)
            kb.dma("sp", V[:, 2 + r * 8:2 + (r + 1) * 8, :], src[:, g * 128:(g + 1) * 128].rearrange("(t p) c -> p t c", p=128),
                   V, writes=[V])

    def load_ckv(self, kb, ckv):
        kb.dma("sp", ckv[:, :, 0:NCTX], self.X["XC"][:, 0:NCTX].rearrange("(c p) t -> p c t", p=128), ckv, writes=[ckv])
        for r in range(4):
            kb.dma("sp", ckv[:, :, NCTX + r * NLAT:NCTX + (r + 1) * NLAT],
                   self.rk("C", r, 0, 512).rearrange("(c p) t -> p c t", p=128), ckv, writes=[ckv])

    def load_kr(self, kb, kr):
        kb.dma("sp", kr[:, 0:NCTX], self.P["dkr"][:, 0:NCTX], kr, writes=[kr])
        for r in range(4):
            kb.dma("sp", kr[:, NCTX + r * NLAT:NCTX + (r + 1) * NLAT], self.rk("V", r, RV_R, 64), kr, writes=[kr])

    def load_halo(self, kb, hu, hg, r0):
        for r in range(4):
            for (t, ro) in ((hu, RV_HU), (hg, RV_HG)):
                src = self.rk("V", r, ro, 32).rearrange("a (c k) -> (a c) k", k=32)
                kb.dma("sp", t[:, r, :], src[r0:r0 + 128, :], t, writes=[t])

    def hS(self, ci, d):
        src = self.rk("S0", ci, 0, 128) if d == 0 else self.rk("S1", ci, 0, 128)
        return src.rearrange("p (h e) -> p h e", e=128)

    def hD(self, ci, d):
        return self.rk("S1", ci, 128 + d, 1).rearrange("o (p h) -> (o p) h", h=8)


def stage_exchange(kb, P, X, ST, GT):
    t = kb.trk("pack")
    for g in range(2):
        kb.dma("sp", ST["K"][g * 128:(g + 1) * 128, :], X["XK"][g][:, NCTX:TL], t)
    kb.dma("sp", ST["C"][:, :], X["XC"][:, NCTX:TL], t)
    kb.dma("sp", ST["V"][RV_R:RV_R + 64, :], P["dkr"][:, NCTX:TL], t)
    kb.dma("sp", ST["V"][RV_V:RV_V + 256, :].rearrange("a (t c) -> (a t) c", c=256), P["av"][NCTX:TL, :], t)
    for (nm, ro) in (("cu", RV_HU), ("cglu", RV_HG)):
        dst = ST["V"][ro:ro + 32, :].rearrange("a (c k) -> (a c) k", k=32)
        kb.dma("sp", dst[:, 0:15], P[nm][:, NCTX:NCTX + 15], t)
        kb.dma("sp", dst[:, 16:31], P[nm][:, TL - 15:TL], t)
    kb.dma("sp", ST["S0"][:, :].rearrange("p (h e) -> p h e", e=128), X["XS"][0], t)
    kb.dma("sp", ST["S1"][0:128, :].rearrange("p (h e) -> p h e", e=128), X["XS"][1], t)
    for d in range(2):
        kb.dma("sp", ST["S1"][128 + d:129 + d, :].rearrange("o (p h) -> (o p) h", h=8), X["XD"][d], t)
    kb.barrier()
    for n in XCH:
        kb.collective("AllGather", [ST[n]], [GT[n]], GROUPS4)
    kb.barrier()


CONST_SPECS = {
    "ident": ([128, 128], BF16), "ones_bf": ([128, 128], BF16), "rmA": ([128, 128], BF16), "rmD": ([64, 64], BF16),
    "cosA": ([128, TL], F32), "sinA": ([128, TL], F32), "cosDq": ([64, TL], F32), "sinDq": ([64, TL], F32),
    "cosDk": ([64, NKEY], F32), "sinDk": ([64, NKEY], F32), "cols": ([2, 128, NCOLS], F32),
    "hM1": ([2, 128, 128], F32), "hM1n": ([2, 128, 128], F32), "hSC": ([2, 128, 128], F32), "hMK": ([2, 128, 128], F32),
    "hMC": ([2, 128, 4], F32), "identf": ([128, 128], F32), "hmask": ([128, 2, 4], F32), "halomask": ([128, 2, 4], F32),
}
X_SPECS = {"XK": ([2, 128, TL], BF16), "XC": ([512, TL], BF16), "XS": ([2, 128, 8, 128], F32), "XD": ([2, 128, 8], F32),
           "MOD": ([2, 128, D], F32)}
W_SPECS = {"w_mod": [2, D, 3 * D], "b_mod": [2, 3 * D], "w_in": [2, D, IN_W], "w_out": [2, D, D],
           "mla_w_uq": [2, 768, 1536], "mla_w_ukv": [2, 512, 2048], "hgrn_lb_logits": [2, 2, 1024],
           "normwT": [2, 128, KC], "cT": [128, KC, 2]}


def build_fused(n_layers=2):
    WL[0] = 0
    nc = bass.Bass("TRN2", target_bir_lowering=False)
    io = {}

    def inp(name, shape, dt=F32):
        io[name] = nc.dram_tensor(name, list(shape), dt, kind="ExternalInput").ap()

    def scratch(name, shape, dt):
        return nc.dram_tensor(name, list(shape), dt, kind="Internal").ap()

    inp("x_loc", [NLAT, D])
    inp("ctx_b", [NCTX, D])
    for n, sh in W_SPECS.items():
        inp(n, sh)
    for n, (sh, dt) in CONST_SPECS.items():
        inp(n, sh, dt)
    out = nc.dram_tensor("out", [NLAT, D], F32, kind="ExternalOutput").ap()
    P = {n: scratch("P_" + n, sh, dt) for n, (sh, dt) in P_SHAPES.items()}
    X = {n: scratch(n, sh, dt) for n, (sh, dt) in X_SPECS.items()}
    Y = scratch("Y", [4096, TL], BF16)
    x1 = scratch("x1", [NLAT, D], F32)
    ctx1 = scratch("ctx1", [NCTX, D], F32)
    ST = {n: scratch("ST_" + n, [r, 1024], dt) for n, (r, dt) in XCH.items()}
    GT = {n: scratch("GT_" + n, [4 * r, 1024], dt) for n, (r, dt) in XCH.items()}
    G = GFused(P, X, GT)
    kb = KB(nc)
    for l in range(n_layers):
        ctx_out = (l == 0)
        x_src = io["x_loc"] if l == 0 else x1
        ctx_src = io["ctx_b"] if l == 0 else ctx1
        last = (l == n_layers - 1)
        with Stage(kb) as keep:
            modcol, gate_bc = stage_mod(kb, io, l, keep)
            t = kb.trk()
            kb.dma("sp", X["MOD"][0], gate_bc[0][:], t, reads=[t])
            kb.dma("sp", X["MOD"][1], gate_bc[1][:], t, reads=[t])
            hT = stage_norm(kb, io, l, keep, modcol, x_src, ctx_src)
            stage_inproj(kb, io, l, hT, P)
            kb.barrier()
        stage_x_gqa(kb, io, l, P, X["XK"])
        stage_x_mla(kb, io, l, P, X["XC"])
        stage_hgrn(kb, io, l, P, None, None, X["XS"], X["XD"], 1, True)
        stage_exchange(kb, P, X, ST, GT)
        stage_attn_A(kb, io, l, P, G, Y, ctx_out)
        stage_hgrn(kb, io, l, P, G, Y, None, None, 2, ctx_out)
        stage_conv(kb, io, l, P, G, Y, ctx_out)
        stage_attn_D(kb, io, l, P, G, Y, ctx_out)
        stage_outproj(kb, io, l, Y, X["MOD"], x_src, ctx_src, out if last else x1, ctx1, ctx_out)
    kb.es.close()
    return nc


BF = ml_dtypes.bfloat16
NCORE = 8


def host_consts(inp):
    c = {}
    c["ident"] = np.eye(128, dtype=np.float32).astype(BF)
    c["ones_bf"] = np.ones((128, 128), np.float32).astype(BF)
    c["rmA"] = rope_rot_matrix(128).astype(BF)
    c["rmD"] = rope_rot_matrix(64).astype(BF)
    c["cosDk"], c["sinDk"] = rope_tables(64, key_tpos())
    c["cols"] = np.stack([make_cols(inp, 0), make_cols(inp, 1)])
    c.update(hgrn_consts())
    per_core = []
    for core in range(NCORE):
        j = core % 4
        d = {}
        d["cosA"], d["sinA"] = rope_tables(128, local_tpos(j))
        d["cosDq"], d["sinDq"] = rope_tables(64, local_tpos(j))
        hm = np.zeros((128, 2, 4), np.float32)
        ha = np.zeros((128, 2, 4), np.float32)
        for i in range(4):
            hm[:, 0, i] = 1.0 if i < j else 0.0
            hm[:, 1, i] = 1.0 if i > j else 0.0
            ha[:, 0, i] = 1.0 if i == j - 1 else 0.0
            ha[:, 1, i] = 1.0 if i == j + 1 else 0.0
        d["hmask"] = hm
        d["halomask"] = ha
        per_core.append(d)
    return c, per_core


def make_in_maps(inp):
    consts, pc = host_consts(inp)
    normwT = np.ascontiguousarray(inp["norm_w"].reshape(2, KC, 128).transpose(0, 2, 1))
    maps = []
    for c in range(NCORE):
        b, j = c // 4, c % 4
        m = {"x_loc": np.ascontiguousarray(inp["x"][b, j * NLAT:(j + 1) * NLAT]),
             "ctx_b": np.ascontiguousarray(inp["ctx"][b]),
             "normwT": normwT,
             "cT": np.ascontiguousarray(np.stack([inp["c"][b], inp["c_ctx"]], axis=-1).reshape(KC, 128, 2).transpose(1, 0, 2))}
        for n in ("w_mod", "b_mod", "w_in", "w_out", "mla_w_uq", "mla_w_ukv", "hgrn_lb_logits"):
            m[n] = inp[n]
        for n in CONST_SPECS:
            m[n] = pc[c][n] if n in pc[c] else consts[n]
        maps.append(m)
    return maps


def kernel(**inp):
    inp = {k: np.asarray(v) for k, v in inp.items()}
    nc = build_fused()
    res = run_bass_kernel_spmd(nc, make_in_maps(inp), core_ids=list(range(NCORE))).results
    out = np.zeros((2, SEQ, D), np.float32)
    for c in range(NCORE):
        out[c // 4, (c % 4) * NLAT:(c % 4 + 1) * NLAT] = np.asarray(res[c]["out"])
    return out
```

```python
import contextlib
import os
import numpy as np
import ml_dtypes
import concourse.bass as bass
import concourse.mybir as mybir
from concourse.bass_utils import run_bass_kernel_spmd

F32 = mybir.dt.float32
BF16 = mybir.dt.bfloat16
AF = mybir.ActivationFunctionType
ALU = mybir.AluOpType
AX = mybir.AxisListType

D = 4096
KC = 32
NCTX = 256
NLAT = 1024
TL = NCTX + NLAT
NT = TL // 128
SEQ = 4096
NKEY = NCTX + SEQ
EPS = 1e-6
IN_W = 13120

GROUPS = [
    ("aq", 1024, "F", BF16), ("ak", 256, "F", BF16), ("av", 256, "T", BF16), ("ag", 1024, "F", BF16),
    ("bq", 1024, "F", BF16), ("bi", 1024, "T", BF16), ("bff", 1024, "T", F32), ("bfb", 1024, "T", F32),
    ("bg", 1024, "F", BF16),
    ("cu", 1024, "F", BF16), ("cglu", 1024, "F", BF16), ("cg", 1024, "F", BF16),
    ("dcq", 768, "F", BF16), ("dckv", 512, "F", BF16), ("dkr", 64, "F", BF16), ("dg", 1024, "F", BF16),
]
GOFF = {}
_o = 0
for _n, _w, _l, _d in GROUPS:
    GOFF[_n] = (_o, _w, _l, _d)
    _o += _w
assert _o == IN_W


class Trk:
    __slots__ = ("name", "lw", "rd", "dsem", "psum")

    def __init__(self, name=""):
        self.name = name
        self.psum = False
        self.lw = None
        self.rd = {}
        self.dsem = None


class Eng:
    def __init__(self, name, eng, sem):
        self.name = name
        self.eng = eng
        self.sem = sem
        self.cnt = 0
        self.pending = False
        self.seen = {}


class KB:
    def __init__(self, nc, n_dma_sems=80):
        self.nc = nc
        self.es = contextlib.ExitStack()
        self.sems = {}
        self.engs = {}
        for name, eng in (("pe", nc.tensor), ("act", nc.scalar), ("dve", nc.vector),
                          ("pool", nc.gpsimd), ("sp", nc.sync)):
            h = self.es.enter_context(nc.semaphore("s_" + name))
            self.sems[name] = h
            self.engs[name] = Eng(name, eng, name)
        self.bar_sem = self.es.enter_context(nc.semaphore("s_bar"))
        self.bar_cnt = 0
        self.cc_sem = self.es.enter_context(nc.semaphore("s_cc"))
        self.cc_cnt = 0
        self.dma_free = []
        self.dma_tot = {}
        for i in range(n_dma_sems):
            k = "d%d" % i
            self.sems[k] = self.es.enter_context(nc.semaphore("s_" + k))
            self.dma_free.append(k)
            self.dma_tot[k] = 0
        self.dma_used = []
        self.trks = []

    def trk(self, name=""):
        t = Trk(name)
        self.trks.append(t)
        return t

    def _wait(self, e, semkey, val):
        if e.seen.get(semkey, 0) >= val:
            return
        e.eng.wait_ge(self.sems[semkey], val)
        e.seen[semkey] = val

    def _deps(self, e, reads, writes, acc):
        deps = {}
        reads = [getattr(t, "k", t) for t in reads]
        writes = [getattr(t, "k", t) for t in writes]

        def add(tok):
            if tok is None:
                return
            k, v = tok
            if deps.get(k, 0) < v:
                deps[k] = v

        for t in reads:
            add(t.lw)
            if t.psum:
                for k, v in t.rd.items():
                    if k != e.sem:
                        add((k, v))
        for t in writes:
            if not (acc and t.lw is not None and t.lw[0] == e.sem):
                add(t.lw)
            for k, v in t.rd.items():
                add((k, v))
        for k, v in deps.items():
            if k in self.dma_tot:
                v = self.dma_tot[k]
            self._wait(e, k, v)

    def _commit(self, tok, reads, writes):
        k, v = tok
        reads = [getattr(t, "k", t) for t in reads]
        writes = [getattr(t, "k", t) for t in writes]
        for t in reads:
            if t.rd.get(k, 0) < v:
                t.rd[k] = v
        for t in writes:
            t.lw = tok
            t.rd = {}

    def op(self, en, fn, reads=(), writes=(), inc=True, acc=False):
        e = self.engs[en]
        self._deps(e, reads, writes, acc)
        ins = fn(e.eng)
        if inc:
            e.cnt += 1
            ins.then_inc(self.sems[e.sem], 1)
            e.pending = False
            tok = (e.sem, e.cnt)
        else:
            e.pending = True
            tok = (e.sem, e.cnt + 1)
        self._commit(tok, reads, writes)
        return ins

    def dma(self, q, out, in_, sb, reads=(), writes=(), **kw):
        e = self.engs[q]
        sb = getattr(sb, "k", sb)
        if sb.dsem is None:
            sb.dsem = self.dma_free.pop()
            self.dma_used.append(sb)
        self._deps(e, reads, writes, False)
        ins = e.eng.dma_start(out=out, in_=in_, **kw)
        k = sb.dsem
        self.dma_tot[k] += 16
        ins.then_inc(self.sems[k], 16)
        self._commit((k, self.dma_tot[k]), reads, writes)
        return ins

    def collective(self, kind, ins, outs, groups):
        e = self.engs["pool"]
        i = e.eng.collective_compute(kind, ALU.bypass, replica_groups=groups,
                                     ins=[a.opt() for a in ins], outs=[a.opt() for a in outs])
        self.cc_cnt += 1
        i.then_inc(self.cc_sem)
        return i

    def barrier(self):
        sp = self.engs["sp"]
        if self.cc_cnt > 0:
            sp.eng.wait_ge(self.cc_sem, self.cc_cnt)
        for en, e in self.engs.items():
            assert not e.pending, "engine %s has un-inc'ed trailing instruction" % en
            if en != "sp" and e.cnt > 0:
                self._wait(sp, e.sem, e.cnt)
        for k, v in self.dma_tot.items():
            if v > 0:
                self._wait(sp, k, v)
        self.bar_cnt += 1
        sp.eng.sem_inc(self.bar_sem, 1)
        for en, e in self.engs.items():
            e.eng.wait_ge(self.bar_sem, self.bar_cnt)
            for en2, e2 in self.engs.items():
                e.seen[e2.sem] = e2.cnt
            for k, v in self.dma_tot.items():
                e.seen[k] = v
        for t in self.dma_used:
            self.dma_free.append(t.dsem)
            t.dsem = None
        self.dma_used = []
        for t in self.trks:
            t.lw = None
            t.rd = {}
        self.trks = []


class T:
    __slots__ = ("t", "k")

    def __init__(self, t, k):
        self.t = t
        self.k = k

    def __getitem__(self, key):
        return self.t[key]


class Stage:
    def __init__(self, kb):
        self.kb = kb
        self.es = contextlib.ExitStack()

    def __enter__(self):
        self.es.__enter__()
        return self

    def __exit__(self, *a):
        return self.es.__exit__(*a)

    _uid = [0]

    def sb(self, name, shape, dtype):
        Stage._uid[0] += 1
        t = self.es.enter_context(self.kb.nc.sbuf_tensor("sb%d_%s" % (Stage._uid[0], name), list(shape), dtype))
        return t

    def tile(self, name, shape, dtype):
        return T(self.sb(name, shape, dtype), self.kb.trk(name))

    def ptile(self, name, shape, dtype=F32):
        t = T(self.ps(name, shape, dtype), self.kb.trk(name))
        t.k.psum = True
        return t

    def ps(self, name, shape, dtype=F32):
        Stage._uid[0] += 1
        t = self.es.enter_context(self.kb.nc.psum_tensor("ps%d_%s" % (Stage._uid[0], name), list(shape), dtype))
        return t


def stage_mod(kb, io, l, st_keep):
    nc = kb.nc
    modcol = [st_keep.sb("modcol%d" % t, [128, 64], F32) for t in range(2)]
    gate_bc = [st_keep.sb("gatebc%d" % t, [128, D], F32) for t in range(2)]
    t_modcol = [kb.trk("modcol") for _ in range(2)]
    t_gate = [kb.trk("gatebc") for _ in range(2)]
    with Stage(kb) as st:
        cT = st.sb("cT", [128, KC, 2], F32)
        scT = st.sb("scT", [128, KC, 2], BF16)
        ones_row = st.sb("ones_row", [1, 128], F32)
        nwcol = st.sb("nwcol", [128, KC], F32)
        brow = [st.sb("brow%d" % i, [1, 512], F32) for i in range(2)]
        row = [st.sb("row%d" % i, [1, 512], F32) for i in range(4)]
        wm = [st.sb("wm%d" % i, [128, KC, 512], BF16) for i in range(2)]
        pacc = [st.ps("pacc%d" % i, [1, 512]) for i in range(2)]
        pcol = st.ps("pcol", [128, 512])
        pbc = [st.ps("pbc%d" % i, [128, 512]) for i in range(2)]
        t_cT, t_scT, t_ones, t_nw = kb.trk(), kb.trk(), kb.trk(), kb.trk()
        t_brow = [kb.trk() for _ in range(2)]
        t_row = [kb.trk() for _ in range(4)]
        t_wm = [kb.trk() for _ in range(2)]
        t_pacc = [kb.trk() for _ in range(2)]
        t_pcol = kb.trk()
        t_pbc = [kb.trk() for _ in range(2)]
        for _t in t_pacc + [t_pcol] + t_pbc:
            _t.psum = True

        kb.dma("sp", cT[:], io["cT"], t_cT, writes=[t_cT])
        kb.dma("sp", nwcol[:], io["normwT"][l], t_nw, writes=[t_nw])
        kb.op("act", lambda e: e.activation(out=scT[:], in_=cT[:], func=AF.Silu),
              reads=[t_cT], writes=[t_scT])
        kb.op("dve", lambda e: e.memset(ones_row[:], 1.0), writes=[t_ones])
        ri = 0
        for j in range(24):
            wb, twb = wm[j % 2], t_wm[j % 2]
            wsrc = io["w_mod"][l - WL[0], :, j * 512:(j + 1) * 512].rearrange("(kc p) c -> p kc c", p=128)
            for kq in range(4):
                kb.dma("pool", wb[:, kq * 8:(kq + 1) * 8, :], wsrc[:, kq * 8:(kq + 1) * 8, :], twb, writes=[twb])
            bb, tbb = brow[j % 2], t_brow[j % 2]
            kb.dma("sp", bb[:], io["b_mod"][l - WL[0]:l - WL[0] + 1, j * 512:(j + 1) * 512], tbb, writes=[tbb])
            for t in range(2):
                pa, tpa = pacc[t], t_pacc[t]
                for kc in range(KC):
                    kb.op("pe", lambda e, kc=kc: e.matmul(pa[:], lhsT=scT[:, kc, t:t + 1], rhs=wb[:, kc, :],
                                                         start=(kc == 0), stop=(kc == KC - 1)),
                          reads=[t_scT, twb], writes=[tpa], inc=(kc == KC - 1), acc=(kc > 0))
                rw, trw = row[ri % 4], t_row[ri % 4]
                ri += 1
                kb.op("dve", lambda e: e.tensor_tensor(out=rw[:], in0=pa[:], in1=bb[:], op=ALU.add),
                      reads=[tpa, tbb], writes=[trw])
                if j < 16:
                    for s in range(4):
                        kb.op("pe", lambda e, s=s: e.matmul(pcol[:, s:s + 1], lhsT=rw[0:1, s * 128:(s + 1) * 128],
                                                           rhs=ones_row[0:1, 0:1], start=True, stop=True),
                              reads=[trw, t_ones], writes=[t_pcol], inc=(s == 3), acc=(s > 0))
                    c0 = j * 4
                    if j < 8:
                        kb.op("dve", lambda e: e.tensor_copy(out=modcol[t][:, c0:c0 + 4], in_=pcol[:, 0:4]),
                              reads=[t_pcol], writes=[t_modcol[t]])
                    else:
                        kb.op("dve", lambda e: e.scalar_tensor_tensor(
                            out=modcol[t][:, c0:c0 + 4], in0=pcol[:, 0:4], scalar=1.0,
                            in1=nwcol[:, c0 - 32:c0 - 32 + 4], op0=ALU.add, op1=ALU.mult),
                            reads=[t_pcol, t_nw], writes=[t_modcol[t]])
                else:
                    pb, tpb = pbc[t], t_pbc[t]
                    kb.op("pe", lambda e: e.matmul(pb[:], lhsT=ones_row[0:1, :], rhs=rw[0:1, :],
                                                   start=True, stop=True),
                          reads=[trw, t_ones], writes=[tpb])
                    g0 = (j - 16) * 512
                    kb.op("act", lambda e: e.copy(out=gate_bc[t][:, g0:g0 + 512], in_=pb[:]),
                          reads=[tpb], writes=[t_gate[t]])
        kb.barrier()
    return modcol, gate_bc


def stage_norm(kb, io, l, st_keep, modcol, x_src, ctx_src):
    nc = kb.nc
    hT = st_keep.sb("hT", [128, KC, TL], BF16)
    t_hT = kb.trk("hT")
    with Stage(kb) as st:
        ident = st.sb("ident", [128, 128], BF16)
        xt = [st.sb("xt%d" % i, [128, D], F32) for i in range(2)]
        xn = [st.sb("xn%d" % i, [128, D], BF16) for i in range(2)]
        junk = st.sb("junk", [128, D], BF16)
        ss = [st.sb("ss%d" % i, [128, 1], F32) for i in range(2)]
        rs = [st.sb("rs%d" % i, [128, 1], F32) for i in range(2)]
        tmp = [st.sb("tmp%d" % i, [128, 8, 128], F32) for i in range(2)]
        ptr = [st.ps("ptr%d" % i, [128, 8, 128], BF16) for i in range(4)]
        t_id = kb.trk()
        t_xt = [kb.trk() for _ in range(2)]
        t_xn = [kb.trk() for _ in range(2)]
        t_junk = kb.trk()
        t_ss = [kb.trk() for _ in range(2)]
        t_rs = [kb.trk() for _ in range(2)]
        t_tmp = [kb.trk() for _ in range(2)]
        t_ptr = [kb.trk() for _ in range(4)]
        for _t in t_ptr:
            _t.psum = True
        kb.dma("sp", ident[:], io["ident"], t_id, writes=[t_id])
        pi = 0
        for i in range(NT):
            b = i % 2
            t = 1 if i < 2 else 0
            src = ctx_src[i * 128:(i + 1) * 128, :] if i < 2 else x_src[(i - 2) * 128:(i - 1) * 128, :]
            kb.dma("sp", xt[b][:], src, t_xt[b], writes=[t_xt[b]])
            kb.op("act", lambda e: e.activation(out=junk[:], in_=xt[b][:], func=AF.Square, accum_out=ss[b][:]),
                  reads=[t_xt[b]], writes=[t_junk, t_ss[b]])
            kb.op("act", lambda e: e.activation(out=ss[b][:], in_=ss[b][:], func=AF.Sqrt, scale=1.0 / D, bias=EPS),
                  reads=[t_ss[b]], writes=[t_ss[b]])
            kb.op("dve", lambda e: e.reciprocal(out=rs[b][:], in_=ss[b][:]),
                  reads=[t_ss[b]], writes=[t_rs[b]])
            kb.op("act", lambda e: e.activation(out=xn[b][:], in_=xt[b][:], func=AF.Copy, scale=rs[b][:, 0:1]),
                  reads=[t_xt[b], t_rs[b]], writes=[t_xn[b]])
            for g in range(4):
                p, tp = ptr[pi % 4], t_ptr[pi % 4]
                pi += 1
                for q in range(8):
                    kc = g * 8 + q
                    kb.op("pe", lambda e, kc=kc, q=q: e.transpose(p[:, q, :], xn[b][:, kc * 128:(kc + 1) * 128], ident[:]),
                          reads=[t_xn[b], t_id], writes=[tp], inc=(q == 7), acc=(q > 0))
                tm, ttm = tmp[g % 2], t_tmp[g % 2]
                s1 = modcol[t][:, 32 + g * 8:32 + g * 8 + 8].unsqueeze(2).broadcast_to([128, 8, 128])
                sh = modcol[t][:, g * 8:g * 8 + 8].unsqueeze(2).broadcast_to([128, 8, 128])
                kb.op("dve", lambda e: e.tensor_tensor(out=tm[:], in0=p[:], in1=s1, op=ALU.mult),
                      reads=[tp], writes=[ttm])
                kb.op("pool", lambda e: e.tensor_tensor(out=hT[:, g * 8:(g + 1) * 8, i * 128:(i + 1) * 128],
                                                        in0=tm[:], in1=sh, op=ALU.add),
                      reads=[ttm], writes=[t_hT])
        kb.barrier()
    return hT


TOKBLK = [(0, 512), (512, 512), (1024, 256)]
TOKBLK_LAT = [(NCTX, 512), (NCTX + 512, 512)]


def stage_inproj(kb, io, l, hT, P):
    with Stage(kb) as st:
        wb = [st.sb("wb%d" % i, [128, KC, 512], BF16) for i in range(2)]
        t_wb = [kb.trk() for _ in range(2)]
        stF = [st.sb("stF%d" % i, [128, TL], BF16) for i in range(3)]
        t_stF = [kb.trk() for _ in range(3)]
        stT = [st.sb("stT%d" % i, [128, 512], BF16) for i in range(3)]
        stT32 = [st.sb("stT32_%d" % i, [128, 512], F32) for i in range(3)]
        t_stT = [kb.trk() for _ in range(3)]
        pacc = [st.ps("pa%d" % i, [128, 512]) for i in range(4)]
        t_pacc = [kb.trk() for _ in range(4)]
        for _t in t_pacc:
            _t.psum = True
        t_hT = kb.trk()
        wi = 0
        pi = 0
        fi = 0
        ti = 0
        ev = 0
        for name, width, lay, dt in GROUPS:
            g0 = GOFF[name][0]
            for c0 in range(0, width, 512):
                ncol = min(512, width - c0)
                w, tw = wb[wi % 2], t_wb[wi % 2]
                wi += 1
                wsrc = io["w_in"][l - WL[0], :, g0 + c0:g0 + c0 + ncol].rearrange("(kc p) c -> p kc c", p=128)
                for kq in range(4):
                    kb.dma("pool", w[:, kq * 8:(kq + 1) * 8, 0:ncol], wsrc[:, kq * 8:(kq + 1) * 8, :], tw, writes=[tw])
                tokblk = TOKBLK if (l == 0 or name in ("ak", "dckv", "dkr")) else TOKBLK_LAT
                if lay == "F":
                    for s0 in range(0, ncol, 128):
                        ns = min(128, ncol - s0)
                        sf, tsf = stF[fi % 3], t_stF[fi % 3]
                        fi += 1
                        for (t0, tn) in tokblk:
                            pa, tpa = pacc[pi % 4], t_pacc[pi % 4]
                            pi += 1
                            for kc in range(KC):
                                kb.op("pe", lambda e, kc=kc: e.matmul(pa[0:ns, 0:tn], lhsT=w[:, kc, s0:s0 + ns],
                                                                     rhs=hT[:, kc, t0:t0 + tn],
                                                                     start=(kc == 0), stop=(kc == KC - 1)),
                                      reads=[tw, t_hT], writes=[tpa], inc=(kc == KC - 1), acc=(kc > 0))
                            en = "act" if ev % 2 == 0 else "dve"
                            ev += 1
                            if en == "act":
                                kb.op("act", lambda e: e.copy(out=sf[0:ns, t0:t0 + tn], in_=pa[0:ns, 0:tn]),
                                      reads=[tpa], writes=[tsf])
                            else:
                                kb.op("dve", lambda e: e.tensor_copy(out=sf[0:ns, t0:t0 + tn], in_=pa[0:ns, 0:tn]),
                                      reads=[tpa], writes=[tsf])
                        tb0 = tokblk[0][0]
                        kb.dma("sp", P[name][c0 + s0:c0 + s0 + ns, tb0:TL], sf[0:ns, tb0:TL], tsf, reads=[tsf])
                else:
                    for i in range(NT):
                        pa, tpa = pacc[pi % 4], t_pacc[pi % 4]
                        pi += 1
                        for kc in range(KC):
                            kb.op("pe", lambda e, kc=kc: e.matmul(pa[:, 0:ncol], lhsT=hT[:, kc, i * 128:(i + 1) * 128],
                                                                 rhs=w[:, kc, 0:ncol],
                                                                 start=(kc == 0), stop=(kc == KC - 1)),
                                  reads=[tw, t_hT], writes=[tpa], inc=(kc == KC - 1), acc=(kc > 0))
                        stt = (stT32 if dt == F32 else stT)[ti % 3]
                        tst = t_stT[ti % 3]
                        ti += 1
                        en = "act" if ev % 2 == 0 else "dve"
                        ev += 1
                        if en == "act":
                            kb.op("act", lambda e: e.copy(out=stt[:, 0:ncol], in_=pa[:, 0:ncol]),
                                  reads=[tpa], writes=[tst])
                        else:
                            kb.op("dve", lambda e: e.tensor_copy(out=stt[:, 0:ncol], in_=pa[:, 0:ncol]),
                                  reads=[tpa], writes=[tst])
                        kb.dma("sp", P[name][i * 128:(i + 1) * 128, c0:c0 + ncol], stt[:, 0:ncol], tst, reads=[tst])
        kb.barrier()


def attn_core(kb, bufs, q_tiles, k_tiles, V, sg, ysb, blocks, scale):
    psS, pO, pD, E, ones_bf, rden, tmpo = bufs
    nq = len(q_tiles)
    work = [(q0, qn, kt, ii == 0, ii == len(ktl) - 1) for (q0, qn, ktl) in blocks for ii, kt in enumerate(ktl)]

    def emit_S(idx):
        q0, qn, kt, _, _ = work[idx]
        pS = psS[idx % 2]
        for qi, ((qt, kp), (ktile, kp2)) in enumerate(zip(q_tiles, k_tiles)):
            kb.op("pe", lambda e: e.matmul(pS[:, 0:qn], lhsT=ktile[0:kp, kt * 128:(kt + 1) * 128],
                                           rhs=qt[0:kp, q0:q0 + qn], start=(qi == 0), stop=(qi == nq - 1)),
                  reads=[ktile, qt], writes=[pS], inc=(qi == nq - 1), acc=(qi > 0))

    emit_S(0)
    for idx, (q0, qn, kt, first, last) in enumerate(work):
        if idx + 1 < len(work):
            emit_S(idx + 1)
        pS = psS[idx % 2]
        Eb = E[idx % 3]
        kb.op("act", lambda e: e.activation(out=Eb[:, 0:qn], in_=pS[:, 0:qn], func=AF.Exp, scale=scale),
              reads=[pS], writes=[Eb])
        kb.op("pe", lambda e: e.matmul(pO[:, 0:qn], lhsT=V[:, kt, :], rhs=Eb[:, 0:qn], start=first, stop=last),
              reads=[V, Eb], writes=[pO], inc=False, acc=(not first))
        kb.op("pe", lambda e: e.matmul(pD[:, 0:qn], lhsT=ones_bf[:, :], rhs=Eb[:, 0:qn], start=first, stop=last),
              reads=[ones_bf, Eb], writes=[pD], inc=True, acc=(not first))
        if last:
            kb.op("dve", lambda e: e.reciprocal(out=rden[:, 0:qn], in_=pD[:, 0:qn]), reads=[pD], writes=[rden])
            kb.op("dve", lambda e: e.tensor_tensor(out=tmpo[:, 0:qn], in0=pO[:, 0:qn], in1=rden[:, 0:qn], op=ALU.mult),
                  reads=[pO, rden], writes=[tmpo])
            kb.op("pool", lambda e: e.tensor_tensor(out=ysb[:, q0:q0 + qn], in0=tmpo[:, 0:qn], in1=sg[:, q0:q0 + qn],
                                                    op=ALU.mult),
                  reads=[tmpo, sg], writes=[ysb])


def attn_bufs(kb, st, ones_bf):
    psS = [st.ptile("psS%d" % i, [128, 512]) for i in range(2)]
    pO = st.ptile("pO", [128, 512])
    pD = st.ptile("pD", [128, 512])
    E = [st.tile("E%d" % i, [128, 512], BF16) for i in range(3)]
    rden = st.tile("rden", [128, 512], F32)
    tmpo = st.tile("tmpo", [128, 512], F32)
    return (psS, pO, pD, E, ones_bf, rden, tmpo)


def rstd_from_psum(kb, pss, sd, rstd, n, inv_dim, rows=128):
    kb.op("act", lambda e: e.activation(out=sd[0:rows, 0:n], in_=pss[0:rows, 0:n], func=AF.Sqrt, scale=inv_dim, bias=EPS),
          reads=[pss], writes=[sd])
    kb.op("dve", lambda e: e.reciprocal(out=rstd[0:rows, 0:n], in_=sd[0:rows, 0:n]), reads=[sd], writes=[rstd])


def rope_apply(kb, xn, rm, cos, sin, c0, out, o0, n, kp, prot, t1, t2):
    kb.op("pe", lambda e: e.matmul(prot[0:kp, 0:n], lhsT=rm[0:kp, 0:kp], rhs=xn[0:kp, 0:n], start=True, stop=True),
          reads=[rm, xn], writes=[prot])
    kb.op("pool", lambda e: e.tensor_tensor(out=t1[0:kp, 0:n], in0=xn[0:kp, 0:n], in1=cos[0:kp, c0:c0 + n], op=ALU.mult),
          reads=[xn, cos], writes=[t1])
    kb.op("dve", lambda e: e.tensor_tensor(out=t2[0:kp, 0:n], in0=prot[0:kp, 0:n], in1=sin[0:kp, c0:c0 + n], op=ALU.mult),
          reads=[prot, sin], writes=[t2])
    kb.op("pool", lambda e: e.tensor_tensor(out=out[0:kp, o0:o0 + n], in0=t1[0:kp, 0:n], in1=t2[0:kp, 0:n], op=ALU.add),
          reads=[t1, t2], writes=[out])


def load_const(kb, st, io, name, shape, dtype, src=None):
    t = st.tile("c_" + name, shape, dtype)
    kb.dma("sp", t[:], io[name] if src is None else src, t, writes=[t])
    return t


def qk_norm_rope_128(kb, st_b, raw, wcol, rm, cos, sin, out, ones_bf):
    sq, pss, sd, rstd, xn, prot, t1, t2 = st_b
    for (t0, tn) in TOKBLK:
        kb.op("act", lambda e: e.activation(out=sq[:, 0:tn], in_=raw[:, t0:t0 + tn], func=AF.Square),
              reads=[raw], writes=[sq])
        kb.op("pe", lambda e: e.matmul(pss[:, 0:tn], lhsT=ones_bf[:, :], rhs=sq[:, 0:tn], start=True, stop=True),
              reads=[ones_bf, sq], writes=[pss])
        rstd_from_psum(kb, pss, sd, rstd, tn, 1.0 / 128)
        kb.op("dve", lambda e: e.scalar_tensor_tensor(out=xn[:, 0:tn], in0=raw[:, t0:t0 + tn], scalar=wcol,
                                                      in1=rstd[:, 0:tn], op0=ALU.mult, op1=ALU.mult),
              reads=[raw, rstd], writes=[xn])
        rope_apply(kb, xn, rm, cos, sin, t0, out, t0, tn, 128, prot, t1, t2)


def normrope_bufs(kb, st):
    sq = st.tile("nr_sq", [128, 512], BF16)
    pss = st.ptile("nr_pss", [128, 512])
    sd = st.tile("nr_sd", [128, 512], F32)
    rstd = st.tile("nr_rstd", [128, 512], F32)
    xn = st.tile("nr_xn", [128, 512], BF16)
    prot = st.ptile("nr_prot", [128, 512])
    t1 = st.tile("nr_t1", [128, 512], F32)
    t2 = st.tile("nr_t2", [128, 512], F32)
    return (sq, pss, sd, rstd, xn, prot, t1, t2)


def stage_x_gqa(kb, io, l, P, XK):
    with Stage(kb) as st:
        ones_bf = load_const(kb, st, io, "ones_bf", [128, 128], BF16)
        rmA = load_const(kb, st, io, "rmA", [128, 128], BF16)
        cosA = load_const(kb, st, io, "cosA", [128, TL], F32)
        sinA = load_const(kb, st, io, "sinA", [128, TL], F32)
        cols = load_const(kb, st, io, "cols", [128, NCOLS], F32, src=io["cols"][l])
        nb = normrope_bufs(kb, st)
        for g in range(2):
            raw = st.tile("kraw%d" % g, [128, TL], BF16)
            out = st.tile("kout%d" % g, [128, TL], BF16)
            kb.dma("sp", raw[:], P["ak"][g * 128:(g + 1) * 128, :], raw, writes=[raw])
            qk_norm_rope_128(kb, nb, raw, cols[:, 1:2], rmA, cosA, sinA, out, ones_bf)
            kb.dma("sp", XK[g], out[:], out, reads=[out])
        kb.barrier()


def stage_attn_A(kb, io, l, P, G, Y, ctx_out):
    with Stage(kb) as st:
        ones_bf = load_const(kb, st, io, "ones_bf", [128, 128], BF16)
        rmA = load_const(kb, st, io, "rmA", [128, 128], BF16)
        cosA = load_const(kb, st, io, "cosA", [128, TL], F32)
        sinA = load_const(kb, st, io, "sinA", [128, TL], F32)
        cols = load_const(kb, st, io, "cols", [128, NCOLS], F32, src=io["cols"][l])
        nb = normrope_bufs(kb, st)
        ab = attn_bufs(kb, st, ones_bf)
        kT = st.tile("kT", [128, NKEY], BF16)
        V = st.tile("V", [128, NKEY // 128, 128], BF16)
        qraw = [st.tile("qraw%d" % i, [128, TL], BF16) for i in range(2)]
        graw = [st.tile("graw%d" % i, [128, TL], BF16) for i in range(2)]
        qr = [st.tile("qr%d" % i, [128, TL], BF16) for i in range(2)]
        sg = [st.tile("sg%d" % i, [128, TL], F32) for i in range(2)]
        ysb = [st.tile("ysb%d" % i, [128, TL], BF16) for i in range(2)]
        blocks = []
        if ctx_out:
            blocks.append((0, NCTX, [0, 1]))
        allk = list(range(NKEY // 128))
        blocks += [(NCTX, 512, allk), (NCTX + 512, 512, allk)]
        t0 = 0 if ctx_out else NCTX
        for g in range(2):
            G.load_kT(kb, kT, g)
            G.load_V(kb, V, g)
            for hh in range(4):
                h = g * 4 + hh
                b = h % 2
                kb.dma("sp", qraw[b][:], P["aq"][h * 128:(h + 1) * 128, :], qraw[b], writes=[qraw[b]])
                kb.dma("sp", graw[b][:], P["ag"][h * 128:(h + 1) * 128, :], graw[b], writes=[graw[b]])
                qk_norm_rope_128(kb, nb, qraw[b], cols[:, 0:1], rmA, cosA, sinA, qr[b], ones_bf)
                kb.op("act", lambda e: e.activation(out=sg[b][:], in_=graw[b][:], func=AF.Silu),
                      reads=[graw[b]], writes=[sg[b]])
                attn_core(kb, ab, [(qr[b], 128)], [(kT, 128)], V, sg[b], ysb[b], blocks, 128 ** -0.5)
                kb.dma("sp", Y[h * 128:(h + 1) * 128, t0:TL], ysb[b][:, t0:TL], ysb[b], reads=[ysb[b]])
        kb.barrier()


NCOLS = 48 + 31 * 8
WL = [0]


def rope_tables(R, tpos):
    half = R // 2
    quarter = half // 2
    inv_freq = (10000.0 ** (-np.arange(quarter, dtype=np.float32) / quarter)).astype(np.float32)
    tpos = np.asarray(tpos)
    rows = (tpos // 64).astype(np.float32)
    cols = (tpos % 64).astype(np.float32)
    cos = np.ones((R, len(tpos)), np.float32)
    sin = np.zeros((R, len(tpos)), np.float32)
    valid = tpos >= 0
    for d in range(R):
        pos = rows if d < half else cols
        i = (d % half) % quarter
        ang = (pos * inv_freq[i]).astype(np.float32)
        cos[d, valid] = np.cos(ang)[valid]
        sin[d, valid] = np.sin(ang)[valid]
    return cos, sin


def rope_rot_matrix(R):
    half = R // 2
    quarter = half // 2
    rm = np.zeros((R, R), np.float32)
    for m in range(R):
        dd = m % half
        if dd < quarter:
            rm[m + quarter, m] = -1.0
        else:
            rm[m - quarter, m] = 1.0
    return rm


def make_cols(inp, l):
    c = np.zeros((128, NCOLS), np.float32)
    c[:, 0] = inp["att_q_norm"][l]
    c[:, 1] = inp["att_k_norm"][l]
    c[:, 2] = inp["mla_qk_q_norm"][l][0:128]
    c[0:64, 3] = inp["mla_qk_q_norm"][l][128:192]
    c[:, 4] = inp["mla_qk_k_norm"][l][0:128]
    c[0:64, 5] = inp["mla_qk_k_norm"][l][128:192]
    c[:, 6:12] = inp["mla_q_norm"][l].reshape(6, 128).T
    c[:, 12:16] = inp["mla_kv_norm"][l].reshape(4, 128).T
    c[:, 16:24] = inp["hgrn_o_norm"][l].reshape(8, 128).T
    c[:, 24:32] = inp["conv_b"][l].reshape(8, 128).T
    c[:, 32:40] = inp["conv_ln_w"][l].reshape(8, 128).T
    c[:, 40:48] = inp["conv_ln_b"][l].reshape(8, 128).T
    c[:, 48:] = inp["conv_w"][l].reshape(31, 8, 128).transpose(2, 0, 1).reshape(128, 31 * 8)
    return c


def local_tpos(j):
    return np.concatenate([-np.ones(NCTX, np.int64), np.arange(NLAT, dtype=np.int64) + NLAT * j])


def key_tpos():
    return np.concatenate([-np.ones(NCTX, np.int64), np.arange(SEQ, dtype=np.int64)])


def rms_tiles_F(kb, st, raw, ntile, wcol0, cols, out, ones_bf, inv_dim, blocks, nbufs):
    sq, pss, sd, rstd = nbufs[0], nbufs[1], nbufs[2], nbufs[3]
    for (t0, tn) in blocks:
        for c in range(ntile):
            kb.op("act", lambda e: e.activation(out=sq[:, 0:tn], in_=raw[:, c, t0:t0 + tn], func=AF.Square),
                  reads=[raw], writes=[sq])
            kb.op("pe", lambda e: e.matmul(pss[:, 0:tn], lhsT=ones_bf[:, :], rhs=sq[:, 0:tn],
                                           start=(c == 0), stop=(c == ntile - 1)),
                  reads=[ones_bf, sq], writes=[pss], acc=(c > 0))
        rstd_from_psum(kb, pss, sd, rstd, tn, inv_dim)
        for c in range(ntile):
            kb.op("dve", lambda e: e.scalar_tensor_tensor(out=out[:, c, t0:t0 + tn], in0=raw[:, c, t0:t0 + tn],
                                                          scalar=cols[:, wcol0 + c:wcol0 + c + 1], in1=rstd[:, 0:tn],
                                                          op0=ALU.mult, op1=ALU.mult),
                  reads=[raw, rstd], writes=[out])


def stage_x_mla(kb, io, l, P, XC):
    with Stage(kb) as st:
        ones_bf = load_const(kb, st, io, "ones_bf", [128, 128], BF16)
        cols = load_const(kb, st, io, "cols", [128, NCOLS], F32, src=io["cols"][l])
        nb = normrope_bufs(kb, st)
        raw = st.tile("ckraw", [128, 4, TL], BF16)
        out = st.tile("ckout", [128, 4, TL], BF16)
        kb.dma("sp", raw[:], P["dckv"].rearrange("(c p) t -> p c t", p=128), raw, writes=[raw])
        rms_tiles_F(kb, st, raw, 4, 12, cols, out, ones_bf, 1.0 / 512, TOKBLK, nb)
        kb.dma("sp", XC.rearrange("(c p) t -> p c t", p=128), out[:], out, reads=[out])
        kb.barrier()


KEYBLK = [(i * 512, min(512, NKEY - i * 512)) for i in range((NKEY + 511) // 512)]


def stage_attn_D(kb, io, l, P, G, Y, ctx_out):
    with Stage(kb) as st:
        ones_bf = load_const(kb, st, io, "ones_bf", [128, 128], BF16)
        rmD = load_const(kb, st, io, "rmD", [64, 64], BF16)
        cols = load_const(kb, st, io, "cols", [128, NCOLS], F32, src=io["cols"][l])
        cosDq = load_const(kb, st, io, "cosDq", [64, TL], F32)
        sinDq = load_const(kb, st, io, "sinDq", [64, TL], F32)
        nb = normrope_bufs(kb, st)
        sq, pss, sd, rstd, xn, prot, t1, t2 = nb
        ab = attn_bufs(kb, st, ones_bf)
        pA = st.ptile("pA", [128, 512])
        pB = st.ptile("pB", [128, 512])
        ckv = st.tile("ckv", [128, 4, NKEY], BF16)
        sqr = st.tile("sqr", [64, NKEY], BF16)
        krr = st.tile("krr", [64, NKEY], F32)
        cqn = st.tile("cqn", [128, 6, TL], BF16)
        G.load_ckv(kb, ckv)
        with Stage(kb) as s0:
            cosDk = load_const(kb, s0, io, "cosDk", [64, NKEY], F32)
            sinDk = load_const(kb, s0, io, "sinDk", [64, NKEY], F32)
            kr = s0.tile("kr", [64, NKEY], BF16)
            cqraw = s0.tile("cqraw", [128, 6, TL], BF16)
            G.load_kr(kb, kr)
            kb.dma("sp", cqraw[:], P["dcq"].rearrange("(c p) t -> p c t", p=128), cqraw, writes=[cqraw])
            kb.op("act", lambda e: e.activation(out=sqr[:], in_=kr[:], func=AF.Square), reads=[kr], writes=[sqr])
            for (k0, kn_) in KEYBLK:
                kb.op("dve", lambda e: e.tensor_scalar(out=xn[0:64, 0:kn_], in0=kr[:, k0:k0 + kn_], scalar1=cols[0:64, 5:6],
                                                       scalar2=None, op0=ALU.mult),
                      reads=[kr], writes=[xn])
                rope_apply(kb, xn, rmD, cosDk, sinDk, k0, krr, k0, kn_, 64, prot, t1, t2)
            rms_tiles_F(kb, s0, cqraw, 6, 6, cols, cqn, ones_bf, 1.0 / 768, TOKBLK, nb)
            kb.barrier()
        wuq = [st.tile("wuq%d" % i, [128, 6, 192], BF16) for i in range(2)]
        wukv = [st.tile("wukv%d" % i, [128, 4, 256], BF16) for i in range(2)]
        qnope = [st.tile("qnope%d" % i, [128, TL], BF16) for i in range(2)]
        qrope = [st.tile("qrope%d" % i, [64, TL], BF16) for i in range(2)]
        kn = st.tile("kn", [128, NKEY], BF16)
        krh = st.tile("krh", [64, NKEY], BF16)
        V = st.tile("Vd", [128, NKEY // 128, 128], BF16)
        graw = [st.tile("dgraw%d" % i, [128, TL], BF16) for i in range(2)]
        sg = [st.tile("dsg%d" % i, [128, TL], F32) for i in range(2)]
        ysb = [st.tile("dysb%d" % i, [128, TL], BF16) for i in range(2)]
        sq2 = st.tile("sq2", [64, 512], BF16)
        sqB = st.tile("sqB", [128, 512], BF16)
        sdB = st.tile("sdB", [128, 512], F32)
        rstdB = st.tile("rstdB", [128, 512], F32)
        ksets = [(pA, pss, sq, sd, rstd), (pB, prot, sqB, sdB, rstdB)]
        blocks = []
        if ctx_out:
            blocks.append((0, NCTX, [0, 1]))
        allk = list(range(NKEY // 128))
        blocks += [(NCTX, 512, allk), (NCTX + 512, 512, allk)]
        t0o = 0 if ctx_out else NCTX
        for h in range(8):
            b = h % 2
            kb.dma("pool", wuq[b][:], io["mla_w_uq"][l - WL[0], :, h * 192:(h + 1) * 192].rearrange("(c p) n -> p c n", p=128),
                   wuq[b], writes=[wuq[b]])
            kb.dma("pool", wukv[b][:], io["mla_w_ukv"][l - WL[0], :, h * 256:(h + 1) * 256].rearrange("(c p) n -> p c n", p=128),
                   wukv[b], writes=[wukv[b]])
            kb.dma("sp", graw[b][:], P["dg"][h * 128:(h + 1) * 128, :], graw[b], writes=[graw[b]])
            kb.op("act", lambda e: e.activation(out=sg[b][:], in_=graw[b][:], func=AF.Silu), reads=[graw[b]], writes=[sg[b]])
            for (t0, tn) in TOKBLK:
                for c in range(6):
                    kb.op("pe", lambda e: e.matmul(pA[:, 0:tn], lhsT=wuq[b][:, c, 0:128], rhs=cqn[:, c, t0:t0 + tn],
                                                   start=(c == 0), stop=(c == 5)),
                          reads=[wuq[b], cqn], writes=[pA], inc=(c == 5), acc=(c > 0))
                for c in range(6):
                    kb.op("pe", lambda e: e.matmul(pB[0:64, 0:tn], lhsT=wuq[b][:, c, 128:192], rhs=cqn[:, c, t0:t0 + tn],
                                                   start=(c == 0), stop=(c == 5)),
                          reads=[wuq[b], cqn], writes=[pB], inc=(c == 5), acc=(c > 0))
                kb.op("act", lambda e: e.activation(out=sq[:, 0:tn], in_=pA[:, 0:tn], func=AF.Square), reads=[pA], writes=[sq])
                kb.op("act", lambda e: e.activation(out=sq2[:, 0:tn], in_=pB[0:64, 0:tn], func=AF.Square), reads=[pB], writes=[sq2])
                kb.op("pe", lambda e: e.matmul(pss[:, 0:tn], lhsT=ones_bf[:, :], rhs=sq[:, 0:tn], start=True, stop=False),
                      reads=[ones_bf, sq], writes=[pss], inc=False)
                kb.op("pe", lambda e: e.matmul(pss[:, 0:tn], lhsT=ones_bf[0:64, :], rhs=sq2[:, 0:tn], start=False, stop=True),
                      reads=[ones_bf, sq2], writes=[pss], acc=True)
                rstd_from_psum(kb, pss, sd, rstd, tn, 1.0 / 192)
                kb.op("dve", lambda e: e.scalar_tensor_tensor(out=qnope[b][:, t0:t0 + tn], in0=pA[:, 0:tn], scalar=cols[:, 2:3],
                                                              in1=rstd[:, 0:tn], op0=ALU.mult, op1=ALU.mult),
                      reads=[pA, rstd], writes=[qnope[b]])
                kb.op("dve", lambda e: e.scalar_tensor_tensor(out=xn[0:64, 0:tn], in0=pB[0:64, 0:tn], scalar=cols[0:64, 3:4],
                                                              in1=rstd[0:64, 0:tn], op0=ALU.mult, op1=ALU.mult),
                      reads=[pB, rstd], writes=[xn])
                rope_apply(kb, xn, rmD, cosDq, sinDq, t0, qrope[b], t0, tn, 64, prot, t1, t2)
            for kbi, (k0, kn_) in enumerate(KEYBLK):
                pA_, pss_, sq_, sd_, rstd_ = ksets[kbi % 2]
                for c in range(4):
                    kb.op("pe", lambda e: e.matmul(pA_[:, 0:kn_], lhsT=wukv[b][:, c, 0:128], rhs=ckv[:, c, k0:k0 + kn_],
                                                   start=(c == 0), stop=(c == 3)),
                          reads=[wukv[b], ckv], writes=[pA_], inc=(c == 3), acc=(c > 0))
                kb.op("act", lambda e: e.activation(out=sq_[:, 0:kn_], in_=pA_[:, 0:kn_], func=AF.Square), reads=[pA_], writes=[sq_])
                kb.op("pe", lambda e: e.matmul(pss_[:, 0:kn_], lhsT=ones_bf[:, :], rhs=sq_[:, 0:kn_], start=True, stop=False),
                      reads=[ones_bf, sq_], writes=[pss_], inc=False)
                kb.op("pe", lambda e: e.matmul(pss_[:, 0:kn_], lhsT=ones_bf[0:64, :], rhs=sqr[:, k0:k0 + kn_], start=False, stop=True),
                      reads=[ones_bf, sqr], writes=[pss_], acc=True)
                rstd_from_psum(kb, pss_, sd_, rstd_, kn_, 1.0 / 192)
                kb.op("dve", lambda e: e.scalar_tensor_tensor(out=kn[:, k0:k0 + kn_], in0=pA_[:, 0:kn_], scalar=cols[:, 4:5],
                                                              in1=rstd_[:, 0:kn_], op0=ALU.mult, op1=ALU.mult),
                      reads=[pA_, rstd_], writes=[kn])
                kb.op("pool", lambda e: e.tensor_tensor(out=krh[:, k0:k0 + kn_], in0=krr[:, k0:k0 + kn_], in1=rstd_[0:64, 0:kn_],
                                                        op=ALU.mult),
                      reads=[krr, rstd_], writes=[krh])
            nkt = NKEY // 128
            for k4 in range(0, nkt, 4):
                n4 = min(4, nkt - k4)
                for i in range(n4):
                    kt = k4 + i
                    for c in range(4):
                        kb.op("pe", lambda e: e.matmul(pB[:, i * 128:(i + 1) * 128], lhsT=ckv[:, c, kt * 128:(kt + 1) * 128],
                                                       rhs=wukv[b][:, c, 128:256], start=(c == 0), stop=(c == 3)),
                              reads=[wukv[b], ckv], writes=[pB], inc=(c == 3 and i == n4 - 1), acc=(c > 0 or i > 0))
                kb.op("act", lambda e: e.copy(out=V[:, k4:k4 + n4, :], in_=pB[:, 0:n4 * 128].rearrange("p (a b) -> p a b", b=128)),
                      reads=[pB], writes=[V])
            attn_core(kb, ab, [(qnope[b], 128), (qrope[b], 64)], [(kn, 128), (krh, 64)], V, sg[b], ysb[b], blocks, 192 ** -0.5)
            kb.dma("sp", Y[3072 + h * 128:3072 + (h + 1) * 128, t0o:TL], ysb[b][:, t0o:TL], ysb[b], reads=[ysb[b]])
        kb.barrier()


XO_CTX = 15
XO_LAT = 15 + NCTX + 15 + 15
XW = XO_LAT + NLAT + 15
CVW = XW - 30
CVBLK = [(0, 512), (512, 512), (1024, CVW - 1024)]
NTAP_DVE = 31


def stage_conv(kb, io, l, P, G, Y, ctx_out):
    with Stage(kb) as st:
        ones_bf = load_const(kb, st, io, "ones_bf", [128, 128], BF16)
        cols = load_const(kb, st, io, "cols", [128, NCOLS], F32, src=io["cols"][l])
        cv = st.tile("cv", [128, 8, CVW], F32)
        ybf = st.tile("ybf", [128, 8, CVW], BF16)
        ysq = st.tile("ysq", [128, 8, CVW], BF16)
        u = [st.tile("cu%d" % i, [128, TL], BF16) for i in range(2)]
        glu = [st.tile("cglu%d" % i, [128, TL], BF16) for i in range(2)]
        hu = [st.tile("hu%d" % i, [128, 4, 32], BF16) for i in range(2)]
        hg = [st.tile("hg%d" % i, [128, 4, 32], BF16) for i in range(2)]
        sig = st.tile("csig", [128, TL], F32)
        hsig = st.tile("chsig", [128, 4, 32], F32)
        hxg = st.tile("chxg", [128, 4, 32], F32)
        hsel = st.tile("chsel", [128, 2, 16], F32)
        hmk = load_const(kb, st, io, "halomask", [128, 2, 4], F32)
        xg = [st.tile("xg%d" % i, [128, XW], F32) for i in range(2)]
        acc2 = st.tile("acc2", [128, CVW], F32)
        for i in range(2):
            kb.op("pool", lambda e: e.memset(xg[i][:], 0.0), writes=[xg[i]])
        for ct in range(8):
            b = ct % 2
            r0 = ct * 128
            kb.dma("sp", u[b][:], P["cu"][r0:r0 + 128, :], u[b], writes=[u[b]])
            kb.dma("sp", glu[b][:], P["cglu"][r0:r0 + 128, :], glu[b], writes=[glu[b]])
            G.load_halo(kb, hu[b], hg[b], r0)
            kb.op("act", lambda e: e.activation(out=sig[:], in_=glu[b][:], func=AF.Sigmoid), reads=[glu[b]], writes=[sig])
            kb.op("act", lambda e: e.activation(out=hsig[:], in_=hg[b][:], func=AF.Sigmoid), reads=[hg[b]], writes=[hsig])
            X = xg[b]
            kb.op("dve", lambda e: e.tensor_tensor(out=X[:, XO_CTX:XO_CTX + NCTX], in0=u[b][:, 0:NCTX], in1=sig[:, 0:NCTX], op=ALU.mult),
                  reads=[u[b], sig], writes=[X])
            kb.op("dve", lambda e: e.tensor_tensor(out=X[:, XO_LAT:XO_LAT + NLAT], in0=u[b][:, NCTX:TL], in1=sig[:, NCTX:TL], op=ALU.mult),
                  reads=[u[b], sig], writes=[X])
            kb.op("dve", lambda e: e.tensor_tensor(out=hxg[:], in0=hu[b][:], in1=hsig[:], op=ALU.mult),
                  reads=[hu[b], hsig], writes=[hxg])
            for side in range(2):
                slot = 1 - side
                for r in range(4):
                    src = hxg[:, r, slot * 16:slot * 16 + 16]
                    mcol = hmk[:, side, r:r + 1]
                    if r == 0:
                        kb.op("dve", lambda e: e.tensor_scalar(out=hsel[:, side, :], in0=src, scalar1=mcol, scalar2=None, op0=ALU.mult),
                              reads=[hxg, hmk], writes=[hsel])
                    else:
                        kb.op("dve", lambda e: e.scalar_tensor_tensor(out=hsel[:, side, :], in0=src, scalar=mcol, in1=hsel[:, side, :],
                                                                      op0=ALU.mult, op1=ALU.add),
                              reads=[hxg, hmk, hsel], writes=[hsel])
            kb.op("dve", lambda e: e.tensor_copy(out=X[:, XO_LAT - 15:XO_LAT], in_=hsel[:, 0, 0:15]), reads=[hsel], writes=[X])
            kb.op("dve", lambda e: e.tensor_copy(out=X[:, XO_LAT + NLAT:XO_LAT + NLAT + 15], in_=hsel[:, 1, 0:15]), reads=[hsel], writes=[X])
            for tap in range(31):
                wc = cols[:, 48 + tap * 8 + ct:48 + tap * 8 + ct + 1]
                if tap < NTAP_DVE:
                    en, dst = "dve", cv[:, ct, :]
                    first = (tap == 0)
                    dtk = cv
                else:
                    en, dst = "pool", acc2[:, :]
                    first = (tap == NTAP_DVE)
                    dtk = acc2
                if first:
                    kb.op(en, lambda e: e.tensor_scalar(out=dst, in0=X[:, tap:tap + CVW], scalar1=wc, scalar2=None, op0=ALU.mult),
                          reads=[X], writes=[dtk])
                else:
                    kb.op(en, lambda e: e.scalar_tensor_tensor(out=dst, in0=X[:, tap:tap + CVW], scalar=wc, in1=dst,
                                                              op0=ALU.mult, op1=ALU.add),
                          reads=[X, dtk], writes=[dtk])
            kb.op("dve", lambda e: e.tensor_scalar(out=cv[:, ct, :], in0=cv[:, ct, :], scalar1=cols[:, 24 + ct:25 + ct],
                                                   scalar2=None, op0=ALU.add),
                  reads=[cv], writes=[cv])
            kb.op("act", lambda e: e.copy(out=ybf[:, ct, :], in_=cv[:, ct, :]), reads=[cv], writes=[ybf])
            kb.op("act", lambda e: e.activation(out=ysq[:, ct, :], in_=cv[:, ct, :], func=AF.Square), reads=[cv], writes=[ysq])
        pm = st.ptile("cpm", [128, 512])
        pq = st.ptile("cpq", [128, 512])
        mean = st.tile("cmean", [128, CVW], F32)
        rstd = st.tile("crstd", [128, CVW], F32)
        var = st.tile("cvar", [128, 512], F32)
        msq = st.tile("cmsq", [128, 512], F32)
        for (c0, cn) in CVBLK:
            for ct in range(8):
                kb.op("pe", lambda e: e.matmul(pm[:, 0:cn], lhsT=ones_bf[:, :], rhs=ybf[:, ct, c0:c0 + cn], start=(ct == 0), stop=(ct == 7)),
                      reads=[ones_bf, ybf], writes=[pm], inc=(ct == 7), acc=(ct > 0))
            for ct in range(8):
                kb.op("pe", lambda e: e.matmul(pq[:, 0:cn], lhsT=ones_bf[:, :], rhs=ysq[:, ct, c0:c0 + cn], start=(ct == 0), stop=(ct == 7)),
                      reads=[ones_bf, ysq], writes=[pq], inc=(ct == 7), acc=(ct > 0))
            kb.op("act", lambda e: e.activation(out=mean[:, c0:c0 + cn], in_=pm[:, 0:cn], func=AF.Copy, scale=1.0 / 1024),
                  reads=[pm], writes=[mean])
            kb.op("dve", lambda e: e.tensor_tensor(out=msq[:, 0:cn], in0=mean[:, c0:c0 + cn], in1=mean[:, c0:c0 + cn], op=ALU.mult),
                  reads=[mean], writes=[msq])
            kb.op("dve", lambda e: e.scalar_tensor_tensor(out=var[:, 0:cn], in0=pq[:, 0:cn], scalar=1.0 / 1024, in1=msq[:, 0:cn],
                                                          op0=ALU.mult, op1=ALU.subtract),
                  reads=[pq, msq], writes=[var])
            kb.op("act", lambda e: e.activation(out=var[:, 0:cn], in_=var[:, 0:cn], func=AF.Sqrt, bias=EPS, scale=1.0),
                  reads=[var], writes=[var])
            kb.op("dve", lambda e: e.reciprocal(out=rstd[:, c0:c0 + cn], in_=var[:, 0:cn]), reads=[var], writes=[rstd])
        gt = [st.tile("cgt%d" % i, [128, TL], BF16) for i in range(2)]
        sgt = [st.tile("csg%d" % i, [128, TL], F32) for i in range(2)]
        z = [st.tile("cz%d" % i, [128, CVW], F32) for i in range(2)]
        yo = [st.tile("cyo%d" % i, [128, TL], BF16) for i in range(2)]
        t0o = 0 if ctx_out else NCTX
        for ct in range(8):
            b = ct % 2
            r0 = ct * 128
            kb.dma("sp", gt[b][:], P["cg"][r0:r0 + 128, :], gt[b], writes=[gt[b]])
            kb.op("act", lambda e: e.activation(out=sgt[b][:], in_=gt[b][:], func=AF.Silu), reads=[gt[b]], writes=[sgt[b]])
            kb.op("dve", lambda e: e.tensor_tensor(out=z[b][:], in0=cv[:, ct, :], in1=mean[:], op=ALU.subtract),
                  reads=[cv, mean], writes=[z[b]])
            kb.op("pool", lambda e: e.tensor_tensor(out=z[b][:], in0=z[b][:], in1=rstd[:], op=ALU.mult),
                  reads=[z[b], rstd], writes=[z[b]])
            kb.op("act", lambda e: e.activation(out=z[b][:], in_=z[b][:], func=AF.Silu, scale=cols[:, 32 + ct:33 + ct],
                                                bias=cols[:, 40 + ct:41 + ct]),
                  reads=[z[b]], writes=[z[b]])
            kb.op("dve", lambda e: e.tensor_tensor(out=yo[b][:, 0:NCTX], in0=z[b][:, 0:NCTX], in1=sgt[b][:, 0:NCTX], op=ALU.mult),
                  reads=[z[b], sgt[b]], writes=[yo[b]])
            kb.op("pool", lambda e: e.tensor_tensor(out=yo[b][:, NCTX:TL], in0=z[b][:, XO_LAT - 15:XO_LAT - 15 + NLAT],
                                                    in1=sgt[b][:, NCTX:TL], op=ALU.mult),
                  reads=[z[b], sgt[b]], writes=[yo[b]])
            kb.dma("sp", Y[2048 + r0:2048 + r0 + 128, t0o:TL], yo[b][:, t0o:TL], yo[b], reads=[yo[b]])
        kb.barrier()


def stage_outproj(kb, io, l, Y, gate_src, x_src, ctx_src, x_dst, ctx_dst, ctx_out):
    CB = 512
    with Stage(kb) as st:
        yT = st.tile("yT", [128, KC, TL], BF16)
        gbc = [[st.tile("gbc%d_%d" % (t, i), [128, CB], F32) for i in range(2)] for t in range(2)]
        wo = [st.tile("wo%d" % i, [128, KC, CB], BF16) for i in range(2)]
        xt = [st.tile("oxt%d" % i, [128, CB], F32) for i in range(3)]
        ot = [st.tile("oot%d" % i, [128, CB], F32) for i in range(3)]
        pacc = [st.ptile("opa%d" % i, [128, CB]) for i in range(4)]
        t0o = 0 if ctx_out else NCTX
        ysrc = Y.rearrange("(c p) t -> p c t", p=128)
        for q in range(4):
            kb.dma("sp", yT[:, q * 8:(q + 1) * 8, t0o:TL], ysrc[:, q * 8:(q + 1) * 8, t0o:TL], yT, writes=[yT])
        pi = 0
        xi = 0
        for cb in range(D // CB):
            c0 = cb * CB
            w = wo[cb % 2]
            wsrc = io["w_out"][l - WL[0], :, c0:c0 + CB].rearrange("(kc p) c -> p kc c", p=128)
            for kq in range(4):
                kb.dma("pool", w[:, kq * 8:(kq + 1) * 8, :], wsrc[:, kq * 8:(kq + 1) * 8, :], w, writes=[w])
            g = [gbc[0][cb % 2], gbc[1][cb % 2]]
            kb.dma("sp", g[0][:], gate_src[0][:, c0:c0 + CB], g[0], writes=[g[0]])
            if ctx_out:
                kb.dma("sp", g[1][:], gate_src[1][:, c0:c0 + CB], g[1], writes=[g[1]])
            for i in range(NT):
                if i < 2 and not ctx_out:
                    continue
                t = 1 if i < 2 else 0
                srcx = ctx_src[i * 128:(i + 1) * 128, c0:c0 + CB] if i < 2 else x_src[(i - 2) * 128:(i - 1) * 128, c0:c0 + CB]
                dstx = ctx_dst[i * 128:(i + 1) * 128, c0:c0 + CB] if i < 2 else x_dst[(i - 2) * 128:(i - 1) * 128, c0:c0 + CB]
                pa = pacc[pi % 4]
                pi += 1
                xx, oo = xt[xi % 3], ot[xi % 3]
                xi += 1
                kb.dma("sp", xx[:], srcx, xx, writes=[xx])
                for kc in range(KC):
                    kb.op("pe", lambda e: e.matmul(pa[:, :], lhsT=yT[:, kc, i * 128:(i + 1) * 128], rhs=w[:, kc, :],
                                                   start=(kc == 0), stop=(kc == KC - 1)),
                          reads=[yT, w], writes=[pa], inc=(kc == KC - 1), acc=(kc > 0))
                kb.op("dve", lambda e: e.tensor_tensor(out=oo[:], in0=pa[:], in1=g[t][:], op=ALU.mult),
                      reads=[pa, g[t]], writes=[oo])
                kb.op("pool", lambda e: e.tensor_tensor(out=oo[:], in0=oo[:], in1=xx[:], op=ALU.add),
                      reads=[oo, xx], writes=[oo])
                kb.dma("sp", dstx, oo[:], oo, reads=[oo])
        kb.barrier()


def hgrn_consts():
    s = np.arange(64)[:, None]
    t = np.arange(64)[None, :]
    out = {}
    M1 = np.zeros((2, 128, 128), np.float32)
    SC = np.zeros((2, 128, 128), np.float32)
    MK = np.zeros((2, 128, 128), np.float32)
    MC = np.zeros((2, 128, 4), np.float32)
    for d in range(2):
        if d == 0:
            m1 = (s <= t).astype(np.float32) - (s <= 31).astype(np.float32)
            sc = (s > t).astype(np.float32)
            mk = (s <= t).astype(np.float32)
            mid = (np.arange(64) <= 31).astype(np.float32)
        else:
            m1 = (s >= t).astype(np.float32) - (s >= 32).astype(np.float32)
            sc = (s < t).astype(np.float32)
            mk = (s >= t).astype(np.float32)
            mid = (np.arange(64) >= 32).astype(np.float32)
        for hf in range(2):
            sl = slice(hf * 64, hf * 64 + 64)
            M1[d, sl, sl] = m1
            SC[d, sl, sl] = sc
            MK[d, sl, sl] = mk
            MC[d, sl, hf] = mid
            MC[d, sl, 2 + hf] = 1.0
    out["hM1"] = M1
    out["hM1n"] = -M1
    out["hSC"] = SC
    out["hMK"] = MK
    out["hMC"] = MC
    out["identf"] = np.eye(128, dtype=np.float32)
    return out


def stage_hgrn(kb, io, l, P, G, Y, XS, XD, pass_id, ctx_out, dbg=None):
    full = (pass_id == 2)
    with Stage(kb) as st:
        identf = load_const(kb, st, io, "identf", [128, 128], F32)
        cols = load_const(kb, st, io, "cols", [128, NCOLS], F32, src=io["cols"][l])
        PA = st.ptile("hPA", [128, 1024])
        PB = st.ptile("hPB", [128, 1024])
        PC = st.ptile("hPC", [128, 1024])
        PD = st.ptile("hPD", [128, 512])
        oml = st.tile("oml", [128, 2, 1024], F32)
        if l == 0:
            kb.op("dve", lambda e: e.memset(oml[:], 1.0), writes=[oml])
        else:
            with Stage(kb) as s0:
                lg = s0.tile("lg", [1, 2, 2, 1024], F32)
                df = s0.tile("lgd", [1, 2, 1024], F32)
                ones_row = s0.tile("onesr", [1, 128], F32)
                kb.dma("sp", lg[:], io["hgrn_lb_logits"].rearrange("(o a) b c -> o a b c", o=1), lg, writes=[lg])
                kb.op("dve", lambda e: e.memset(ones_row[:], 1.0), writes=[ones_row])
                kb.op("dve", lambda e: e.tensor_tensor(out=df[:], in0=lg[:, 0, :, :], in1=lg[:, 1, :, :], op=ALU.subtract),
                      reads=[lg], writes=[df])
                kb.op("act", lambda e: e.activation(out=df[:], in_=df[:], func=AF.Sigmoid), reads=[df], writes=[df])
                for d in range(2):
                    for n in range(2):
                        kb.op("pe", lambda e: e.matmul(PA[:, n * 512:(n + 1) * 512], lhsT=ones_row[0:1, :],
                                                       rhs=df[0:1, d, n * 512:(n + 1) * 512], start=True, stop=True),
                              reads=[ones_row, df], writes=[PA], acc=(n > 0))
                    kb.op("act", lambda e: e.copy(out=oml[:, d, :], in_=PA[:, :]), reads=[PA], writes=[oml])
                kb.barrier()
        S = st.tile("hS", [128, 8, 128], F32)
        Sbf = st.tile("hSbf", [128, 8, 128], BF16)
        stmp = st.tile("hstmp", [128, 8, 128], F32)
        Dcum = st.tile("hDcum", [128, 8], F32)
        ft = [st.tile("hf%d" % i, [128, 1024], F32) for i in range(2)]
        kt = st.tile("hk", [128, 1024], F32)
        lf = [st.tile("hlf%d" % i, [128, 1024], F32) for i in range(2)]
        logk = [st.tile("hlogk%d" % i, [128, 1024], F32) for i in range(2)]
        Kd = [st.tile("hKd%d" % i, [128, 1024], BF16) for i in range(2)]
        vt = [st.tile("hv%d" % i, [128, 1024], BF16) for i in range(2)]
        gc = [st.tile("hgc%d" % i, [128, 8, 4], F32) for i in range(2)]
        if full:
            qs = st.tile("hqs", [128, 8, TL], BF16)
            oacc = st.tile("hoacc", [128, 8, TL], F32)
            eA = st.tile("heA", [128, 8, 128], F32)
            AT = [st.tile("hAT%d" % i, [128, 8, 128], BF16) for i in range(2)]
            BT = [st.tile("hBT%d" % i, [128, 8, 2, 128], BF16) for i in range(2)]
            for i in range(2):
                kb.op("pool", lambda e: e.memset(BT[i][:], 0.0), writes=[BT[i]])
            scm = [st.tile("hscm%d" % i, [128, 8, 128], BF16) for i in range(2)]
            kb.dma("sp", qs[:], P["bq"].rearrange("(h p) t -> p h t", p=128), qs, writes=[qs])
            kb.op("act", lambda e: e.activation(out=qs[:], in_=qs[:], func=AF.Silu), reads=[qs], writes=[qs])
            kb.op("dve", lambda e: e.tensor_scalar(out=qs[:], in0=qs[:], scalar1=float(128 ** -0.5), scalar2=None, op0=ALU.mult),
                  reads=[qs], writes=[qs])
        ti = 0
        for d in range(2):
            M1 = load_const(kb, st, io, "hM1_%d" % d, [128, 128], F32, src=io["hM1"][d])
            M1n = load_const(kb, st, io, "hM1n_%d" % d, [128, 128], F32, src=io["hM1n"][d])
            SC = load_const(kb, st, io, "hSC_%d" % d, [128, 128], F32, src=io["hSC"][d])
            MK = load_const(kb, st, io, "hMK_%d" % d, [128, 128], F32, src=io["hMK"][d])
            MC = load_const(kb, st, io, "hMC_%d" % d, [128, 4], F32, src=io["hMC"][d])
            if full:
                MK8 = st.tile("hMK8_%d" % d, [128, 8, 128], F32)
                kb.op("dve", lambda e: e.tensor_copy(out=MK8[:], in_=MK[:, :].unsqueeze(1).broadcast_to([128, 8, 128])),
                      reads=[MK], writes=[MK8])
                for i in range(2):
                    kb.op("pool", lambda e: e.memset(scm[i][:], 0.0), writes=[scm[i]])
            fsrc = P["bff"] if d == 0 else P["bfb"]
            tiles = ([0, 1] + list(range(2, NT))) if d == 0 else ([1, 0] + list(range(NT - 1, 1, -1)))
            kb.op("pool", lambda e: e.memset(S[:], 0.0), writes=[S])
            for n_, i in enumerate(tiles):
                b = ti % 2
                ti += 1
                if n_ == 2:
                    if not full:
                        kb.op("pool", lambda e: e.memset(S[:], 0.0), writes=[S])
                        kb.op("pool", lambda e: e.memset(Dcum[:], 1.0), writes=[Dcum])
                    else:
                        with Stage(kb) as s1:
                            hm = s1.tile("hm", [128, 2, 4], F32)
                            kb.dma("sp", hm[:], io["hmask"], hm, writes=[hm])
                            Si = [s1.tile("hSi%d" % q, [128, 8, 128], F32) for q in range(2)]
                            Di = [s1.tile("hDi%d" % q, [128, 8], F32) for q in range(2)]
                            order = [0, 1, 2, 3] if d == 0 else [3, 2, 1, 0]
                            for q_, ci in enumerate(order):
                                sb_, db_ = Si[q_ % 2], Di[q_ % 2]
                                kb.dma("sp", sb_[:], G.hS(ci, d), sb_, writes=[sb_])
                                kb.dma("sp", db_[:], G.hD(ci, d), db_, writes=[db_])
                                mcol = hm[:, d, ci:ci + 1]
                                kb.op("dve", lambda e: e.tensor_scalar(out=db_[:], in0=db_[:], scalar1=-1.0, scalar2=mcol,
                                                                       op0=ALU.add, op1=ALU.mult),
                                      reads=[db_, hm], writes=[db_])
                                kb.op("dve", lambda e: e.tensor_scalar(out=db_[:], in0=db_[:], scalar1=1.0, scalar2=None, op0=ALU.add),
                                      reads=[db_], writes=[db_])
                                kb.op("dve", lambda e: e.tensor_tensor(out=stmp[:], in0=S[:],
                                                                       in1=db_[:].unsqueeze(2).broadcast_to([128, 8, 128]), op=ALU.mult),
                                      reads=[S, db_], writes=[stmp])
                                kb.op("dve", lambda e: e.scalar_tensor_tensor(out=S[:], in0=sb_[:], scalar=mcol, in1=stmp[:],
                                                                              op0=ALU.mult, op1=ALU.add),
                                      reads=[sb_, stmp, hm], writes=[S])
                            kb.barrier()
                F_ = ft[b]
                kb.dma("sp", F_[:], fsrc[i * 128:(i + 1) * 128, :], F_, writes=[F_])
                kb.dma("sp", vt[b][:], P["bi"][i * 128:(i + 1) * 128, :], vt[b], writes=[vt[b]])
                kb.op("act", lambda e: e.activation(out=kt[:], in_=F_[:], func=AF.Sigmoid, scale=-1.0), reads=[F_], writes=[kt])
                kb.op("dve", lambda e: e.tensor_tensor(out=kt[:], in0=kt[:], in1=oml[:, d, :], op=ALU.mult), reads=[kt, oml], writes=[kt])
                kb.op("act", lambda e: e.activation(out=logk[b][:], in_=kt[:], func=AF.Ln), reads=[kt], writes=[logk[b]])
                kb.op("act", lambda e: e.activation(out=lf[b][:], in_=kt[:], func=AF.Ln, scale=-1.0, bias=1.0), reads=[kt], writes=[lf[b]])
                for n in range(2):
                    kb.op("pe", lambda e: e.matmul(PA[:, n * 512:(n + 1) * 512], lhsT=SC[:, :], rhs=lf[b][:, n * 512:(n + 1) * 512],
                                                   start=True, stop=False),
                          reads=[SC, lf[b]], writes=[PA], inc=False, acc=(n > 0))
                    kb.op("pe", lambda e: e.matmul(PA[:, n * 512:(n + 1) * 512], lhsT=identf[:, :], rhs=logk[b][:, n * 512:(n + 1) * 512],
                                                   start=False, stop=True),
                          reads=[identf, logk[b]], writes=[PA], inc=(n == 1), acc=True)
                kb.op("act", lambda e: e.activation(out=Kd[b][:], in_=PA[:, :], func=AF.Exp), reads=[PA], writes=[Kd[b]])
                for h in range(8):
                    kb.op("pe", lambda e: e.matmul(PD[:, h * 4:(h + 1) * 4], lhsT=lf[b][:, h * 128:(h + 1) * 128], rhs=MC[:, :],
                                                   start=True, stop=True),
                          reads=[MC, lf[b]], writes=[PD], inc=(h == 7), acc=(h > 0))
                kb.op("act", lambda e: e.activation(out=gc[b][:], in_=PD[:, 0:32].rearrange("p (h c) -> p h c", c=4), func=AF.Exp),
                      reads=[PD], writes=[gc[b]])
                if full:
                    for h in range(8):
                        kb.op("pe", lambda e: e.matmul(PB[:, h * 128:(h + 1) * 128], lhsT=lf[b][:, h * 128:(h + 1) * 128], rhs=M1[:, :],
                                                       start=True, stop=True),
                              reads=[M1, lf[b]], writes=[PB], inc=(h == 7), acc=(h > 0))
                    for h in range(8):
                        kb.op("pe", lambda e: e.matmul(PC[:, h * 128:(h + 1) * 128], lhsT=lf[b][:, h * 128:(h + 1) * 128], rhs=M1n[:, :],
                                                       start=True, stop=False),
                              reads=[M1n, lf[b]], writes=[PC], inc=False, acc=(h > 0))
                        kb.op("pe", lambda e: e.matmul(PC[:, h * 128:(h + 1) * 128], lhsT=logk[b][:, h * 128:(h + 1) * 128], rhs=identf[:, :],
                                                       start=False, stop=True),
                              reads=[identf, logk[b]], writes=[PC], inc=(h == 7), acc=True)
                    kb.op("act", lambda e: e.activation(out=eA[:], in_=PB[:, :].rearrange("p (h t) -> p h t", t=128), func=AF.Exp),
                          reads=[PB], writes=[eA])
                    kb.op("dve", lambda e: e.tensor_tensor(out=AT[b][:], in0=eA[:], in1=qs[:, :, i * 128:(i + 1) * 128], op=ALU.mult),
                          reads=[eA, qs], writes=[AT[b]])
                    pcv = PC[:, :].rearrange("p (h t) -> p h t", t=128)
                    for c in range(2):
                        kb.op("act", lambda e: e.activation(out=BT[b][:, :, c, c * 64:(c + 1) * 64], in_=pcv[:, :, c * 64:(c + 1) * 64],
                                                            func=AF.Exp),
                              reads=[PC], writes=[BT[b]])
                    for h in range(8):
                        for c in range(2):
                            kb.op("pe", lambda e: e.matmul(PA[:, h * 128 + c * 64:h * 128 + (c + 1) * 64], lhsT=BT[b][:, h, c, :],
                                                           rhs=AT[b][:, h, c * 64:(c + 1) * 64], start=True, stop=True),
                                  reads=[BT[b], AT[b]], writes=[PA], inc=(h == 7 and c == 1), acc=(h > 0 or c > 0))
                    kb.op("dve", lambda e: e.copy_predicated(out=scm[b][:], mask=MK8[:].bitcast(mybir.dt.uint32),
                                                             data=PA[:, :].rearrange("p (h t) -> p h t", t=128)),
                          reads=[PA, MK8, scm[b]], writes=[scm[b]])
                    for h in range(8):
                        kb.op("pe", lambda e: e.matmul(PB[:, h * 128:(h + 1) * 128], lhsT=vt[b][:, h * 128:(h + 1) * 128], rhs=scm[b][:, h, :],
                                                       start=(h % 4 == 0), stop=False, skip_group_check=True),
                              reads=[vt[b], scm[b]], writes=[PB], inc=(h == 7), acc=(h > 0))
                for hf in ((0, 1) if d == 0 else (1, 0)):
                    p0 = hf * 64
                    if full:
                        kb.op("dve", lambda e: e.tensor_tensor(out=Sbf[:], in0=S[:],
                                                               in1=gc[b][:, :, hf:hf + 1].broadcast_to([128, 8, 128]), op=ALU.mult),
                              reads=[S, gc[b]], writes=[Sbf])
                        for h in range(8):
                            kb.op("pe", lambda e: e.matmul(PB[:, h * 128 + p0:h * 128 + p0 + 64], lhsT=Sbf[:, h, :],
                                                           rhs=AT[b][:, h, p0:p0 + 64], start=False, stop=True, skip_group_check=True),
                                  reads=[Sbf, AT[b]], writes=[PB], inc=(h == 7), acc=True)
                    for h in range(8):
                        kb.op("pe", lambda e: e.matmul(PC[:, h * 128:(h + 1) * 128], lhsT=Kd[b][p0:p0 + 64, h * 128:(h + 1) * 128],
                                                       rhs=vt[b][p0:p0 + 64, h * 128:(h + 1) * 128], start=True, stop=True),
                              reads=[Kd[b], vt[b]], writes=[PC], inc=(h == 7), acc=(h > 0))
                    kb.op("dve", lambda e: e.tensor_tensor(out=stmp[:], in0=S[:],
                                                           in1=gc[b][:, :, 2 + hf:3 + hf].broadcast_to([128, 8, 128]), op=ALU.mult),
                          reads=[S, gc[b]], writes=[stmp])
                    kb.op("dve", lambda e: e.tensor_tensor(out=S[:], in0=stmp[:], in1=PC[:, :].rearrange("p (h t) -> p h t", t=128), op=ALU.add),
                          reads=[stmp, PC], writes=[S])
                    if not full and n_ >= 2:
                        kb.op("pool", lambda e: e.tensor_tensor(out=Dcum[:], in0=Dcum[:], in1=gc[b][:, :, 2 + hf], op=ALU.mult),
                              reads=[Dcum, gc[b]], writes=[Dcum])
                if dbg is not None and d == 0 and n_ == 0:
                    for nm, tt in (("lf", lf[b]), ("logk", logk[b]), ("Kd", Kd[b]), ("gc", gc[b]), ("S", S), ("kt", kt)) + \
                            ((("AT", AT[b]), ("scm", scm[b]), ("eA", eA)) if full else ()):
                        kb.dma("sp", dbg[nm], tt[:], tt, reads=[tt])
                if full:
                    pov = PB[:, :].rearrange("p (h t) -> p h t", t=128)
                    if d == 0:
                        kb.op("act", lambda e: e.copy(out=oacc[:, :, i * 128:(i + 1) * 128], in_=pov), reads=[PB], writes=[oacc])
                    else:
                        kb.op("dve", lambda e: e.tensor_tensor(out=oacc[:, :, i * 128:(i + 1) * 128], in0=oacc[:, :, i * 128:(i + 1) * 128],
                                                               in1=pov, op=ALU.add),
                              reads=[PB, oacc], writes=[oacc])
            if dbg is not None and full:
                kb.dma("sp", dbg["oacc%d" % d], oacc[:], oacc, reads=[oacc])
                kb.dma("sp", dbg["Send%d" % d], S[:], S, reads=[S])
            if not full:
                kb.dma("sp", XS[d], S[:], S, reads=[S])
                kb.dma("sp", XD[d], Dcum[:], Dcum, reads=[Dcum])
        if full:
            ones_bf = load_const(kb, st, io, "ones_bf", [128, 128], BF16)
            sq = st.tile("hsq", [128, 512], BF16)
            sd = st.tile("hsd", [128, 512], F32)
            rstd = st.tile("hrstd", [128, 512], F32)
            gt = [st.tile("hgt%d" % q, [128, TL], BF16) for q in range(2)]
            sgt = [st.tile("hsgt%d" % q, [128, TL], F32) for q in range(2)]
            yo = [st.tile("hyo%d" % q, [128, TL], BF16) for q in range(2)]
            tmp = st.tile("htmp", [128, 512], F32)
            t0o = 0 if ctx_out else NCTX
            for h in range(8):
                b = h % 2
                kb.dma("sp", gt[b][:], P["bg"][h * 128:(h + 1) * 128, :], gt[b], writes=[gt[b]])
                kb.op("act", lambda e: e.activation(out=sgt[b][:], in_=gt[b][:], func=AF.Silu), reads=[gt[b]], writes=[sgt[b]])
                for (t0, tn) in TOKBLK:
                    kb.op("act", lambda e: e.activation(out=sq[:, 0:tn], in_=oacc[:, h, t0:t0 + tn], func=AF.Square), reads=[oacc], writes=[sq])
                    kb.op("pe", lambda e: e.matmul(PA[:, 0:tn], lhsT=ones_bf[:, :], rhs=sq[:, 0:tn], start=True, stop=True),
                          reads=[ones_bf, sq], writes=[PA])
                    rstd_from_psum(kb, PA, sd, rstd, tn, 1.0 / 128)
                    kb.op("dve", lambda e: e.scalar_tensor_tensor(out=tmp[:, 0:tn], in0=oacc[:, h, t0:t0 + tn], scalar=cols[:, 16 + h:17 + h],
                                                                  in1=rstd[:, 0:tn], op0=ALU.mult, op1=ALU.mult),
                          reads=[oacc, rstd], writes=[tmp])
                    kb.op("pool", lambda e: e.tensor_tensor(out=yo[b][:, t0:t0 + tn], in0=tmp[:, 0:tn], in1=sgt[b][:, t0:t0 + tn], op=ALU.mult),
                          reads=[tmp, sgt[b]], writes=[yo[b]])
                kb.dma("sp", Y[1024 + h * 128:1024 + (h + 1) * 128, t0o:TL], yo[b][:, t0o:TL], yo[b], reads=[yo[b]])
        kb.barrier()


P_SHAPES = {name: ([w, TL] if lay == "F" else [TL, w], dt) for name, w, lay, dt in GROUPS}
XCH = {"K": (256, BF16), "C": (512, BF16), "V": (384, BF16), "S0": (128, F32), "S1": (130, F32)}
RV_V, RV_R, RV_HU, RV_HG = 0, 256, 320, 352
GROUPS4 = [[0, 1, 2, 3], [4, 5, 6, 7]]


class GFused:
    def __init__(self, P, X, GT):
        self.P, self.X, self.GT = P, X, GT

    def rk(self, name, r, r0, n):
        rows = XCH[name][0]
        return self.GT[name][r * rows + r0:r * rows + r0 + n, :]

    def load_kT(self, kb, kT, g):
        kb.dma("sp", kT[:, 0:NCTX], self.X["XK"][g][:, 0:NCTX], kT, writes=[kT])
        for r in range(4):
            kb.dma("sp", kT[:, NCTX + r * NLAT:NCTX + (r + 1) * NLAT], self.rk("K", r, g * 128, 128), kT, writes=[kT])

    def load_V(self, kb, V, g):
        kb.dma("sp", V[:, 0:2, :], self.P["av"][0:NCTX, g * 128:(g + 1) * 128].rearrange("(t p) c -> p t c", p=128), V, writes=[V])
        for r in range(4):
            src = self.rk("V", r, RV_V, 256).rearrange("a (t c) -> (a t) c", c=256)
            kb.dma("sp", V[:, 2 + r * 8:2 + (r + 1) * 8, :], src[:, g * 128:(g + 1) * 128].rearrange("(t p) c -> p t c", p=128),
                   V, writes=[V])

    def load_ckv(self, kb, ckv):
        kb.dma("sp", ckv[:, :, 0:NCTX], self.X["XC"][:, 0:NCTX].rearrange("(c p) t -> p c t", p=128), ckv, writes=[ckv])
        for r in range(4):
            kb.dma("sp", ckv[:, :, NCTX + r * NLAT:NCTX + (r + 1) * NLAT],
                   self.rk("C", r, 0, 512).rearrange("(c p) t -> p c t", p=128), ckv, writes=[ckv])

    def load_kr(self, kb, kr):
        kb.dma("sp", kr[:, 0:NCTX], self.P["dkr"][:, 0:NCTX], kr, writes=[kr])
        for r in range(4):
            kb.dma("sp", kr[:, NCTX + r * NLAT:NCTX + (r + 1) * NLAT], self.rk("V", r, RV_R, 64), kr, writes=[kr])

    def load_halo(self, kb, hu, hg, r0):
        for r in range(4):
            for (t, ro) in ((hu, RV_HU), (hg, RV_HG)):
                src = self.rk("V", r, ro, 32).rearrange("a (c k) -> (a c) k", k=32)
                kb.dma("sp", t[:, r, :], src[r0:r0 + 128, :], t, writes=[t])

    def hS(self, ci, d):
        src = self.rk("S0", ci, 0, 128) if d == 0 else self.rk("S1", ci, 0, 128)
        return src.rearrange("p (h e) -> p h e", e=128)

    def hD(self, ci, d):
        return self.rk("S1", ci, 128 + d, 1).rearrange("o (p h) -> (o p) h", h=8)


def stage_exchange(kb, P, X, ST, GT):
    t = kb.trk("pack")
    for g in range(2):
        kb.dma("sp", ST["K"][g * 128:(g + 1) * 128, :], X["XK"][g][:, NCTX:TL], t)
    kb.dma("sp", ST["C"][:, :], X["XC"][:, NCTX:TL], t)
    kb.dma("sp", ST["V"][RV_R:RV_R + 64, :], P["dkr"][:, NCTX:TL], t)
    kb.dma("sp", ST["V"][RV_V:RV_V + 256, :].rearrange("a (t c) -> (a t) c", c=256), P["av"][NCTX:TL, :], t)
    for (nm, ro) in (("cu", RV_HU), ("cglu", RV_HG)):
        dst = ST["V"][ro:ro + 32, :].rearrange("a (c k) -> (a c) k", k=32)
        kb.dma("sp", dst[:, 0:15], P[nm][:, NCTX:NCTX + 15], t)
        kb.dma("sp", dst[:, 16:31], P[nm][:, TL - 15:TL], t)
    kb.dma("sp", ST["S0"][:, :].rearrange("p (h e) -> p h e", e=128), X["XS"][0], t)
    kb.dma("sp", ST["S1"][0:128, :].rearrange("p (h e) -> p h e", e=128), X["XS"][1], t)
    for d in range(2):
        kb.dma("sp", ST["S1"][128 + d:129 + d, :].rearrange("o (p h) -> (o p) h", h=8), X["XD"][d], t)
    kb.barrier()
    for n in XCH:
        kb.collective("AllGather", [ST[n]], [GT[n]], GROUPS4)
    kb.barrier()


CONST_SPECS = {
    "ident": ([128, 128], BF16), "ones_bf": ([128, 128], BF16), "rmA": ([128, 128], BF16), "rmD": ([64, 64], BF16),
    "cosA": ([128, TL], F32), "sinA": ([128, TL], F32), "cosDq": ([64, TL], F32), "sinDq": ([64, TL], F32),
    "cosDk": ([64, NKEY], F32), "sinDk": ([64, NKEY], F32), "cols": ([2, 128, NCOLS], F32),
    "hM1": ([2, 128, 128], F32), "hM1n": ([2, 128, 128], F32), "hSC": ([2, 128, 128], F32), "hMK": ([2, 128, 128], F32),
    "hMC": ([2, 128, 4], F32), "identf": ([128, 128], F32), "hmask": ([128, 2, 4], F32), "halomask": ([128, 2, 4], F32),
}
X_SPECS = {"XK": ([2, 128, TL], BF16), "XC": ([512, TL], BF16), "XS": ([2, 128, 8, 128], F32), "XD": ([2, 128, 8], F32),
           "MOD": ([2, 128, D], F32)}
W_SPECS = {"w_mod": [2, D, 3 * D], "b_mod": [2, 3 * D], "w_in": [2, D, IN_W], "w_out": [2, D, D],
           "mla_w_uq": [2, 768, 1536], "mla_w_ukv": [2, 512, 2048], "hgrn_lb_logits": [2, 2, 1024],
           "normwT": [2, 128, KC], "cT": [128, KC, 2]}


def build_fused(n_layers=2):
    WL[0] = 0
    nc = bass.Bass("TRN2", target_bir_lowering=False)
    io = {}

    def inp(name, shape, dt=F32):
        io[name] = nc.dram_tensor(name, list(shape), dt, kind="ExternalInput").ap()

    def scratch(name, shape, dt):
        return nc.dram_tensor(name, list(shape), dt, kind="Internal").ap()

    inp("x_loc", [NLAT, D])
    inp("ctx_b", [NCTX, D])
    for n, sh in W_SPECS.items():
        inp(n, sh)
    for n, (sh, dt) in CONST_SPECS.items():
        inp(n, sh, dt)
    out = nc.dram_tensor("out", [NLAT, D], F32, kind="ExternalOutput").ap()
    P = {n: scratch("P_" + n, sh, dt) for n, (sh, dt) in P_SHAPES.items()}
    X = {n: scratch(n, sh, dt) for n, (sh, dt) in X_SPECS.items()}
    Y = scratch("Y", [4096, TL], BF16)
    x1 = scratch("x1", [NLAT, D], F32)
    ctx1 = scratch("ctx1", [NCTX, D], F32)
    ST = {n: scratch("ST_" + n, [r, 1024], dt) for n, (r, dt) in XCH.items()}
    GT = {n: scratch("GT_" + n, [4 * r, 1024], dt) for n, (r, dt) in XCH.items()}
    G = GFused(P, X, GT)
    kb = KB(nc)
    for l in range(n_layers):
        ctx_out = (l == 0)
        x_src = io["x_loc"] if l == 0 else x1
        ctx_src = io["ctx_b"] if l == 0 else ctx1
        last = (l == n_layers - 1)
        with Stage(kb) as keep:
            modcol, gate_bc = stage_mod(kb, io, l, keep)
            t = kb.trk()
            kb.dma("sp", X["MOD"][0], gate_bc[0][:], t, reads=[t])
            kb.dma("sp", X["MOD"][1], gate_bc[1][:], t, reads=[t])
            hT = stage_norm(kb, io, l, keep, modcol, x_src, ctx_src)
            stage_inproj(kb, io, l, hT, P)
            kb.barrier()
        stage_x_gqa(kb, io, l, P, X["XK"])
        stage_x_mla(kb, io, l, P, X["XC"])
        stage_hgrn(kb, io, l, P, None, None, X["XS"], X["XD"], 1, True)
        stage_exchange(kb, P, X, ST, GT)
        stage_attn_A(kb, io, l, P, G, Y, ctx_out)
        stage_hgrn(kb, io, l, P, G, Y, None, None, 2, ctx_out)
        stage_conv(kb, io, l, P, G, Y, ctx_out)
        stage_attn_D(kb, io, l, P, G, Y, ctx_out)
        stage_outproj(kb, io, l, Y, X["MOD"], x_src, ctx_src, out if last else x1, ctx1, ctx_out)
    kb.es.close()
    return nc


BF = ml_dtypes.bfloat16
NCORE = 8


def host_consts(inp):
    c = {}
    c["ident"] = np.eye(128, dtype=np.float32).astype(BF)
    c["ones_bf"] = np.ones((128, 128), np.float32).astype(BF)
    c["rmA"] = rope_rot_matrix(128).astype(BF)
    c["rmD"] = rope_rot_matrix(64).astype(BF)
    c["cosDk"], c["sinDk"] = rope_tables(64, key_tpos())
    c["cols"] = np.stack([make_cols(inp, 0), make_cols(inp, 1)])
    c.update(hgrn_consts())
    per_core = []
    for core in range(NCORE):
        j = core % 4
        d = {}
        d["cosA"], d["sinA"] = rope_tables(128, local_tpos(j))
        d["cosDq"], d["sinDq"] = rope_tables(64, local_tpos(j))
        hm = np.zeros((128, 2, 4), np.float32)
        ha = np.zeros((128, 2, 4), np.float32)
        for i in range(4):
            hm[:, 0, i] = 1.0 if i < j else 0.0
            hm[:, 1, i] = 1.0 if i > j else 0.0
            ha[:, 0, i] = 1.0 if i == j - 1 else 0.0
            ha[:, 1, i] = 1.0 if i == j + 1 else 0.0
        d["hmask"] = hm
        d["halomask"] = ha
        per_core.append(d)
    return c, per_core


def make_in_maps(inp):
    consts, pc = host_consts(inp)
    normwT = np.ascontiguousarray(inp["norm_w"].reshape(2, KC, 128).transpose(0, 2, 1))
    maps = []
    for c in range(NCORE):
        b, j = c // 4, c % 4
        m = {"x_loc": np.ascontiguousarray(inp["x"][b, j * NLAT:(j + 1) * NLAT]),
             "ctx_b": np.ascontiguousarray(inp["ctx"][b]),
             "normwT": normwT,
             "cT": np.ascontiguousarray(np.stack([inp["c"][b], inp["c_ctx"]], axis=-1).reshape(KC, 128, 2).transpose(1, 0, 2))}
        for n in ("w_mod", "b_mod", "w_in", "w_out", "mla_w_uq", "mla_w_ukv", "hgrn_lb_logits"):
            m[n] = inp[n]
        for n in CONST_SPECS:
            m[n] = pc[c][n] if n in pc[c] else consts[n]
        maps.append(m)
    return maps


def kernel(**inp):
    inp = {k: np.asarray(v) for k, v in inp.items()}
    nc = build_fused()
    res = run_bass_kernel_spmd(nc, make_in_maps(inp), core_ids=list(range(NCORE))).results
    out = np.zeros((2, SEQ, D), np.float32)
    for c in range(NCORE):
        out[c // 4, (c % 4) * NLAT:(c % 4 + 1) * NLAT] = np.asarray(res[c]["out"])
    return out
```
